# Optimizing a Trainium2 kernel written in Bass

```python
import math
import jax, jax.numpy as jnp
from jax import lax
import numpy as np

D_MODEL = 1024
BATCH = 8
SEQ = 2048
DEPTH = 4

N_MIXERS = 2
HEAD_DIM = 64
N_HEADS_A = 16
N_KV_HEADS_A = 4
GROUP_A = N_HEADS_A // N_KV_HEADS_A
WINDOW = 128
BLOCK_A = WINDOW
N_HEADS_B = 16
BLOCK_B = 128
D_FF = 4 * D_MODEL
N_BUCKETS = 32
MAX_DISTANCE = 128
EPS = 1e-6
NEG = -1e30
N_LAYERS_A = (DEPTH + 1) // 2
N_LAYERS_B = DEPTH // 2
QKV_A = (N_HEADS_A + 2 * N_KV_HEADS_A) * HEAD_DIM
HD_B = N_HEADS_B * HEAD_DIM
QKVF_B = 3 * HD_B + N_HEADS_B
FORGET_BIAS_MEAN = 3.0

kernel_name = "hybrid_swa_sink_fox_sqrelu"


def rmsnorm(x, g):
    xf = x.astype(jnp.float32)
    y = xf * lax.rsqrt(jnp.mean(xf * xf, axis=-1, keepdims=True) + EPS)
    return (y * g.astype(jnp.float32)).astype(x.dtype)


def t5_bucket(dist):
    max_exact = N_BUCKETS // 2
    d = jnp.maximum(dist, 0)
    dl = jnp.maximum(d, 1).astype(jnp.float32)
    large = max_exact + (jnp.log(dl / max_exact) / math.log(MAX_DISTANCE / max_exact)
                         * (N_BUCKETS - max_exact)).astype(jnp.int32)
    large = jnp.minimum(large, N_BUCKETS - 1)
    return jnp.where(d < max_exact, d, large)


def swa_sink_attention(h, w_qkv, b_qkv, w_o, b_o, sinks, rel_bias):
    B, S, _ = h.shape
    nb = S // BLOCK_A
    qkv = h @ w_qkv + b_qkv
    q, k, v = jnp.split(qkv, [N_HEADS_A * HEAD_DIM, (N_HEADS_A + N_KV_HEADS_A) * HEAD_DIM], axis=-1)
    q = q.reshape(B, nb, BLOCK_A, N_KV_HEADS_A, GROUP_A, HEAD_DIM)
    k = k.reshape(B, nb, BLOCK_A, N_KV_HEADS_A, HEAD_DIM)
    v = v.reshape(B, nb, BLOCK_A, N_KV_HEADS_A, HEAD_DIM)
    pad = jnp.zeros_like(k[:, :1])
    k2 = jnp.concatenate([jnp.concatenate([pad, k[:, :-1]], axis=1), k], axis=2)
    v2 = jnp.concatenate([jnp.concatenate([pad, v[:, :-1]], axis=1), v], axis=2)
    scale = 1.0 / math.sqrt(HEAD_DIM)
    scores = jnp.einsum('bnqhgd,bnkhd->bnhgqk', q, k2).astype(jnp.float32) * scale
    qi = jnp.arange(BLOCK_A, dtype=jnp.int32)[:, None]
    kj = jnp.arange(2 * BLOCK_A, dtype=jnp.int32)[None, :]
    dist = qi + BLOCK_A - kj
    bias = rel_bias[t5_bucket(dist)].astype(jnp.float32)
    bias = bias.transpose(2, 0, 1).reshape(N_KV_HEADS_A, GROUP_A, BLOCK_A, 2 * BLOCK_A)
    in_window = (dist >= 0) & (dist < WINDOW)
    blk = jnp.arange(nb, dtype=jnp.int32)[:, None, None]
    valid = in_window[None] & ((blk > 0) | (kj >= BLOCK_A)[None])
    scores = jnp.where(valid[None, :, None, None], scores + bias, NEG)
    sink = jnp.broadcast_to(sinks.astype(jnp.float32).reshape(N_KV_HEADS_A, GROUP_A, 1, 1),
                            scores.shape[:-1] + (1,))
    probs = jax.nn.softmax(jnp.concatenate([scores, sink], axis=-1), axis=-1)[..., :-1]
    out = jnp.einsum('bnhgqk,bnkhd->bnqhgd', probs.astype(v2.dtype), v2)
    out = out.reshape(B, S, N_HEADS_A * HEAD_DIM)
    return out @ w_o + b_o


def forgetting_attention(h, w_qkvf, b_f, w_o):
    B, S, _ = h.shape
    proj = h @ w_qkvf
    q, k, v, fz = jnp.split(proj, [HD_B, 2 * HD_B, 3 * HD_B], axis=-1)
    q = q.reshape(B, S, N_HEADS_B, HEAD_DIM)
    k = k.reshape(B, S, N_HEADS_B, HEAD_DIM)
    v = v.reshape(B, S, N_HEADS_B, HEAD_DIM)
    log_f = jax.nn.log_sigmoid((fz + b_f).astype(jnp.float32))
    c = jnp.cumsum(log_f, axis=1).transpose(0, 2, 1)
    scale = 1.0 / math.sqrt(HEAD_DIM)
    outs = []
    for n in range(S // BLOCK_B):
        q0, q1 = n * BLOCK_B, (n + 1) * BLOCK_B
        s = jnp.einsum('bqhd,bkhd->bhqk', q[:, q0:q1], k[:, :q1]).astype(jnp.float32) * scale
        decay = c[:, :, q0:q1, None] - c[:, :, None, :q1]
        causal = (q0 + jnp.arange(BLOCK_B)[:, None]) >= jnp.arange(q1)[None, :]
        p = jax.nn.softmax(jnp.where(causal, s + decay, NEG), axis=-1)
        outs.append(jnp.einsum('bhqk,bkhd->bqhd', p.astype(v.dtype), v[:, :q1]))
    out = jnp.concatenate(outs, axis=1).reshape(B, S, HD_B)
    return out @ w_o


def setup_inputs(seed: int = 0) -> dict:
    key = jax.random.key(seed)
    ks = jax.random.split(key, 16)
    nrm = jax.random.normal
    D = D_MODEL
    x = nrm(ks[0], (BATCH, SEQ, D), jnp.float32)
    rel_bias = 0.5 * nrm(ks[1], (N_BUCKETS, N_HEADS_A), jnp.float32)
    norm_mix = 1.0 + 0.05 * nrm(ks[2], (DEPTH, D), jnp.float32)
    norm_mlp = 1.0 + 0.05 * nrm(ks[3], (DEPTH, D), jnp.float32)
    w_qkv_a = nrm(ks[4], (N_LAYERS_A, D, QKV_A), jnp.float32) * D ** -0.5
    b_qkv_a = 0.02 * nrm(ks[5], (N_LAYERS_A, QKV_A), jnp.float32)
    sinks_a = 0.5 * nrm(ks[6], (N_LAYERS_A, N_HEADS_A), jnp.float32)
    w_o_a = nrm(ks[7], (N_LAYERS_A, N_HEADS_A * HEAD_DIM, D), jnp.float32) * (N_HEADS_A * HEAD_DIM) ** -0.5
    b_o_a = 0.02 * nrm(ks[8], (N_LAYERS_A, D), jnp.float32)
    col_scale = jnp.concatenate([jnp.ones((3 * HD_B,), jnp.float32),
                                 0.1 * jnp.ones((N_HEADS_B,), jnp.float32)])
    w_qkvf_b = nrm(ks[9], (N_LAYERS_B, D, QKVF_B), jnp.float32) * D ** -0.5 * col_scale
    b_f_b = FORGET_BIAS_MEAN + 0.5 * nrm(ks[10], (N_LAYERS_B, N_HEADS_B), jnp.float32)
    w_o_b = nrm(ks[11], (N_LAYERS_B, HD_B, D), jnp.float32) * HD_B ** -0.5
    w_up = nrm(ks[12], (DEPTH, D, D_FF), jnp.float32) * D ** -0.5
    w_down = nrm(ks[13], (DEPTH, D_FF, D), jnp.float32) * D_FF ** -0.5
    norm_final = 1.0 + 0.05 * nrm(ks[14], (D,), jnp.float32)
    return {"x": x, "rel_bias": rel_bias, "norm_mix": norm_mix, "norm_mlp": norm_mlp,
            "w_qkv_a": w_qkv_a, "b_qkv_a": b_qkv_a, "sinks_a": sinks_a, "w_o_a": w_o_a,
            "b_o_a": b_o_a, "w_qkvf_b": w_qkvf_b, "b_f_b": b_f_b, "w_o_b": w_o_b,
            "w_up": w_up, "w_down": w_down, "norm_final": norm_final}


def reference(x, rel_bias, norm_mix, norm_mlp, w_qkv_a, b_qkv_a, sinks_a, w_o_a, b_o_a,
              w_qkvf_b, b_f_b, w_o_b, w_up, w_down, norm_final):
    for i in range(DEPTH):
        h = rmsnorm(x, norm_mix[i])
        j = i // N_MIXERS
        if i % N_MIXERS == 0:
            x = x + swa_sink_attention(h, w_qkv_a[j], b_qkv_a[j], w_o_a[j], b_o_a[j],
                                       sinks_a[j], rel_bias)
        else:
            x = x + forgetting_attention(h, w_qkvf_b[j], b_f_b[j], w_o_b[j])
        h = rmsnorm(x, norm_mlp[i])
        x = x + jnp.square(jax.nn.relu(h @ w_up[i])) @ w_down[i]
    return rmsnorm(x, norm_final)
```

```python
from contextlib import ExitStack
import math
import numpy as np
import concourse.bass as bass
import concourse.mybir as mybir
from concourse.bass_utils import run_bass_kernel_spmd

F32 = mybir.dt.float32
BF16 = mybir.dt.bfloat16
AF = mybir.ActivationFunctionType
ALU = mybir.AluOpType

S_LEN = 2048
D = 1024
DFF = 4096
NT = 16
DEPTH = 4
NEG = -1e30
EPS = 1e-6

ENGS = ("tensor", "scalar", "vector", "gpsimd", "sync")
STRICT = True


class Buf:
    __slots__ = ("name", "writers", "readers")

    def __init__(self, name):
        self.name = name
        self.writers = []
        self.readers = []


class Op:
    __slots__ = ("idx", "eng", "fn", "deps", "wdeps", "is_dma", "sem", "semval", "milestone", "needed", "eff")

    def __init__(self, idx, eng, fn, is_dma=False):
        self.idx = idx
        self.eng = eng
        self.fn = fn
        self.deps = set()
        self.wdeps = set()
        self.eff = ()
        self.is_dma = is_dma
        self.sem = None
        self.semval = 0
        self.milestone = 0
        self.needed = False


class Sched:
    def __init__(self, nc, stack):
        self.nc = nc
        self.stack = stack
        self.ops = []
        self.dma_sems = {}
        self.eng_sem = {}
        for e in ENGS:
            self.eng_sem[e] = stack.enter_context(nc.semaphore("es_" + e))

    def _dma_sem(self, key):
        if key not in self.dma_sems:
            h = self.stack.enter_context(self.nc.semaphore("ds_%s" % key))
            self.dma_sems[key] = [h, 0]
        return self.dma_sems[key]

    def _add(self, op, reads, writes):
        for b in reads:
            for w in b.writers:
                op.deps.add(w)
        for b in writes:
            for w in b.writers:
                op.wdeps.add(w)
            for r in b.readers:
                op.wdeps.add(r)
        op.deps.discard(op.idx)
        op.wdeps.discard(op.idx)
        for b in reads:
            b.readers.append(op.idx)
        for b in writes:
            b.writers = [op.idx]
            b.readers = []
        self.ops.append(op)
        return op

    def op(self, eng, fn, reads=(), writes=()):
        return self._add(Op(len(self.ops), eng, fn), reads, writes)

    def dma(self, eng, fn, semkey, reads=(), writes=(), join=False):
        o = Op(len(self.ops), eng, fn, is_dma=True)
        s = self._dma_sem(semkey)
        s[1] += 16
        o.sem = s[0]
        o.semval = s[1]
        if join:
            for b in reads:
                for w in b.writers:
                    o.deps.add(w)
                b.readers.append(o.idx)
            for b in writes:
                for w in b.writers:
                    o.deps |= self.ops[w].deps
                    o.wdeps |= self.ops[w].wdeps
                b.writers = b.writers + [o.idx]
            self.ops.append(o)
            return o
        return self._add(o, reads, writes)

    def emit(self, block, final_wait_ops=()):
        ops = self.ops
        for o in ops:
            eff = []
            for d in o.deps:
                p = ops[d]
                if (not p.is_dma) and p.eng == o.eng and p.eng == "tensor":
                    continue
                eff.append(d)
            for d in o.wdeps:
                if d in o.deps:
                    continue
                p = ops[d]
                if (not p.is_dma) and p.eng == o.eng and not o.is_dma and (p.eng == "tensor" or not STRICT):
                    continue
                eff.append(d)
            o.eff = eff
            for d in eff:
                if not ops[d].is_dma:
                    ops[d].needed = True
        cnt = {e: 0 for e in ENGS}
        for o in ops:
            if not o.is_dma and o.needed:
                cnt[o.eng] += 1
                o.milestone = cnt[o.eng]
        per_eng = {e: [] for e in ENGS}
        for o in ops:
            per_eng[o.eng].append(o)
        sched = self

        def run(engname, eng):
            waited = {}
            for o in per_eng[engname]:
                need = {}
                for d in o.eff:
                    p = ops[d]
                    if p.is_dma:
                        key = ("d", id(p.sem))
                        v = p.semval
                        h = p.sem
                    else:
                        key = ("e", p.eng)
                        v = p.milestone
                        h = sched.eng_sem[p.eng]
                    if need.get(key, (None, 0))[1] < v:
                        need[key] = (h, v)
                for key, (h, v) in need.items():
                    if waited.get(key, 0) >= v:
                        continue
                    eng.wait_ge(h, v)
                    waited[key] = v
                ins = o.fn(eng)
                if o.is_dma:
                    ins.then_inc(o.sem, 16)
                elif o.needed:
                    ins.then_inc(sched.eng_sem[engname], 1)
            if engname == "sync":
                for o in final_wait_ops:
                    eng.wait_ge(o.sem, o.semval)

        @block.tensor
        def _(e):
            run("tensor", e)

        @block.scalar
        def _(e):
            run("scalar", e)

        @block.vector
        def _(e):
            run("vector", e)

        @block.gpsimd
        def _(e):
            run("gpsimd", e)

        @block.sync
        def _(e):
            run("sync", e)


NPAR = 96
PC_GMIX, PC_GMLP, PC_BQ, PC_BK, PC_BF = 0, 32, 64, 80, 88


def _t5_bucket_np(dist):
    n_buckets, max_distance = 32, 128
    max_exact = n_buckets // 2
    d = np.maximum(dist, 0)
    dl = np.maximum(d, 1).astype(np.float32)
    large = max_exact + (np.log(dl / np.float32(max_exact)) / np.float32(math.log(max_distance / max_exact))
                         * np.float32(n_buckets - max_exact)).astype(np.int32)
    large = np.minimum(large, n_buckets - 1)
    return np.where(d < max_exact, d, large)


def _bucket_table():
    return _t5_bucket_np(np.arange(256, dtype=np.int32))


def _layout_params(norm_mix, norm_mlp, b_qkv_a, b_f_b):
    p = np.zeros((128, NPAR), np.float32)
    for l in range(DEPTH):
        p[:, PC_GMIX + l * 8:PC_GMIX + l * 8 + 8] = norm_mix[l].reshape(8, 128).T
        p[:, PC_GMLP + l * 8:PC_GMLP + l * 8 + 8] = norm_mlp[l].reshape(8, 128).T
    for a in range(2):
        p[:, PC_BQ + a * 8:PC_BQ + a * 8 + 8] = b_qkv_a[a][0:1024].reshape(8, 128).T
        p[0:64, PC_BK + a * 4:PC_BK + a * 4 + 4] = b_qkv_a[a][1024:1280].reshape(4, 64).T
        p[0:16, PC_BF + a] = b_f_b[a]
    return p


def _layout_bias(rel_bias):
    bt = _bucket_table()
    k = np.arange(128)[:, None]
    q = np.arange(128)[None, :]
    out = np.empty((2, 4, 128, 4, 128), np.float32)
    for v in range(2):
        dist = q - k + (128 if v == 1 else 0)
        valid = (dist >= 0) & (dist < 128)
        idx = bt[np.clip(dist, 0, 255)]
        for h in range(16):
            g = rel_bias[idx, h]
            out[v, h // 4, :, h % 4, :] = np.where(valid, g, np.float32(NEG))
    return out


def build(layers=(0, 1, 2, 3), final=True):
    nc = bass.Bass("TRN2", target_bir_lowering=False)
    x_d = nc.dram_tensor("x", [S_LEN, D], F32, kind="ExternalInput").ap()
    par_d = nc.dram_tensor("params", [128, NPAR], F32, kind="ExternalInput").ap()
    bias_d = nc.dram_tensor("biasT", [2, 4, 128, 512], F32, kind="ExternalInput").ap()
    wqkv_d = nc.dram_tensor("w_qkv_a", [2, D, 1536], F32, kind="ExternalInput").ap()
    bqkv_d = nc.dram_tensor("b_qkv_a", [2, 1536], F32, kind="ExternalInput").ap()
    sinks_d = nc.dram_tensor("sinks_a", [2, 16], F32, kind="ExternalInput").ap()
    woa_d = nc.dram_tensor("w_o_a", [2, D, D], F32, kind="ExternalInput").ap()
    boa_d = nc.dram_tensor("b_o_a", [2, D], F32, kind="ExternalInput").ap()
    wqkvf_d = nc.dram_tensor("w_qkvf_b", [2, D, 3088], F32, kind="ExternalInput").ap()
    wob_d = nc.dram_tensor("w_o_b", [2, D, D], F32, kind="ExternalInput").ap()
    wup_d = nc.dram_tensor("w_up", [DEPTH, D, DFF], F32, kind="ExternalInput").ap()
    wdn_d = nc.dram_tensor("w_down", [DEPTH, DFF, D], F32, kind="ExternalInput").ap()
    gfin_d = nc.dram_tensor("norm_final", [D], F32, kind="ExternalInput").ap()
    out_d = nc.dram_tensor("out", [S_LEN, D], F32, kind="ExternalOutput").ap()

    with ExitStack() as st:
        S = Sched(nc, st)

        def T(name, shape, dt):
            return st.enter_context(nc.sbuf_tensor(name, shape, dt))

        X = T("X", [128, NT, D], F32)
        hT = T("hT", [128, 8, S_LEN], BF16)
        ARENA = 24576
        WA = T("WA", [128, ARENA], BF16)
        R12 = T("R12", [128, 8192], BF16)
        R3 = T("R3", [128, 4096], BF16)
        R4 = T("R4", [128, 4096], BF16)
        PT = [T("PT%d" % i, [128, 512], BF16) for i in range(4)]
        TMP = [T("TMP%d" % i, [128, 512], F32) for i in range(3)]
        BIA = [T("BIA0", [128, 2, 512], F32)]
        ident = T("ident", [128, 128], BF16)
        identf = T("identf", [128, 128], F32)
        tri = T("tri", [128, 128], BF16)
        par = T("par", [128, NPAR], F32)
        ss = T("ss", [128, 16], F32)
        ms = T("ms", [128, 16], F32)
        rstd = T("rstd", [128, 16], F32)
        XN = [T("XN%d" % i, [128, D], BF16) for i in range(2)]
        junk = XN[1]
        bvrep = T("bvrep", [128, 64], F32)
        DEN = T("DEN", [128, 512], F32)
        DENb = DEN[:].bitcast(BF16)
        PTX = [DENb[:, 0:512], DENb[:, 512:1024]]
        sinkraw = T("sinkraw", [128, 16], F32)
        esink = T("esink", [128, 16], F32)
        borep = T("borep", [128, D], F32)
        AUG = T("AUG", [128, S_LEN], BF16)
        ones16 = borep[0:16, 0:512]
        midS = T("midS", [16, 512], BF16)
        carry = T("carry", [16, 4], F32)
        nbf = T("nbf", [16, 2], F32)
        psF = st.enter_context(nc.psum_tensor("psF", [128, 8, 512], F32))
        psTv = [psF[:, 6, :].bitcast(BF16), psF[:, 7, :].bitcast(BF16)]

        bX = [Buf("X%d" % t) for t in range(NT)]
        b_hT = [Buf("hT%d" % g) for g in range(4)]
        b_R1, b_R2, b_R3, b_R4 = Buf("R1"), Buf("R2"), Buf("R3"), Buf("R4")
        b_R3v = Buf("R3v")
        b_R4b = Buf("R4b")
        b_QAaug = [Buf("QAaug0"), Buf("QAaug1")]
        b_KAaug = [Buf("KAaug0"), Buf("KAaug1")]
        fox_state = {}
        b_PT = [Buf("PT%d" % i) for i in range(4)]
        b_TMP = [Buf("TMP%d" % i) for i in range(3)]
        b_BIA = [Buf("BIA0")]
        b_ident, b_identf, b_tri, b_par = Buf("ident"), Buf("identf"), Buf("tri"), Buf("par")
        b_ss, b_ms, b_rstd = Buf("ss"), Buf("ms"), Buf("rstd")
        b_ssg = [Buf("ssg%d" % g) for g in range(4)]
        b_msg = [Buf("msg%d" % g) for g in range(4)]
        b_rstdg = [Buf("rstdg%d" % g) for g in range(4)]
        norm_done = set()
        b_XN = [Buf("XN0"), Buf("XN1")]
        b_junk = b_XN[1]
        b_DEN = Buf("DEN")
        b_PTx = [Buf("PTx0"), Buf("PTx1")]
        b_bv, b_sinkraw, b_esink, b_borep = Buf("bv"), Buf("sinkraw"), Buf("esink"), Buf("borep")
        b_AUG, b_midS, b_carry, b_nbf, b_ones16 = Buf("AUG"), Buf("midS"), Buf("carry"), Buf("nbf"), Buf("ones16")
        b_psF = [Buf("psF%d" % i) for i in range(8)]
        b_psT = [b_psF[6], b_psF[7]]

        rr = {"A": 0, "B": 0, "T": 0, "pt": 0, "tmp": 0, "xn": 0, "ptf": 0, "fsc": 0, "facc": 0}
        PTF = PT + PTX
        b_PTF = b_PT + b_PTx

        def bankA():
            i = rr["A"] % 4
            rr["A"] += 1
            return i

        def bankB():
            i = 4 + rr["B"] % 4
            rr["B"] += 1
            return i

        def nxt(key, n):
            i = rr[key] % n
            rr[key] += 1
            return i

        arena = {"off": 0, "live": []}
        arena_done = set()

        def arena_alloc(n, name, align=1):
            off = ((arena["off"] + align - 1) // align) * align
            if off + n > ARENA:
                off = 0
            end = off + n
            over = [a for a in arena["live"] if not (a[1] <= off or a[0] >= end)]
            for a_ in over:
                assert a_[2].name in arena_done, "arena overlap with pending allocation %s" % a_[2].name
            arena["live"] = [a for a in arena["live"] if (a[1] <= off or a[0] >= end)]
            b = Buf(name)
            arena["live"].append((off, end, b))
            arena["off"] = end
            return off, b, [a[2] for a in over]

        wcount = [0]

        def wload(dst_ap, src_ap, b, over, first):
            wcount[0] += 1
            key = "w_" + b.name
            if first:
                S.dma("gpsimd", lambda e: e.dma_start(out=dst_ap, in_=src_ap), key, writes=[b] + over)
            else:
                S.dma("gpsimd", lambda e: e.dma_start(out=dst_ap, in_=src_ap), key, writes=[b], join=True)

        S.dma("sync", lambda e: e.dma_start(out=par[:], in_=par_d), "par", writes=[b_par])
        xv = x_d.rearrange("(t p) d -> p t d", p=128)
        for t4 in range(4):
            S.dma("sync", lambda e, t4=t4: e.dma_start(out=X[:, 4 * t4:4 * t4 + 4, :], in_=xv[:, 4 * t4:4 * t4 + 4, :]),
                  "x%d" % t4, writes=bX[4 * t4:4 * t4 + 4])
        S.op("gpsimd", lambda e: e.memset(identf[:], 1.0), writes=[b_identf])
        S.op("gpsimd", lambda e: e.affine_select(out=identf[:], in_=identf[:], pattern=[[1, 128]],
                                                 compare_op=ALU.is_equal, fill=0.0, base=0, channel_multiplier=-1),
             reads=[b_identf], writes=[b_identf])
        S.op("vector", lambda e: e.tensor_copy(out=ident[:], in_=identf[:]), reads=[b_identf], writes=[b_ident])
        S.op("gpsimd", lambda e: e.memset(identf[:], 1.0), reads=[b_identf], writes=[b_identf])
        S.op("gpsimd", lambda e: e.affine_select(out=identf[:], in_=identf[:], pattern=[[1, 128]],
                                                 compare_op=ALU.is_ge, fill=0.0, base=0, channel_multiplier=-1),
             reads=[b_identf], writes=[b_identf])
        S.op("vector", lambda e: e.tensor_copy(out=tri[:], in_=identf[:]), reads=[b_identf], writes=[b_tri])
        S.op("vector", lambda e: e.tensor_scalar(out=nbf[:], in0=par[0:16, PC_BF:PC_BF + 2], scalar1=-1.0, scalar2=None,
                                                 op0=ALU.mult), reads=[b_par], writes=[b_nbf])

        def rms_stats():
            for t in range(NT):
                S.op("scalar", lambda e, t=t: e.activation(out=junk[:], in_=X[:, t, :], func=AF.Square,
                                                           accum_out=ss[:, t:t + 1]),
                     reads=[bX[t]], writes=[b_junk, b_ss] + b_ssg)
            S.op("vector", lambda e: e.tensor_scalar(out=ms[:], in0=ss[:], scalar1=1.0 / D, scalar2=EPS,
                                                     op0=ALU.mult, op1=ALU.add), reads=[b_ss], writes=[b_ms] + b_msg)
            S.op("scalar", lambda e: e.activation(out=ms[:], in_=ms[:], func=AF.Sqrt), reads=[b_ms], writes=[b_ms])
            S.op("vector", lambda e: e.reciprocal(out=rstd[:], in_=ms[:]), reads=[b_ms], writes=[b_rstd] + b_rstdg)

        def norm_phase(gcol):
            rms_stats()
            for t in range(NT):
                xi = nxt("xn", 2)
                ti = nxt("T", 2)
                S.op("scalar", lambda e, t=t, xi=xi: e.activation(out=XN[xi][:], in_=X[:, t, :], func=AF.Copy,
                                                                  scale=rstd[:, t:t + 1]),
                     reads=[bX[t], b_rstd], writes=[b_XN[xi]])
                for c in range(8):
                    S.op("tensor", lambda e, c=c, xi=xi, ti=ti: e.transpose(
                        out=psTv[ti][:, c * 128:(c + 1) * 128], in_=XN[xi][:, c * 128:(c + 1) * 128], identity=ident[:]),
                        reads=[b_XN[xi], b_ident], writes=[b_psT[ti]])
                S.op("vector", lambda e, t=t, ti=ti: e.tensor_tensor(
                    out=hT[:, :, t * 128:(t + 1) * 128],
                    in0=psTv[ti].rearrange("p (c t) -> p c t", c=8),
                    in1=par[:, gcol:gcol + 8].unsqueeze(2).broadcast_to([128, 8, 128]), op=ALU.mult),
                    reads=[b_psT[ti], b_par], writes=[b_hT[t // 4]])

        def norm_group_stats(g):
            for t in range(4 * g, 4 * g + 4):
                S.op("scalar", lambda e, t=t: e.activation(out=junk[:], in_=X[:, t, :], func=AF.Square,
                                                           accum_out=ss[:, t:t + 1]),
                     reads=[bX[t]], writes=[b_junk, b_ssg[g]])
            S.op("vector", lambda e: e.tensor_scalar(out=ms[:, 4 * g:4 * g + 4], in0=ss[:, 4 * g:4 * g + 4],
                                                     scalar1=1.0 / D, scalar2=EPS, op0=ALU.mult, op1=ALU.add),
                 reads=[b_ssg[g]], writes=[b_msg[g]])
            S.op("scalar", lambda e: e.activation(out=ms[:, 4 * g:4 * g + 4], in_=ms[:, 4 * g:4 * g + 4], func=AF.Sqrt),
                 reads=[b_msg[g]], writes=[b_msg[g]])
            S.op("vector", lambda e: e.reciprocal(out=rstd[:, 4 * g:4 * g + 4], in_=ms[:, 4 * g:4 * g + 4]),
                 reads=[b_msg[g]], writes=[b_rstdg[g]])

        def norm_group_tiles(g, gcol):
            for t in range(4 * g, 4 * g + 4):
                xi = nxt("xn", 2)
                ti = nxt("T", 2)
                S.op("scalar", lambda e, t=t, xi=xi: e.activation(out=XN[xi][:], in_=X[:, t, :], func=AF.Copy,
                                                                  scale=rstd[:, t:t + 1]),
                     reads=[bX[t], b_rstdg[g]], writes=[b_XN[xi]])
                for c in range(8):
                    S.op("tensor", lambda e, c=c, xi=xi, ti=ti: e.transpose(
                        out=psTv[ti][:, c * 128:(c + 1) * 128], in_=XN[xi][:, c * 128:(c + 1) * 128], identity=ident[:]),
                        reads=[b_XN[xi], b_ident], writes=[b_psT[ti]])
                S.op("vector", lambda e, t=t, ti=ti: e.tensor_tensor(
                    out=hT[:, :, t * 128:(t + 1) * 128],
                    in0=psTv[ti].rearrange("p (c t) -> p c t", c=8),
                    in1=par[:, gcol:gcol + 8].unsqueeze(2).broadcast_to([128, 8, 128]), op=ALU.mult),
                    reads=[b_psT[ti], b_par], writes=[b_hT[t // 4]])

        def mm(out_ap, lhsT, rhs, start, stop, reads, wbuf):
            S.op("tensor", lambda e: e.matmul(out_ap, lhsT=lhsT, rhs=rhs, start=start, stop=stop),
                 reads=reads, writes=[wbuf])

        def add_to_X_split(t, half, bk, k):
            if k % 2 == 0:
                add_to_X(t, half, bk)
                return
            ti = nxt("tmp", 3)
            S.op("scalar", lambda e: e.activation(out=TMP[ti][:], in_=psF[:, bk, :], func=AF.Copy),
                 reads=[b_psF[bk]], writes=[b_TMP[ti]])
            S.op("gpsimd", lambda e: e.tensor_tensor(out=X[:, t, 512 * half:512 * half + 512], in0=TMP[ti][:],
                                                     in1=X[:, t, 512 * half:512 * half + 512], op=ALU.add),
                 reads=[b_TMP[ti], bX[t]], writes=[bX[t]])

        def add_to_X(t, half, bk):
            S.op("vector", lambda e: e.tensor_tensor(out=X[:, t, 512 * half:512 * half + 512], in0=psF[:, bk, :],
                                                     in1=X[:, t, 512 * half:512 * half + 512], op=ALU.add),
                 reads=[b_psF[bk], bX[t]], writes=[bX[t]])

        def swa_loads(a, jj):
            off, b, over = arena_alloc(5120, "swa%d_%d" % (a, jj))
            wq = WA[:, off:off + 2048].rearrange("p (c n) -> p c n", c=8)
            wkv = WA[:, off + 2048:off + 3072].rearrange("p (c n) -> p c n", c=8)
            wo = WA[:, off + 3072:off + 5120].rearrange("p (c n) -> p c n", c=2)
            src = wqkv_d[a].rearrange("(c p) n -> p c n", p=128)
            wload(wq, src[:, :, 256 * jj:256 * jj + 256], b, over, True)
            wload(wkv[:, :, 0:64], src[:, :, 1024 + 64 * jj:1024 + 64 * jj + 64], b, over, False)
            wload(wkv[:, :, 64:128], src[:, :, 1280 + 64 * jj:1280 + 64 * jj + 64], b, over, False)
            wload(wo, woa_d[a][256 * jj:256 * jj + 256, :].rearrange("(c p) n -> p c n", p=128), b, over, False)
            return dict(wq=wq, wkv=wkv, wo=wo, b=b)

        def swa_batch(a, jj, W):
            wq, wkv, wo, bw = W["wq"], W["wkv"], W["wo"], W["b"]
            QT = R12[:].rearrange("p (h t) -> p h t", h=4)
            KT = R3[:, 0:2048]
            VP = R3[:, 2048:4096].rearrange("p (t n) -> p t n", t=16)
            OT = R4[:].rearrange("p (c t) -> p c t", c=2)
            bi = 0
            S.dma("sync", lambda e: e.dma_start(out=BIA[bi][:], in_=bias_d[:, jj].rearrange("v k f -> k v f")),
                  "bia%d" % bi, writes=[b_BIA[bi]])
            S.dma("sync", lambda e: e.dma_start(
                out=bvrep[:], in_=bqkv_d[a][1280 + 64 * jj:1280 + 64 * jj + 64].partition_broadcast(128)),
                "bv", writes=[b_bv])
            S.op("gpsimd", lambda e: e.memset(VP[:, :, 64:128], 1.0), writes=[b_R3v, b_R3])
            for cc in range(2):
                col = PC_BQ + a * 8 + jj * 2 + cc
                for g in range(4):
                    bk = bankA()
                    for c in range(8):
                        mm(psF[:, bk, :], wq[:, c, 128 * cc:128 * cc + 128], hT[:, c, 512 * g:512 * g + 512],
                           c == 0, c == 7, [bw, b_hT[g]], b_psF[bk])
                    bR = b_R1 if cc == 0 else b_R2
                    S.op("scalar", lambda e, bk=bk, cc=cc, g=g, col=col: e.activation(
                        out=QT[0:64, 2 * cc, 512 * g:512 * g + 512], in_=psF[0:64, bk, :], func=AF.Identity,
                        bias=par[0:64, col:col + 1]), reads=[b_psF[bk], b_par], writes=[bR])
                    S.op("scalar", lambda e, bk=bk, cc=cc, g=g, col=col: e.activation(
                        out=QT[0:64, 2 * cc + 1, 512 * g:512 * g + 512], in_=psF[64:128, bk, :], func=AF.Identity,
                        bias=par[64:128, col:col + 1]), reads=[b_psF[bk], b_par], writes=[bR])
            colk = PC_BK + a * 4 + jj
            for g in range(4):
                bk = bankA()
                for c in range(8):
                    mm(psF[0:64, bk, :], wkv[:, c, 0:64], hT[:, c, 512 * g:512 * g + 512], c == 0, c == 7,
                       [bw, b_hT[g]], b_psF[bk])
                S.op("scalar", lambda e, bk=bk, g=g: e.activation(
                    out=KT[0:64, 512 * g:512 * g + 512], in_=psF[0:64, bk, :], func=AF.Identity,
                    bias=par[0:64, colk:colk + 1]), reads=[b_psF[bk], b_par], writes=[b_R3])
            for t8 in range(2):
                bk = bankA()
                for ti in range(8):
                    t = t8 * 8 + ti
                    for c in range(8):
                        mm(psF[:, bk, ti * 64:ti * 64 + 64], hT[:, c, 128 * t:128 * t + 128], wkv[:, c, 64:128],
                           c == 0, c == 7, [bw, b_hT[t // 4]], b_psF[bk])
                S.op("vector", lambda e, bk=bk, t8=t8: e.tensor_tensor(
                    out=VP[:, 8 * t8:8 * t8 + 8, 0:64],
                    in0=psF[:, bk, :].rearrange("p (t n) -> p t n", t=8),
                    in1=bvrep[:].unsqueeze(1).broadcast_to([128, 8, 64]), op=ALU.add),
                    reads=[b_psF[bk], b_bv], writes=[b_R3v])
            blocks = []
            for n in range(16):
                kbs = [n - 1, n] if n > 0 else [n]
                for idx, kb in enumerate(kbs):
                    blocks.append((n, idx, kb, len(kbs)))
            state = {}

            for v_ in range(2):
                S.op("scalar", lambda e, v_=v_: e.activation(out=BIA[bi][:, v_, :], in_=BIA[bi][:, v_, :], func=AF.Exp),
                     reads=[b_BIA[bi]], writes=[b_BIA[bi]])

            def stage1(bi_):
                n, idx, kb, nk = blocks[bi_]
                v = 0 if kb == n else 1
                sb = bankA()
                mm(psF[:, sb, :], KT[0:64, 128 * kb:128 * kb + 128], QT[0:64, :, 128 * n:128 * n + 128],
                   True, True, [b_R1, b_R2, b_R3], b_psF[sb])
                ti = nxt("tmp", 3)
                pi = nxt("pt", 4)
                S.op("scalar", lambda e, sb=sb, ti=ti: e.activation(out=TMP[ti][:], in_=psF[:, sb, :], func=AF.Exp,
                                                                    scale=0.125),
                     reads=[b_psF[sb]], writes=[b_TMP[ti]])
                S.op("gpsimd",
                     lambda e, ti=ti, pi=pi, v=v: e.tensor_tensor(out=PT[pi][:], in0=TMP[ti][:],
                                                                  in1=BIA[bi][:, v, :], op=ALU.mult),
                     reads=[b_TMP[ti], b_BIA[bi]], writes=[b_PT[pi]])
                state[bi_] = pi

            def stage2(bi_):
                n, idx, kb, nk = blocks[bi_]
                pi = state.pop(bi_)
                if idx == 0:
                    state["acc"] = bankB()
                acc = state["acc"]
                mm(psF[:, acc, :], VP[:, kb, :], PT[pi][:], idx == 0, idx == nk - 1,
                   [b_R3v, b_PT[pi]], b_psF[acc])
                if idx != nk - 1:
                    return
                state[("n", n)] = acc

            def stage3(n):
                acc = state.pop(("n", n))
                S.op("vector", lambda e: e.tensor_tensor(
                    out=DEN[0:64, :].rearrange("p (h q) -> p h q", h=4),
                    in0=psF[64:128, acc, :].rearrange("p (h q) -> p h q", h=4),
                    in1=esink[64:128, 4 * jj:4 * jj + 4].unsqueeze(2).broadcast_to([64, 4, 128]), op=ALU.add),
                    reads=[b_psF[acc], b_esink], writes=[b_DEN])
                S.op("scalar", lambda e: e.activation(out=DEN[0:64, :], in_=DEN[0:64, :], func=AF.Ln),
                     reads=[b_DEN], writes=[b_DEN])
                S.op("scalar", lambda e: e.activation(out=DEN[0:64, :], in_=DEN[0:64, :], func=AF.Exp, scale=-1.0),
                     reads=[b_DEN], writes=[b_DEN])
                state[("m", n)] = acc

            def stage4(n):
                acc = state.pop(("m", n))
                for two in range(2):
                    S.op("vector", lambda e, two=two: e.tensor_tensor(
                        out=OT[64 * two:64 * two + 64, :, 128 * n:128 * n + 128],
                        in0=psF[0:64, acc, :].rearrange("p (c two q) -> p c two q", c=2, two=2)[:, :, two, :],
                        in1=DEN[0:64, :].rearrange("p (c two q) -> p c two q", c=2, two=2)[:, :, two, :],
                        op=ALU.mult), reads=[b_psF[acc], b_DEN], writes=[b_R4, b_R4b])

            LOOK = 2
            pend3, pend4 = [], []
            for step in range(len(blocks) + LOOK + 2):
                if pend4:
                    stage4(pend4.pop(0))
                if pend3:
                    n_ = pend3.pop(0)
                    stage3(n_)
                    pend4.append(n_)
                if step < len(blocks):
                    stage1(step)
                b2 = step - LOOK
                if 0 <= b2 < len(blocks):
                    stage2(b2)
                    n, idx, kb, nk = blocks[b2]
                    if idx == nk - 1:
                        pend3.append(n)
            assert not pend3 and not pend4

            for t in range(NT):
                for half in range(2):
                    bk = bankA()
                    for c in range(2):
                        mm(psF[:, bk, :], OT[:, c, 128 * t:128 * t + 128], wo[:, c, 512 * half:512 * half + 512],
                           c == 0, c == 1, [b_R4, b_R4b, bw], b_psF[bk])
                    add_to_X_split(t, half, bk, 2 * t + half)

        def swa_layer(a, W_first, prefetch_next):
            QT = R12[:].rearrange("p (h t) -> p h t", h=4)
            KT = R3[:, 0:2048]
            VP = R3[:, 2048:4096].rearrange("p (t n) -> p t n", t=16)
            OT = R4[:].rearrange("p (c t) -> p c t", c=2)
            bQ = [Buf("swaQ%d" % g) for g in range(4)]
            bK = [Buf("swaK%d" % g) for g in range(4)]
            bV = [Buf("swaV%d" % g) for g in range(4)]
            coarse = [b_R1, b_R2, b_R3, b_R3v]
            S.op("gpsimd", lambda e: e.memset(VP[:, :, 64:128], 1.0), writes=coarse + bQ + bK + bV + [b_DEN] + b_PTx)
            Ws = {0: W_first}
            bi = 0

            def jobs(jj, g):
                W = Ws[jj]
                wq, wkv, bw = W["wq"], W["wkv"], W["b"]
                colk = PC_BK + a * 4 + jj
                out = []

                def q_job(cc):
                    col = PC_BQ + a * 8 + jj * 2 + cc
                    bk = bankA()
                    for c in range(8):
                        mm(psF[:, bk, :], wq[:, c, 128 * cc:128 * cc + 128], hT[:, c, 512 * g:512 * g + 512],
                           c == 0, c == 7, [bw, b_hT[g]], b_psF[bk])
                    S.op("scalar", lambda e: e.activation(
                        out=QT[0:64, 2 * cc, 512 * g:512 * g + 512], in_=psF[0:64, bk, :], func=AF.Identity,
                        bias=par[0:64, col:col + 1]), reads=[b_psF[bk], b_par], writes=[bQ[g]])
                    S.op("scalar", lambda e: e.activation(
                        out=QT[0:64, 2 * cc + 1, 512 * g:512 * g + 512], in_=psF[64:128, bk, :], func=AF.Identity,
                        bias=par[64:128, col:col + 1]), reads=[b_psF[bk], b_par], writes=[bQ[g]])

                def k_job():
                    bk = bankA()
                    for c in range(8):
                        mm(psF[0:64, bk, :], wkv[:, c, 0:64], hT[:, c, 512 * g:512 * g + 512], c == 0, c == 7,
                           [bw, b_hT[g]], b_psF[bk])
                    S.op("scalar", lambda e: e.activation(
                        out=KT[0:64, 512 * g:512 * g + 512], in_=psF[0:64, bk, :], func=AF.Identity,
                        bias=par[0:64, colk:colk + 1]), reads=[b_psF[bk], b_par], writes=[bK[g]])

                def v_job():
                    if g == 0:
                        S.dma("sync", lambda e: e.dma_start(
                            out=bvrep[:], in_=bqkv_d[a][1280 + 64 * jj:1280 + 64 * jj + 64].partition_broadcast(128)),
                            "bv", writes=[b_bv])
                    bk = bankA()
                    for ti in range(4):
                        t = 4 * g + ti
                        for c in range(8):
                            mm(psF[:, bk, ti * 64:ti * 64 + 64], hT[:, c, 128 * t:128 * t + 128], wkv[:, c, 64:128],
                               c == 0, c == 7, [bw, b_hT[g]], b_psF[bk])
                    S.op("vector", lambda e: e.tensor_tensor(
                        out=VP[:, 4 * g:4 * g + 4, 0:64],
                        in0=psF[:, bk, 0:256].rearrange("p (t n) -> p t n", t=4),
                        in1=bvrep[:].unsqueeze(1).broadcast_to([128, 4, 64]), op=ALU.add),
                        reads=[b_psF[bk], b_bv], writes=[bV[g]])

                return [lambda: q_job(0), lambda: q_job(1), k_job, v_job]

            for j0 in jobs(0, 0):
                j0()
            def batch(jj):
                W = Ws[jj]
                wo, bw = W["wo"], W["b"]
                if jj < 3:
                    Ws[jj + 1] = swa_loads(a, jj + 1)
                else:
                    prefetch_next()
                S.dma("sync", lambda e, jj=jj: e.dma_start(out=BIA[bi][:], in_=bias_d[:, jj].rearrange("v k f -> k v f")),
                      "bia%d" % bi, writes=[b_BIA[bi]])
                for v_ in range(2):
                    S.op("scalar", lambda e, v_=v_: e.activation(out=BIA[bi][:, v_, :], in_=BIA[bi][:, v_, :], func=AF.Exp),
                         reads=[b_BIA[bi]], writes=[b_BIA[bi]])
                blocks = []
                idx0 = {}
                for n in range(16):
                    kbs = [n - 1, n] if n > 0 else [n]
                    if n % 4 == 0:
                        idx0[n // 4] = len(blocks)
                    for idx, kb in enumerate(kbs):
                        blocks.append((n, idx, kb, len(kbs)))
                idx0[4] = len(blocks)
                jsched = {}
                for g in range(4):
                    if g < 3:
                        js = jobs(jj, g + 1)
                    elif jj < 3:
                        js = jobs(jj + 1, 0)
                    else:
                        js = []
                    for k, jb in enumerate(js):
                        st_ = min(idx0[g] + 1 + 2 * k, idx0[g + 1] - 1)
                        jsched.setdefault(st_, []).append(jb)
                state = {}

                def stage1(bi_):
                    n, idx, kb, nk = blocks[bi_]
                    v = 0 if kb == n else 1
                    sb = bankA()
                    mm(psF[:, sb, :], KT[0:64, 128 * kb:128 * kb + 128], QT[0:64, :, 128 * n:128 * n + 128],
                       True, True, [bQ[n // 4], bK[kb // 4]] + coarse, b_psF[sb])
                    ti = nxt("tmp", 3)
                    pi = nxt("pt", 4)
                    S.op("scalar", lambda e: e.activation(out=TMP[ti][:], in_=psF[:, sb, :], func=AF.Exp, scale=0.125),
                         reads=[b_psF[sb]], writes=[b_TMP[ti]])
                    S.op("gpsimd", lambda e: e.tensor_tensor(out=PT[pi][:], in0=TMP[ti][:], in1=BIA[bi][:, v, :],
                                                             op=ALU.mult),
                         reads=[b_TMP[ti], b_BIA[bi]], writes=[b_PT[pi]])
                    state[bi_] = pi

                def stage2(bi_):
                    n, idx, kb, nk = blocks[bi_]
                    pi = state.pop(bi_)
                    if idx == 0:
                        state["acc"] = bankB()
                    acc = state["acc"]
                    mm(psF[:, acc, :], VP[:, kb, :], PT[pi][:], idx == 0, idx == nk - 1,
                       [bV[kb // 4], b_R3v, b_PT[pi]], b_psF[acc])
                    if idx == nk - 1:
                        state[("n", n)] = acc

                def stage3(n):
                    acc = state.pop(("n", n))
                    S.op("vector", lambda e: e.tensor_tensor(
                        out=DEN[0:64, :].rearrange("p (h q) -> p h q", h=4),
                        in0=psF[64:128, acc, :].rearrange("p (h q) -> p h q", h=4),
                        in1=esink[64:128, 4 * jj:4 * jj + 4].unsqueeze(2).broadcast_to([64, 4, 128]), op=ALU.add),
                        reads=[b_psF[acc], b_esink], writes=[b_DEN])
                    S.op("scalar", lambda e: e.activation(out=DEN[0:64, :], in_=DEN[0:64, :], func=AF.Ln),
                         reads=[b_DEN], writes=[b_DEN])
                    S.op("scalar", lambda e: e.activation(out=DEN[0:64, :], in_=DEN[0:64, :], func=AF.Exp, scale=-1.0),
                         reads=[b_DEN], writes=[b_DEN])
                    state[("m", n)] = acc

                def stage4(n):
                    acc = state.pop(("m", n))
                    for two in range(2):
                        S.op("vector", lambda e, two=two: e.tensor_tensor(
                            out=OT[64 * two:64 * two + 64, :, 128 * n:128 * n + 128],
                            in0=psF[0:64, acc, :].rearrange("p (c two q) -> p c two q", c=2, two=2)[:, :, two, :],
                            in1=DEN[0:64, :].rearrange("p (c two q) -> p c two q", c=2, two=2)[:, :, two, :],
                            op=ALU.mult), reads=[b_psF[acc], b_DEN], writes=[b_R4, b_R4b])

                LOOK = 3
                pend3, pend4 = [], []
                for step in range(len(blocks) + LOOK + 2):
                    if pend4:
                        stage4(pend4.pop(0))
                    if pend3:
                        n_ = pend3.pop(0)
                        stage3(n_)
                        pend4.append(n_)
                    if step < len(blocks):
                        stage1(step)
                    b2 = step - LOOK
                    if 0 <= b2 < len(blocks):
                        stage2(b2)
                        n, idx, kb, nk = blocks[b2]
                        if idx == nk - 1:
                            pend3.append(n)
                    for jb in jsched.pop(step, []):
                        jb()
                assert not pend3 and not pend4 and not jsched
                for t in range(NT):
                    for half in range(2):
                        bk = bankA()
                        for c in range(2):
                            mm(psF[:, bk, :], OT[:, c, 128 * t:128 * t + 128], wo[:, c, 512 * half:512 * half + 512],
                               c == 0, c == 1, [b_R4, b_R4b, bw], b_psF[bk])
                        add_to_X_split(t, half, bk, 2 * t + half)
                arena_done.add(bw.name)

            for jj_ in range(4):
                batch(jj_)

        def swa_layer_pre(a):
            S.dma("sync", lambda e: e.dma_start(out=sinkraw[:], in_=sinks_d[a].partition_broadcast(128)),
                  "sink", writes=[b_sinkraw])
            S.op("scalar", lambda e: e.activation(out=esink[:], in_=sinkraw[:], func=AF.Exp),
                 reads=[b_sinkraw], writes=[b_esink])
            S.dma("sync", lambda e: e.dma_start(out=borep[:], in_=boa_d[a].partition_broadcast(128)),
                  "borep", writes=[b_borep])
            for t in range(NT):
                S.op("gpsimd", lambda e, t=t: e.tensor_tensor(out=X[:, t, :], in0=X[:, t, :], in1=borep[:], op=ALU.add),
                     reads=[bX[t], b_borep], writes=[bX[t]])

        def fox_pre_loads(a):
            off, b, over = arena_alloc(128, "wf%d" % a)
            wf = WA[:, off:off + 128].rearrange("p (c n) -> p c n", c=8)
            wload(wf, wqkvf_d[a].rearrange("(c p) n -> p c n", p=128)[:, :, 3072:3088], b, over, True)
            return dict(wf=wf, b=b)

        def fox_pre(a, W):
            wf, bw = W["wf"], W["b"]
            VP = R3[:].rearrange("p (t s n) -> p t s n", t=16, s=2)
            S.op("gpsimd", lambda e: e.memset(ones16, 1.0), writes=[b_borep, b_DEN] + b_PTx)
            S.op("gpsimd", lambda e: e.memset(VP[:, :, 0, 64:128], 1.0), writes=[b_R3v, b_R3])
            S.op("gpsimd", lambda e: e.memset(VP[:, :, 1, 0:64], 1.0), reads=[b_R3v], writes=[b_R3v])
            for g in range(4):
                bk = bankA()
                for c in range(8):
                    mm(psF[0:16, bk, :], wf[:, c, :], hT[:, c, 512 * g:512 * g + 512], c == 0, c == 7,
                       [bw, b_hT[g]], b_psF[bk])
                t1 = nxt("tmp", 3)
                S.op("scalar", lambda e, bk=bk, t1=t1: e.activation(out=TMP[t1][0:16, :], in_=psF[0:16, bk, :],
                                                                    func=AF.Exp, scale=-1.0, bias=nbf[:, a:a + 1]),
                     reads=[b_psF[bk], b_nbf], writes=[b_TMP[t1]])
                S.op("scalar", lambda e, t1=t1: e.activation(out=TMP[t1][0:16, :], in_=TMP[t1][0:16, :], func=AF.Ln,
                                                             bias=1.0), reads=[b_TMP[t1]], writes=[b_TMP[t1]])
                t2 = nxt("tmp", 3)
                if g == 0:
                    S.op("vector", lambda e, t1=t1, t2=t2: e.tensor_tensor_scan(
                        out=TMP[t2][0:16, :], data0=ones16, data1=TMP[t1][0:16, :], initial=0.0,
                        op0=ALU.mult, op1=ALU.subtract), reads=[b_borep, b_TMP[t1]], writes=[b_TMP[t2]])
                else:
                    S.op("vector", lambda e, t1=t1, t2=t2, g=g: e.tensor_tensor_scan(
                        out=TMP[t2][0:16, :], data0=ones16, data1=TMP[t1][0:16, :], initial=carry[:, g - 1:g],
                        op0=ALU.mult, op1=ALU.subtract), reads=[b_borep, b_TMP[t1], b_carry], writes=[b_TMP[t2]])
                S.op("vector", lambda e, t2=t2, g=g: e.tensor_copy(out=carry[:, g:g + 1], in_=TMP[t2][0:16, 511:512]),
                     reads=[b_TMP[t2]], writes=[b_carry])
                cs = slice(512 * g, 512 * g + 512)
                S.op("vector", lambda e, t2=t2, cs=cs: e.tensor_scalar(out=AUG[0:16, cs], in0=TMP[t2][0:16, :], scalar1=8.0,
                                                                       scalar2=None, op0=ALU.mult),
                     reads=[b_TMP[t2]], writes=[b_AUG])
                S.op("vector", lambda e, t1=t1, t2=t2, cs=cs: e.scalar_tensor_tensor(
                    out=TMP[t1][0:16, :], in0=TMP[t2][0:16, :], scalar=8.0, in1=AUG[0:16, cs], op0=ALU.mult,
                    op1=ALU.subtract), reads=[b_TMP[t2], b_AUG], writes=[b_TMP[t1]])
                S.op("vector", lambda e, t1=t1: e.tensor_copy(out=midS[:], in_=TMP[t1][0:16, :]),
                     reads=[b_TMP[t1]], writes=[b_midS])
                S.op("vector", lambda e, cs=cs: e.tensor_copy(out=AUG[32:48, cs], in_=midS[:]),
                     reads=[b_midS], writes=[b_AUG])
                S.op("vector", lambda e, t1=t1, t2=t2: e.tensor_tensor(out=TMP[t2][0:16, :], in0=TMP[t1][0:16, :],
                                                                       in1=midS[:], op=ALU.subtract),
                     reads=[b_TMP[t1], b_midS], writes=[b_TMP[t2]])
                S.op("vector", lambda e, t2=t2, cs=cs: e.tensor_copy(out=AUG[64:80, cs], in_=TMP[t2][0:16, :]),
                     reads=[b_TMP[t2]], writes=[b_AUG])

        def fox_loads(a, b_):
            off, b, over = arena_alloc(4096, "fox%d_%d" % (a, b_))
            wq = WA[:, off:off + 1024].rearrange("p (c n) -> p c n", c=8)
            wk = WA[:, off + 1024:off + 2048].rearrange("p (c n) -> p c n", c=8)
            wv = WA[:, off + 2048:off + 3072].rearrange("p (c n) -> p c n", c=8)
            wo = WA[:, off + 3072:off + 4096]
            src = wqkvf_d[a].rearrange("(c p) n -> p c n", p=128)
            wload(wq, src[:, :, 128 * b_:128 * b_ + 128], b, over, True)
            wload(wk, src[:, :, 1024 + 128 * b_:1024 + 128 * b_ + 128], b, over, False)
            wload(wv, src[:, :, 2048 + 128 * b_:2048 + 128 * b_ + 128], b, over, False)
            wload(wo, wob_d[a][128 * b_:128 * b_ + 128, :], b, over, False)
            return dict(wq=wq, wk=wk, wv=wv, wo=wo, b=b)

        def fox_batch(a, b_, W):
            wq, wk, wv, wo, bw = W["wq"], W["wk"], W["wv"], W["wo"], W["b"]
            QA = R12[:, 0:4096].rearrange("p (s t) -> p s t", s=2)
            KA = R12[:, 4096:8192].rearrange("p (s t) -> p s t", s=2)
            VP = R3[:].rearrange("p (t s n) -> p t s n", t=16, s=2)
            OT = R4[:, 2048 * (b_ % 2):2048 * (b_ % 2) + 2048]
            bOT = b_R4 if b_ % 2 == 0 else b_R4b
            extra = [b_R1, b_R2] if b_ == 0 else []
            for s in range(2):
                S.op("gpsimd", lambda e, s=s: e.memset(QA[64:70, s, :], -1.0), writes=[b_QAaug[s]] + extra)
                S.op("gpsimd", lambda e, s=s: e.memset(KA[64:70, s, :], 1.0), writes=[b_KAaug[s]] + extra)
            for s in range(2):
                h = 2 * b_ + s
                src = bass.AP(AUG, h * S_LEN, [[32 * S_LEN, 3], [1, S_LEN]])
                S.dma("sync", lambda e, s=s, src=src: e.dma_start(out=QA[64:67, s, :], in_=src), "augq%d" % s,
                      reads=[b_AUG], writes=[b_QAaug[s]])
                S.dma("sync", lambda e, s=s, src=src: e.dma_start(out=KA[67:70, s, :], in_=src), "augk%d" % s,
                      reads=[b_AUG], writes=[b_KAaug[s]])
            for (w_, dst, bR) in ((wq, QA, b_R1), (wk, KA, b_R2)):
                for g in range(4):
                    bk = bankA()
                    for c in range(8):
                        mm(psF[:, bk, :], w_[:, c, :], hT[:, c, 512 * g:512 * g + 512], c == 0, c == 7,
                           [bw, b_hT[g]], b_psF[bk])
                    S.op("scalar", lambda e, bk=bk, g=g, dst=dst: e.activation(
                        out=dst[0:64, 0, 512 * g:512 * g + 512], in_=psF[0:64, bk, :], func=AF.Copy),
                        reads=[b_psF[bk]], writes=[bR])
                    S.op("vector", lambda e, bk=bk, g=g, dst=dst: e.tensor_copy(
                        out=dst[0:64, 1, 512 * g:512 * g + 512], in_=psF[64:128, bk, :]),
                        reads=[b_psF[bk]], writes=[bR])
            for g in range(4):
                bk = bankA()
                for c in range(8):
                    mm(psF[:, bk, :], wv[:, c, :], hT[:, c, 512 * g:512 * g + 512], c == 0, c == 7,
                       [bw, b_hT[g]], b_psF[bk])
                pi = nxt("pt", 4)
                S.op("scalar", lambda e, bk=bk, pi=pi: e.activation(out=PT[pi][:], in_=psF[:, bk, :], func=AF.Copy),
                     reads=[b_psF[bk]], writes=[b_PT[pi]])
                ti = nxt("T", 2)
                for i4 in range(4):
                    S.op("tensor", lambda e, pi=pi, ti=ti, i4=i4: e.transpose(
                        out=psTv[ti][:, 128 * i4:128 * i4 + 128], in_=PT[pi][:, 128 * i4:128 * i4 + 128],
                        identity=ident[:]), reads=[b_PT[pi], b_ident], writes=[b_psT[ti]])
                S.op("vector", lambda e, ti=ti, g=g: e.tensor_copy(
                    out=VP[:, 4 * g:4 * g + 4, 0, 0:64],
                    in_=psTv[ti][:, 0:512].rearrange("p (t n) -> p t n", t=4)[:, :, 0:64]),
                    reads=[b_psT[ti]], writes=[b_R3v])
                S.op("vector", lambda e, ti=ti, g=g: e.tensor_copy(
                    out=VP[:, 4 * g:4 * g + 4, 1, 64:128],
                    in_=psTv[ti][:, 0:512].rearrange("p (t n) -> p t n", t=4)[:, :, 64:128]),
                    reads=[b_psT[ti]], writes=[b_R3v])
            blocks = []
            for s_ in range(2):
                for G in range(4):
                    nkb = 4 * G + 4
                    for j in range(nkb):
                        blocks.append((s_, G, j, nkb))
            state = {}

            def stage1(bi_):
                s_, G, j, nkb = blocks[bi_]
                i = j - 4 * G
                qoff = 128 * i if i > 0 else 0
                ncols = 512 - qoff
                q0 = 512 * G + qoff
                sb = nxt("fsc", 6)
                mm(psF[:, sb, 0:ncols], KA[0:70, s_, 128 * j:128 * j + 128], QA[0:70, s_, q0:q0 + ncols],
                   True, True, [b_R1, b_R2, b_QAaug[s_], b_KAaug[s_]], b_psF[sb])
                pi = nxt("ptf", 6)
                S.op("scalar", lambda e, sb=sb, pi=pi, ncols=ncols: e.activation(
                    out=PTF[pi][:, 0:ncols], in_=psF[:, sb, 0:ncols], func=AF.Exp, scale=0.125),
                    reads=[b_psF[sb]], writes=[b_PTF[pi]])
                if i >= 0:
                    S.op("gpsimd", lambda e, pi=pi: e.tensor_tensor(out=PTF[pi][:, 0:128], in0=PTF[pi][:, 0:128],
                                                                    in1=tri[:], op=ALU.mult),
                         reads=[b_PTF[pi], b_tri], writes=[b_PTF[pi]])
                state[bi_] = (pi, qoff, ncols)

            def stage2(bi_):
                s_, G, j, nkb = blocks[bi_]
                pi, qoff, ncols = state.pop(bi_)
                if j == 0:
                    state["acc"] = 6 + nxt("facc", 2)
                acc = state["acc"]
                mm(psF[:, acc, qoff:512], VP[:, j, s_, :], PTF[pi][:, 0:ncols], j == 0, j == nkb - 1,
                   [b_R3v, b_PTF[pi]], b_psF[acc])
                if j != nkb - 1:
                    return
                ti = nxt("tmp", 3)
                lo, hi = (0, 64) if s_ == 0 else (64, 128)
                slo, shi = (64, 128) if s_ == 0 else (0, 64)
                S.op("vector", lambda e, acc=acc, ti=ti, lo=lo, hi=hi, slo=slo, shi=shi: e.reciprocal(
                    out=TMP[ti][lo:hi, :], in_=psF[slo:shi, acc, :]), reads=[b_psF[acc]], writes=[b_TMP[ti]])
                S.op("vector", lambda e, acc=acc, ti=ti, lo=lo, hi=hi, G=G: e.tensor_tensor(
                    out=OT[lo:hi, 512 * G:512 * G + 512], in0=psF[lo:hi, acc, :], in1=TMP[ti][lo:hi, :],
                    op=ALU.mult), reads=[b_psF[acc], b_TMP[ti]], writes=[bOT])

            LOOK = 5
            for step in range(len(blocks) + LOOK):
                if step < len(blocks):
                    stage1(step)
                if step - LOOK >= 0:
                    stage2(step - LOOK)
            fox_state["wo%d" % (b_ % 2)] = (wo, bw)
            if b_ % 2 == 1:
                OT2 = R4[:].rearrange("p (c t) -> p c t", c=2)
                for t in range(NT):
                    for half in range(2):
                        bk = bankA()
                        for c in range(2):
                            wo_c, bw_c = fox_state["wo%d" % c]
                            mm(psF[:, bk, :], OT2[:, c, 128 * t:128 * t + 128], wo_c[:, 512 * half:512 * half + 512],
                               c == 0, c == 1, [b_R4, b_R4b, bw_c], b_psF[bk])
                        add_to_X_split(t, half, bk, 2 * t + half)

        def mlp_load_up(l, q4):
            off, b, over = arena_alloc(8192, "mup%d_%d" % (l, q4), align=8192)
            wu = WA[:, off:off + 8192].rearrange("p (c n) -> p c n", c=8)
            src = wup_d[l].rearrange("(c p) n -> p c n", p=128)
            wload(wu[:, :, 0:512], src[:, :, 1024 * q4:1024 * q4 + 512], b, over, True)
            wload(wu[:, :, 512:1024], src[:, :, 1024 * q4 + 512:1024 * q4 + 1024], b, over, False)
            return dict(wu=wu, b=b)

        def mlp_load_dn(l, q4):
            off, b, over = arena_alloc(8192, "mdn%d_%d" % (l, q4), align=8192)
            wd = WA[:, off:off + 8192].rearrange("p (f n) -> p f n", f=8)
            src = wdn_d[l][1024 * q4:1024 * q4 + 1024, :].rearrange("(f p) n -> p f n", p=128)
            wload(wd[:, 0:4, :], src[:, 0:4, :], b, over, True)
            wload(wd[:, 4:8, :], src[:, 4:8, :], b, over, False)
            return dict(wd=wd, b=b)

        def mlp_pass(l, q4, W0, W1, hook, next_gcol=None):
            pending_tiles = []
            wu, bwu = W0["wu"], W0["b"]
            wd, bwd = W1["wd"], W1["b"]
            for g in range(4):
                ai = g % 2
                aT = (R12[:, 0:4096] if ai == 0 else R12[:, 4096:8192]).rearrange("p (f t) -> p f t", f=8)
                bA = b_R1 if ai == 0 else b_R2
                for f in range(8):
                    bk = bankA()
                    for c in range(8):
                        mm(psF[:, bk, :], wu[:, c, 128 * f:128 * f + 128], hT[:, c, 512 * g:512 * g + 512],
                           c == 0, c == 7, [bwu, b_hT[g]], b_psF[bk])
                    ti = nxt("tmp", 3)
                    S.op("scalar", lambda e, bk=bk, ti=ti: e.activation(out=TMP[ti][:], in_=psF[:, bk, :], func=AF.Relu),
                         reads=[b_psF[bk]], writes=[b_TMP[ti]])
                    S.op("gpsimd", lambda e, ti=ti, f=f, aT=aT: e.tensor_tensor(out=aT[:, f, :], in0=TMP[ti][:],
                                                                                in1=TMP[ti][:], op=ALU.mult),
                         reads=[b_TMP[ti]], writes=[bA])
                if g == 3:
                    arena_done.add(bwu.name)
                    hook()
                if next_gcol is not None and pending_tiles:
                    norm_group_tiles(pending_tiles.pop(0), next_gcol)
                for tt in range(4):
                    t = 4 * g + tt
                    for half in range(2):
                        bk = bankB()
                        for f in range(8):
                            mm(psF[:, bk, :], aT[:, f, 128 * tt:128 * tt + 128], wd[:, f, 512 * half:512 * half + 512],
                               f == 0, f == 7, [bA, bwd], b_psF[bk])
                        add_to_X(t, half, bk)
                if next_gcol is not None:
                    norm_group_stats(g)
                    pending_tiles.append(g)
            if next_gcol is not None:
                while pending_tiles:
                    norm_group_tiles(pending_tiles.pop(0), next_gcol)
                norm_done.add(next_gcol)

        units = []

        def U(l0, l1, comp, self_pf=False):
            units.append(dict(l0=l0, l1=l1, comp=comp, self_pf=self_pf))

        for l in layers:
            a = l // 2
            U(None, None, lambda W0, W1, hook, l=l: (None if (PC_GMIX + l * 8) in norm_done else norm_phase(PC_GMIX + l * 8)))
            if l % 2 == 0:
                U(None, None, lambda W0, W1, hook, a=a: swa_layer_pre(a))
                U(lambda a=a: swa_loads(a, 0), None, lambda W0, W1, hook, pf, a=a: swa_layer(a, W0, pf), self_pf=True)
            else:
                U(lambda a=a: fox_pre_loads(a), None, lambda W0, W1, hook, a=a: fox_pre(a, W0))
                for b_ in range(8):
                    U(lambda a=a, b_=b_: fox_loads(a, b_), None, lambda W0, W1, hook, a=a, b_=b_: fox_batch(a, b_, W0))
            U(None, None, lambda W0, W1, hook, l=l: norm_phase(PC_GMLP + l * 8))
            for q4 in range(4):
                U(lambda l=l, q4=q4: mlp_load_up(l, q4), lambda l=l, q4=q4: mlp_load_dn(l, q4),
                  lambda W0, W1, hook, l=l, q4=q4: mlp_pass(
                      l, q4, W0, W1, hook,
                      next_gcol=(PC_GMIX + (l + 1) * 8) if (q4 == 3 and (l + 1) in layers) else None))

        L0, L1 = {}, {}

        def load0(i):
            if i < len(units) and units[i]["l0"] is not None and i not in L0:
                L0[i] = units[i]["l0"]()

        def load1(i):
            if i < len(units) and units[i]["l1"] is not None and i not in L1:
                L1[i] = units[i]["l1"]()

        def next_loader(i):
            j = i + 1
            while j < len(units) and units[j]["l0"] is None:
                j += 1
            return j

        first = 0 if units[0]["l0"] is not None else next_loader(0)
        load0(first)
        load1(first)
        for i, u in enumerate(units):
            if u["l0"] is None:
                u["comp"](None, None, None)
                continue
            load0(i)
            load1(i)
            nj = next_loader(i)
            if not u["self_pf"]:
                load0(nj)
            called = [False]

            def hook(nj=nj, called=called):
                if not called[0]:
                    called[0] = True
                    load1(nj)

            if u["self_pf"]:
                u["comp"](L0[i], L1.get(i), hook, lambda nj=nj: load0(nj))
                load0(nj)
            else:
                u["comp"](L0[i], L1.get(i), hook)
            arena_done.add(L0[i]["b"].name)
            if i in L1:
                arena_done.add(L1[i]["b"].name)
            hook()

        if final:
            S.dma("sync", lambda e: e.dma_start(out=borep[:], in_=gfin_d.partition_broadcast(128)),
                  "borep", writes=[b_borep])
            rms_stats()
            for t in range(NT):
                S.op("vector", lambda e, t=t: e.scalar_tensor_tensor(
                    out=X[:, t, :], in0=X[:, t, :], scalar=rstd[:, t:t + 1], in1=borep[:], op0=ALU.mult, op1=ALU.mult),
                    reads=[bX[t], b_rstd, b_borep], writes=[bX[t]])
        ov = out_d.rearrange("(t p) d -> p t d", p=128)
        outs = []
        for t4 in range(4):
            outs.append(S.dma("sync", lambda e, t4=t4: e.dma_start(out=ov[:, 4 * t4:4 * t4 + 4, :],
                                                                    in_=X[:, 4 * t4:4 * t4 + 4, :]),
                              "out%d" % t4, reads=bX[4 * t4:4 * t4 + 4]))
        with nc.Block() as block:
            S.emit(block, final_wait_ops=outs)
    return nc


_PROGRAMS = {}


def _get_program(key):
    if key not in _PROGRAMS:
        _PROGRAMS[key] = build(*key)
    return _PROGRAMS[key]


LAUNCH_PLAN = [((0, 1, 2, 3), True)]


def kernel(x, rel_bias, norm_mix, norm_mlp, w_qkv_a, b_qkv_a, sinks_a, w_o_a, b_o_a,
           w_qkvf_b, b_f_b, w_o_b, w_up, w_down, norm_final):
    f = lambda a: np.ascontiguousarray(np.asarray(a, dtype=np.float32))
    x = f(x)
    params = _layout_params(f(norm_mix), f(norm_mlp), f(b_qkv_a), f(b_f_b))
    biasT = _layout_bias(f(rel_bias)).reshape(2, 4, 128, 512)
    shared = {"params": params, "biasT": biasT, "w_qkv_a": f(w_qkv_a), "b_qkv_a": f(b_qkv_a),
              "sinks_a": f(sinks_a), "w_o_a": f(w_o_a), "b_o_a": f(b_o_a), "w_qkvf_b": f(w_qkvf_b),
              "w_o_b": f(w_o_b), "w_up": f(w_up), "w_down": f(w_down), "norm_final": f(norm_final)}
    cur = [x[b] for b in range(8)]
    for key in LAUNCH_PLAN:
        nc = _get_program(key)
        in_maps = [dict(shared, x=cur[b]) for b in range(8)]
        res = run_bass_kernel_spmd(nc, in_maps, core_ids=list(range(8)))
        cur = [np.asarray(res.results[b]["out"], dtype=np.float32) for b in range(8)]
    return np.stack(cur, axis=0).astype(np.float32)
```

```python
from contextlib import ExitStack
import math
import numpy as np
import concourse.bass as bass
import concourse.mybir as mybir
from concourse.bass_utils import run_bass_kernel_spmd

F32 = mybir.dt.float32
BF16 = mybir.dt.bfloat16
AF = mybir.ActivationFunctionType
ALU = mybir.AluOpType

S_LEN = 2048
D = 1024
DFF = 4096
NT = 16
DEPTH = 4
NEG = -1e30
EPS = 1e-6

ENGS = ("tensor", "scalar", "vector", "gpsimd", "sync")
STRICT = True


class Buf:
    __slots__ = ("name", "writers", "readers")

    def __init__(self, name):
        self.name = name
        self.writers = []
        self.readers = []


class Op:
    __slots__ = ("idx", "eng", "fn", "deps", "wdeps", "is_dma", "sem", "semval", "milestone", "needed", "eff")

    def __init__(self, idx, eng, fn, is_dma=False):
        self.idx = idx
        self.eng = eng
        self.fn = fn
        self.deps = set()
        self.wdeps = set()
        self.eff = ()
        self.is_dma = is_dma
        self.sem = None
        self.semval = 0
        self.milestone = 0
        self.needed = False


class Sched:
    def __init__(self, nc, stack):
        self.nc = nc
        self.stack = stack
        self.ops = []
        self.dma_sems = {}
        self.eng_sem = {}
        for e in ENGS:
            self.eng_sem[e] = stack.enter_context(nc.semaphore("es_" + e))

    def _dma_sem(self, key):
        if key not in self.dma_sems:
            h = self.stack.enter_context(self.nc.semaphore("ds_%s" % key))
            self.dma_sems[key] = [h, 0]
        return self.dma_sems[key]

    def _add(self, op, reads, writes):
        for b in reads:
            for w in b.writers:
                op.deps.add(w)
        for b in writes:
            for w in b.writers:
                op.wdeps.add(w)
            for r in b.readers:
                op.wdeps.add(r)
        op.deps.discard(op.idx)
        op.wdeps.discard(op.idx)
        for b in reads:
            b.readers.append(op.idx)
        for b in writes:
            b.writers = [op.idx]
            b.readers = []
        self.ops.append(op)
        return op

    def op(self, eng, fn, reads=(), writes=()):
        return self._add(Op(len(self.ops), eng, fn), reads, writes)

    def dma(self, eng, fn, semkey, reads=(), writes=(), join=False):
        o = Op(len(self.ops), eng, fn, is_dma=True)
        s = self._dma_sem(semkey)
        s[1] += 16
        o.sem = s[0]
        o.semval = s[1]
        if join:
            for b in reads:
                for w in b.writers:
                    o.deps.add(w)
                b.readers.append(o.idx)
            for b in writes:
                for w in b.writers:
                    o.deps |= self.ops[w].deps
                    o.wdeps |= self.ops[w].wdeps
                b.writers = b.writers + [o.idx]
            self.ops.append(o)
            return o
        return self._add(o, reads, writes)

    def emit(self, block, final_wait_ops=()):
        ops = self.ops
        for o in ops:
            eff = []
            for d in o.deps:
                p = ops[d]
                if (not p.is_dma) and p.eng == o.eng and p.eng == "tensor":
                    continue
                eff.append(d)
            for d in o.wdeps:
                if d in o.deps:
                    continue
                p = ops[d]
                if (not p.is_dma) and p.eng == o.eng and not o.is_dma and (p.eng == "tensor" or not STRICT):
                    continue
                eff.append(d)
            o.eff = eff
            for d in eff:
                if not ops[d].is_dma:
                    ops[d].needed = True
        cnt = {e: 0 for e in ENGS}
        for o in ops:
            if not o.is_dma and o.needed:
                cnt[o.eng] += 1
                o.milestone = cnt[o.eng]
        per_eng = {e: [] for e in ENGS}
        for o in ops:
            per_eng[o.eng].append(o)
        sched = self

        def run(engname, eng):
            waited = {}
            for o in per_eng[engname]:
                need = {}
                for d in o.eff:
                    p = ops[d]
                    if p.is_dma:
                        key = ("d", id(p.sem))
                        v = p.semval
                        h = p.sem
                    else:
                        key = ("e", p.eng)
                        v = p.milestone
                        h = sched.eng_sem[p.eng]
                    if need.get(key, (None, 0))[1] < v:
                        need[key] = (h, v)
                for key, (h, v) in need.items():
                    if waited.get(key, 0) >= v:
                        continue
                    eng.wait_ge(h, v)
                    waited[key] = v
                ins = o.fn(eng)
                if o.is_dma:
                    ins.then_inc(o.sem, 16)
                elif o.needed:
                    ins.then_inc(sched.eng_sem[engname], 1)
            if engname == "sync":
                for o in final_wait_ops:
                    eng.wait_ge(o.sem, o.semval)

        @block.tensor
        def _(e):
            run("tensor", e)

        @block.scalar
        def _(e):
            run("scalar", e)

        @block.vector
        def _(e):
            run("vector", e)

        @block.gpsimd
        def _(e):
            run("gpsimd", e)

        @block.sync
        def _(e):
            run("sync", e)


NPAR = 96
PC_GMIX, PC_GMLP, PC_BQ, PC_BK, PC_BF = 0, 32, 64, 80, 88


def _t5_bucket_np(dist):
    n_buckets, max_distance = 32, 128
    max_exact = n_buckets // 2
    d = np.maximum(dist, 0)
    dl = np.maximum(d, 1).astype(np.float32)
    large = max_exact + (np.log(dl / np.float32(max_exact)) / np.float32(math.log(max_distance / max_exact))
                         * np.float32(n_buckets - max_exact)).astype(np.int32)
    large = np.minimum(large, n_buckets - 1)
    return np.where(d < max_exact, d, large)


def _bucket_table():
    return _t5_bucket_np(np.arange(256, dtype=np.int32))


def _layout_params(norm_mix, norm_mlp, b_qkv_a, b_f_b):
    p = np.zeros((128, NPAR), np.float32)
    for l in range(DEPTH):
        p[:, PC_GMIX + l * 8:PC_GMIX + l * 8 + 8] = norm_mix[l].reshape(8, 128).T
        p[:, PC_GMLP + l * 8:PC_GMLP + l * 8 + 8] = norm_mlp[l].reshape(8, 128).T
    for a in range(2):
        p[:, PC_BQ + a * 8:PC_BQ + a * 8 + 8] = b_qkv_a[a][0:1024].reshape(8, 128).T
        p[0:64, PC_BK + a * 4:PC_BK + a * 4 + 4] = b_qkv_a[a][1024:1280].reshape(4, 64).T
        p[0:16, PC_BF + a] = b_f_b[a]
    return p


def _layout_bias(rel_bias):
    bt = _bucket_table()
    k = np.arange(128)[:, None]
    q = np.arange(128)[None, :]
    out = np.empty((2, 4, 128, 4, 128), np.float32)
    for v in range(2):
        dist = q - k + (128 if v == 1 else 0)
        valid = (dist >= 0) & (dist < 128)
        idx = bt[np.clip(dist, 0, 255)]
        for h in range(16):
            g = rel_bias[idx, h]
            out[v, h // 4, :, h % 4, :] = np.where(valid, g, np.float32(NEG))
    return out


def build(layers=(0, 1, 2, 3), final=True):
    nc = bass.Bass("TRN2", target_bir_lowering=False)
    x_d = nc.dram_tensor("x", [S_LEN, D], F32, kind="ExternalInput").ap()
    par_d = nc.dram_tensor("params", [128, NPAR], F32, kind="ExternalInput").ap()
    bias_d = nc.dram_tensor("biasT", [2, 4, 128, 512], F32, kind="ExternalInput").ap()
    wqkv_d = nc.dram_tensor("w_qkv_a", [2, D, 1536], F32, kind="ExternalInput").ap()
    bqkv_d = nc.dram_tensor("b_qkv_a", [2, 1536], F32, kind="ExternalInput").ap()
    sinks_d = nc.dram_tensor("sinks_a", [2, 16], F32, kind="ExternalInput").ap()
    woa_d = nc.dram_tensor("w_o_a", [2, D, D], F32, kind="ExternalInput").ap()
    boa_d = nc.dram_tensor("b_o_a", [2, D], F32, kind="ExternalInput").ap()
    wqkvf_d = nc.dram_tensor("w_qkvf_b", [2, D, 3088], F32, kind="ExternalInput").ap()
    wob_d = nc.dram_tensor("w_o_b", [2, D, D], F32, kind="ExternalInput").ap()
    wup_d = nc.dram_tensor("w_up", [DEPTH, D, DFF], F32, kind="ExternalInput").ap()
    wdn_d = nc.dram_tensor("w_down", [DEPTH, DFF, D], F32, kind="ExternalInput").ap()
    gfin_d = nc.dram_tensor("norm_final", [D], F32, kind="ExternalInput").ap()
    out_d = nc.dram_tensor("out", [S_LEN, D], F32, kind="ExternalOutput").ap()

    with ExitStack() as st:
        S = Sched(nc, st)

        def T(name, shape, dt):
            return st.enter_context(nc.sbuf_tensor(name, shape, dt))

        X = T("X", [128, NT, D], F32)
        hT = T("hT", [128, 8, S_LEN], BF16)
        ARENA = 24576
        WA = T("WA", [128, ARENA], BF16)
        R12 = T("R12", [128, 8192], BF16)
        R3 = T("R3", [128, 4096], BF16)
        R4 = T("R4", [128, 4096], BF16)
        PT = [T("PT%d" % i, [128, 512], BF16) for i in range(4)]
        TMP = [T("TMP%d" % i, [128, 512], F32) for i in range(3)]
        BIA = [T("BIA0", [128, 2, 512], F32)]
        ident = T("ident", [128, 128], BF16)
        identf = T("identf", [128, 128], F32)
        tri = T("tri", [128, 128], BF16)
        par = T("par", [128, NPAR], F32)
        ss = T("ss", [128, 16], F32)
        ms = T("ms", [128, 16], F32)
        rstd = T("rstd", [128, 16], F32)
        XN = [T("XN%d" % i, [128, D], BF16) for i in range(2)]
        junk = XN[1]
        bvrep = T("bvrep", [128, 64], F32)
        DEN = T("DEN", [128, 512], F32)
        DENb = DEN[:].bitcast(BF16)
        PTX = [DENb[:, 0:512], DENb[:, 512:1024]]
        sinkraw = T("sinkraw", [128, 16], F32)
        esink = T("esink", [128, 16], F32)
        borep = T("borep", [128, D], F32)
        AUG = T("AUG", [128, S_LEN], BF16)
        ones16 = borep[0:16, 0:512]
        midS = T("midS", [16, 512], BF16)
        carry = T("carry", [16, 4], F32)
        nbf = T("nbf", [16, 2], F32)
        psF = st.enter_context(nc.psum_tensor("psF", [128, 8, 512], F32))
        psTv = [psF[:, 6, :].bitcast(BF16), psF[:, 7, :].bitcast(BF16)]

        bX = [Buf("X%d" % t) for t in range(NT)]
        b_hT = [Buf("hT%d" % g) for g in range(4)]
        b_R1, b_R2, b_R3, b_R4 = Buf("R1"), Buf("R2"), Buf("R3"), Buf("R4")
        b_R3v = Buf("R3v")
        b_R4b = Buf("R4b")
        b_QAaug = [Buf("QAaug0"), Buf("QAaug1")]
        b_KAaug = [Buf("KAaug0"), Buf("KAaug1")]
        fox_state = {}
        b_PT = [Buf("PT%d" % i) for i in range(4)]
        b_TMP = [Buf("TMP%d" % i) for i in range(3)]
        b_BIA = [Buf("BIA0")]
        b_ident, b_identf, b_tri, b_par = Buf("ident"), Buf("identf"), Buf("tri"), Buf("par")
        b_ss, b_ms, b_rstd = Buf("ss"), Buf("ms"), Buf("rstd")
        b_ssg = [Buf("ssg%d" % g) for g in range(4)]
        b_msg = [Buf("msg%d" % g) for g in range(4)]
        b_rstdg = [Buf("rstdg%d" % g) for g in range(4)]
        norm_done = set()
        b_XN = [Buf("XN0"), Buf("XN1")]
        b_junk = b_XN[1]
        b_DEN = Buf("DEN")
        b_PTx = [Buf("PTx0"), Buf("PTx1")]
        b_bv, b_sinkraw, b_esink, b_borep = Buf("bv"), Buf("sinkraw"), Buf("esink"), Buf("borep")
        b_AUG, b_midS, b_carry, b_nbf, b_ones16 = Buf("AUG"), Buf("midS"), Buf("carry"), Buf("nbf"), Buf("ones16")
        b_psF = [Buf("psF%d" % i) for i in range(8)]
        b_psT = [b_psF[6], b_psF[7]]

        rr = {"A": 0, "B": 0, "T": 0, "pt": 0, "tmp": 0, "xn": 0, "ptf": 0}
        PTF = PT + PTX
        b_PTF = b_PT + b_PTx

        def bankA():
            i = rr["A"] % 4
            rr["A"] += 1
            return i

        def bankB():
            i = 4 + rr["B"] % 4
            rr["B"] += 1
            return i

        def nxt(key, n):
            i = rr[key] % n
            rr[key] += 1
            return i

        arena = {"off": 0, "live": []}
        arena_done = set()

        def arena_alloc(n, name, align=1):
            off = ((arena["off"] + align - 1) // align) * align
            if off + n > ARENA:
                off = 0
            end = off + n
            over = [a for a in arena["live"] if not (a[1] <= off or a[0] >= end)]
            for a_ in over:
                assert a_[2].name in arena_done, "arena overlap with pending allocation %s" % a_[2].name
            arena["live"] = [a for a in arena["live"] if (a[1] <= off or a[0] >= end)]
            b = Buf(name)
            arena["live"].append((off, end, b))
            arena["off"] = end
            return off, b, [a[2] for a in over]

        wcount = [0]

        def wload(dst_ap, src_ap, b, over, first):
            wcount[0] += 1
            key = "w_" + b.name
            if first:
                S.dma("gpsimd", lambda e: e.dma_start(out=dst_ap, in_=src_ap), key, writes=[b] + over)
            else:
                S.dma("gpsimd", lambda e: e.dma_start(out=dst_ap, in_=src_ap), key, writes=[b], join=True)

        S.dma("sync", lambda e: e.dma_start(out=par[:], in_=par_d), "par", writes=[b_par])
        xv = x_d.rearrange("(t p) d -> p t d", p=128)
        for t4 in range(4):
            S.dma("sync", lambda e, t4=t4: e.dma_start(out=X[:, 4 * t4:4 * t4 + 4, :], in_=xv[:, 4 * t4:4 * t4 + 4, :]),
                  "x%d" % t4, writes=bX[4 * t4:4 * t4 + 4])
        S.op("gpsimd", lambda e: e.memset(identf[:], 1.0), writes=[b_identf])
        S.op("gpsimd", lambda e: e.affine_select(out=identf[:], in_=identf[:], pattern=[[1, 128]],
                                                 compare_op=ALU.is_equal, fill=0.0, base=0, channel_multiplier=-1),
             reads=[b_identf], writes=[b_identf])
        S.op("vector", lambda e: e.tensor_copy(out=ident[:], in_=identf[:]), reads=[b_identf], writes=[b_ident])
        S.op("gpsimd", lambda e: e.memset(identf[:], 1.0), reads=[b_identf], writes=[b_identf])
        S.op("gpsimd", lambda e: e.affine_select(out=identf[:], in_=identf[:], pattern=[[1, 128]],
                                                 compare_op=ALU.is_ge, fill=0.0, base=0, channel_multiplier=-1),
             reads=[b_identf], writes=[b_identf])
        S.op("vector", lambda e: e.tensor_copy(out=tri[:], in_=identf[:]), reads=[b_identf], writes=[b_tri])
        S.op("vector", lambda e: e.tensor_scalar(out=nbf[:], in0=par[0:16, PC_BF:PC_BF + 2], scalar1=-1.0, scalar2=None,
                                                 op0=ALU.mult), reads=[b_par], writes=[b_nbf])

        def rms_stats():
            for t in range(NT):
                S.op("scalar", lambda e, t=t: e.activation(out=junk[:], in_=X[:, t, :], func=AF.Square,
                                                           accum_out=ss[:, t:t + 1]),
                     reads=[bX[t]], writes=[b_junk, b_ss] + b_ssg)
            S.op("vector", lambda e: e.tensor_scalar(out=ms[:], in0=ss[:], scalar1=1.0 / D, scalar2=EPS,
                                                     op0=ALU.mult, op1=ALU.add), reads=[b_ss], writes=[b_ms] + b_msg)
            S.op("scalar", lambda e: e.activation(out=ms[:], in_=ms[:], func=AF.Sqrt), reads=[b_ms], writes=[b_ms])
            S.op("vector", lambda e: e.reciprocal(out=rstd[:], in_=ms[:]), reads=[b_ms], writes=[b_rstd] + b_rstdg)

        def norm_phase(gcol):
            rms_stats()
            for t in range(NT):
                xi = nxt("xn", 2)
                ti = nxt("T", 2)
                S.op("scalar", lambda e, t=t, xi=xi: e.activation(out=XN[xi][:], in_=X[:, t, :], func=AF.Copy,
                                                                  scale=rstd[:, t:t + 1]),
                     reads=[bX[t], b_rstd], writes=[b_XN[xi]])
                for c in range(8):
                    S.op("tensor", lambda e, c=c, xi=xi, ti=ti: e.transpose(
                        out=psTv[ti][:, c * 128:(c + 1) * 128], in_=XN[xi][:, c * 128:(c + 1) * 128], identity=ident[:]),
                        reads=[b_XN[xi], b_ident], writes=[b_psT[ti]])
                S.op("vector", lambda e, t=t, ti=ti: e.tensor_tensor(
                    out=hT[:, :, t * 128:(t + 1) * 128],
                    in0=psTv[ti].rearrange("p (c t) -> p c t", c=8),
                    in1=par[:, gcol:gcol + 8].unsqueeze(2).broadcast_to([128, 8, 128]), op=ALU.mult),
                    reads=[b_psT[ti], b_par], writes=[b_hT[t // 4]])

        def norm_group_stats(g):
            for t in range(4 * g, 4 * g + 4):
                S.op("scalar", lambda e, t=t: e.activation(out=junk[:], in_=X[:, t, :], func=AF.Square,
                                                           accum_out=ss[:, t:t + 1]),
                     reads=[bX[t]], writes=[b_junk, b_ssg[g]])
            S.op("vector", lambda e: e.tensor_scalar(out=ms[:, 4 * g:4 * g + 4], in0=ss[:, 4 * g:4 * g + 4],
                                                     scalar1=1.0 / D, scalar2=EPS, op0=ALU.mult, op1=ALU.add),
                 reads=[b_ssg[g]], writes=[b_msg[g]])
            S.op("scalar", lambda e: e.activation(out=ms[:, 4 * g:4 * g + 4], in_=ms[:, 4 * g:4 * g + 4], func=AF.Sqrt),
                 reads=[b_msg[g]], writes=[b_msg[g]])
            S.op("vector", lambda e: e.reciprocal(out=rstd[:, 4 * g:4 * g + 4], in_=ms[:, 4 * g:4 * g + 4]),
                 reads=[b_msg[g]], writes=[b_rstdg[g]])

        def norm_group_tiles(g, gcol):
            for t in range(4 * g, 4 * g + 4):
                xi = nxt("xn", 2)
                ti = nxt("T", 2)
                S.op("scalar", lambda e, t=t, xi=xi: e.activation(out=XN[xi][:], in_=X[:, t, :], func=AF.Copy,
                                                                  scale=rstd[:, t:t + 1]),
                     reads=[bX[t], b_rstdg[g]], writes=[b_XN[xi]])
                for c in range(8):
                    S.op("tensor", lambda e, c=c, xi=xi, ti=ti: e.transpose(
                        out=psTv[ti][:, c * 128:(c + 1) * 128], in_=XN[xi][:, c * 128:(c + 1) * 128], identity=ident[:]),
                        reads=[b_XN[xi], b_ident], writes=[b_psT[ti]])
                S.op("vector", lambda e, t=t, ti=ti: e.tensor_tensor(
                    out=hT[:, :, t * 128:(t + 1) * 128],
                    in0=psTv[ti].rearrange("p (c t) -> p c t", c=8),
                    in1=par[:, gcol:gcol + 8].unsqueeze(2).broadcast_to([128, 8, 128]), op=ALU.mult),
                    reads=[b_psT[ti], b_par], writes=[b_hT[t // 4]])

        def mm(out_ap, lhsT, rhs, start, stop, reads, wbuf):
            S.op("tensor", lambda e: e.matmul(out_ap, lhsT=lhsT, rhs=rhs, start=start, stop=stop),
                 reads=reads, writes=[wbuf])

        def add_to_X_split(t, half, bk, k):
            if k % 2 == 0:
                add_to_X(t, half, bk)
                return
            ti = nxt("tmp", 3)
            S.op("scalar", lambda e: e.activation(out=TMP[ti][:], in_=psF[:, bk, :], func=AF.Copy),
                 reads=[b_psF[bk]], writes=[b_TMP[ti]])
            S.op("gpsimd", lambda e: e.tensor_tensor(out=X[:, t, 512 * half:512 * half + 512], in0=TMP[ti][:],
                                                     in1=X[:, t, 512 * half:512 * half + 512], op=ALU.add),
                 reads=[b_TMP[ti], bX[t]], writes=[bX[t]])

        def add_to_X(t, half, bk):
            S.op("vector", lambda e: e.tensor_tensor(out=X[:, t, 512 * half:512 * half + 512], in0=psF[:, bk, :],
                                                     in1=X[:, t, 512 * half:512 * half + 512], op=ALU.add),
                 reads=[b_psF[bk], bX[t]], writes=[bX[t]])

        def swa_loads(a, jj):
            off, b, over = arena_alloc(5120, "swa%d_%d" % (a, jj))
            wq = WA[:, off:off + 2048].rearrange("p (c n) -> p c n", c=8)
            wkv = WA[:, off + 2048:off + 3072].rearrange("p (c n) -> p c n", c=8)
            wo = WA[:, off + 3072:off + 5120].rearrange("p (c n) -> p c n", c=2)
            src = wqkv_d[a].rearrange("(c p) n -> p c n", p=128)
            wload(wq, src[:, :, 256 * jj:256 * jj + 256], b, over, True)
            wload(wkv[:, :, 0:64], src[:, :, 1024 + 64 * jj:1024 + 64 * jj + 64], b, over, False)
            wload(wkv[:, :, 64:128], src[:, :, 1280 + 64 * jj:1280 + 64 * jj + 64], b, over, False)
            wload(wo, woa_d[a][256 * jj:256 * jj + 256, :].rearrange("(c p) n -> p c n", p=128), b, over, False)
            return dict(wq=wq, wkv=wkv, wo=wo, b=b)

        def swa_batch(a, jj, W):
            wq, wkv, wo, bw = W["wq"], W["wkv"], W["wo"], W["b"]
            QT = R12[:].rearrange("p (h t) -> p h t", h=4)
            KT = R3[:, 0:2048]
            VP = R3[:, 2048:4096].rearrange("p (t n) -> p t n", t=16)
            OT = R4[:].rearrange("p (c t) -> p c t", c=2)
            bi = 0
            S.dma("sync", lambda e: e.dma_start(out=BIA[bi][:], in_=bias_d[:, jj].rearrange("v k f -> k v f")),
                  "bia%d" % bi, writes=[b_BIA[bi]])
            S.dma("sync", lambda e: e.dma_start(
                out=bvrep[:], in_=bqkv_d[a][1280 + 64 * jj:1280 + 64 * jj + 64].partition_broadcast(128)),
                "bv", writes=[b_bv])
            S.op("gpsimd", lambda e: e.memset(VP[:, :, 64:128], 1.0), writes=[b_R3v, b_R3])
            for cc in range(2):
                col = PC_BQ + a * 8 + jj * 2 + cc
                for g in range(4):
                    bk = bankA()
                    for c in range(8):
                        mm(psF[:, bk, :], wq[:, c, 128 * cc:128 * cc + 128], hT[:, c, 512 * g:512 * g + 512],
                           c == 0, c == 7, [bw, b_hT[g]], b_psF[bk])
                    bR = b_R1 if cc == 0 else b_R2
                    S.op("scalar", lambda e, bk=bk, cc=cc, g=g, col=col: e.activation(
                        out=QT[0:64, 2 * cc, 512 * g:512 * g + 512], in_=psF[0:64, bk, :], func=AF.Identity,
                        bias=par[0:64, col:col + 1]), reads=[b_psF[bk], b_par], writes=[bR])
                    S.op("scalar", lambda e, bk=bk, cc=cc, g=g, col=col: e.activation(
                        out=QT[0:64, 2 * cc + 1, 512 * g:512 * g + 512], in_=psF[64:128, bk, :], func=AF.Identity,
                        bias=par[64:128, col:col + 1]), reads=[b_psF[bk], b_par], writes=[bR])
            colk = PC_BK + a * 4 + jj
            for g in range(4):
                bk = bankA()
                for c in range(8):
                    mm(psF[0:64, bk, :], wkv[:, c, 0:64], hT[:, c, 512 * g:512 * g + 512], c == 0, c == 7,
                       [bw, b_hT[g]], b_psF[bk])
                S.op("scalar", lambda e, bk=bk, g=g: e.activation(
                    out=KT[0:64, 512 * g:512 * g + 512], in_=psF[0:64, bk, :], func=AF.Identity,
                    bias=par[0:64, colk:colk + 1]), reads=[b_psF[bk], b_par], writes=[b_R3])
            for t8 in range(2):
                bk = bankA()
                for ti in range(8):
                    t = t8 * 8 + ti
                    for c in range(8):
                        mm(psF[:, bk, ti * 64:ti * 64 + 64], hT[:, c, 128 * t:128 * t + 128], wkv[:, c, 64:128],
                           c == 0, c == 7, [bw, b_hT[t // 4]], b_psF[bk])
                S.op("vector", lambda e, bk=bk, t8=t8: e.tensor_tensor(
                    out=VP[:, 8 * t8:8 * t8 + 8, 0:64],
                    in0=psF[:, bk, :].rearrange("p (t n) -> p t n", t=8),
                    in1=bvrep[:].unsqueeze(1).broadcast_to([128, 8, 64]), op=ALU.add),
                    reads=[b_psF[bk], b_bv], writes=[b_R3v])
            blocks = []
            for n in range(16):
                kbs = [n - 1, n] if n > 0 else [n]
                for idx, kb in enumerate(kbs):
                    blocks.append((n, idx, kb, len(kbs)))
            state = {}

            for v_ in range(2):
                S.op("scalar", lambda e, v_=v_: e.activation(out=BIA[bi][:, v_, :], in_=BIA[bi][:, v_, :], func=AF.Exp),
                     reads=[b_BIA[bi]], writes=[b_BIA[bi]])

            def stage1(bi_):
                n, idx, kb, nk = blocks[bi_]
                v = 0 if kb == n else 1
                sb = bankA()
                mm(psF[:, sb, :], KT[0:64, 128 * kb:128 * kb + 128], QT[0:64, :, 128 * n:128 * n + 128],
                   True, True, [b_R1, b_R2, b_R3], b_psF[sb])
                ti = nxt("tmp", 3)
                pi = nxt("pt", 4)
                S.op("scalar", lambda e, sb=sb, ti=ti: e.activation(out=TMP[ti][:], in_=psF[:, sb, :], func=AF.Exp,
                                                                    scale=0.125),
                     reads=[b_psF[sb]], writes=[b_TMP[ti]])
                S.op("gpsimd",
                     lambda e, ti=ti, pi=pi, v=v: e.tensor_tensor(out=PT[pi][:], in0=TMP[ti][:],
                                                                  in1=BIA[bi][:, v, :], op=ALU.mult),
                     reads=[b_TMP[ti], b_BIA[bi]], writes=[b_PT[pi]])
                state[bi_] = pi

            def stage2(bi_):
                n, idx, kb, nk = blocks[bi_]
                pi = state.pop(bi_)
                if idx == 0:
                    state["acc"] = bankB()
                acc = state["acc"]
                mm(psF[:, acc, :], VP[:, kb, :], PT[pi][:], idx == 0, idx == nk - 1,
                   [b_R3v, b_PT[pi]], b_psF[acc])
                if idx != nk - 1:
                    return
                state[("n", n)] = acc

            def stage3(n):
                acc = state.pop(("n", n))
                S.op("vector", lambda e: e.tensor_tensor(
                    out=DEN[0:64, :].rearrange("p (h q) -> p h q", h=4),
                    in0=psF[64:128, acc, :].rearrange("p (h q) -> p h q", h=4),
                    in1=esink[64:128, 4 * jj:4 * jj + 4].unsqueeze(2).broadcast_to([64, 4, 128]), op=ALU.add),
                    reads=[b_psF[acc], b_esink], writes=[b_DEN])
                S.op("scalar", lambda e: e.activation(out=DEN[0:64, :], in_=DEN[0:64, :], func=AF.Ln),
                     reads=[b_DEN], writes=[b_DEN])
                S.op("scalar", lambda e: e.activation(out=DEN[0:64, :], in_=DEN[0:64, :], func=AF.Exp, scale=-1.0),
                     reads=[b_DEN], writes=[b_DEN])
                state[("m", n)] = acc

            def stage4(n):
                acc = state.pop(("m", n))
                for two in range(2):
                    S.op("vector", lambda e, two=two: e.tensor_tensor(
                        out=OT[64 * two:64 * two + 64, :, 128 * n:128 * n + 128],
                        in0=psF[0:64, acc, :].rearrange("p (c two q) -> p c two q", c=2, two=2)[:, :, two, :],
                        in1=DEN[0:64, :].rearrange("p (c two q) -> p c two q", c=2, two=2)[:, :, two, :],
                        op=ALU.mult), reads=[b_psF[acc], b_DEN], writes=[b_R4, b_R4b])

            LOOK = 2
            pend3, pend4 = [], []
            for step in range(len(blocks) + LOOK + 2):
                if pend4:
                    stage4(pend4.pop(0))
                if pend3:
                    n_ = pend3.pop(0)
                    stage3(n_)
                    pend4.append(n_)
                if step < len(blocks):
                    stage1(step)
                b2 = step - LOOK
                if 0 <= b2 < len(blocks):
                    stage2(b2)
                    n, idx, kb, nk = blocks[b2]
                    if idx == nk - 1:
                        pend3.append(n)
            assert not pend3 and not pend4

            for t in range(NT):
                for half in range(2):
                    bk = bankA()
                    for c in range(2):
                        mm(psF[:, bk, :], OT[:, c, 128 * t:128 * t + 128], wo[:, c, 512 * half:512 * half + 512],
                           c == 0, c == 1, [b_R4, b_R4b, bw], b_psF[bk])
                    add_to_X_split(t, half, bk, 2 * t + half)

        def swa_layer(a, W_first, prefetch_next):
            QT = R12[:].rearrange("p (h t) -> p h t", h=4)
            KT = R3[:, 0:2048]
            VP = R3[:, 2048:4096].rearrange("p (t n) -> p t n", t=16)
            OT = R4[:].rearrange("p (c t) -> p c t", c=2)
            bQ = [Buf("swaQ%d" % g) for g in range(4)]
            bK = [Buf("swaK%d" % g) for g in range(4)]
            bV = [Buf("swaV%d" % g) for g in range(4)]
            coarse = [b_R1, b_R2, b_R3, b_R3v]
            S.op("gpsimd", lambda e: e.memset(VP[:, :, 64:128], 1.0), writes=coarse + bQ + bK + bV + [b_DEN] + b_PTx)
            Ws = {0: W_first}
            bi = 0

            def jobs(jj, g):
                W = Ws[jj]
                wq, wkv, bw = W["wq"], W["wkv"], W["b"]
                colk = PC_BK + a * 4 + jj
                out = []

                def q_job(cc):
                    col = PC_BQ + a * 8 + jj * 2 + cc
                    bk = bankA()
                    for c in range(8):
                        mm(psF[:, bk, :], wq[:, c, 128 * cc:128 * cc + 128], hT[:, c, 512 * g:512 * g + 512],
                           c == 0, c == 7, [bw, b_hT[g]], b_psF[bk])
                    S.op("scalar", lambda e: e.activation(
                        out=QT[0:64, 2 * cc, 512 * g:512 * g + 512], in_=psF[0:64, bk, :], func=AF.Identity,
                        bias=par[0:64, col:col + 1]), reads=[b_psF[bk], b_par], writes=[bQ[g]])
                    S.op("scalar", lambda e: e.activation(
                        out=QT[0:64, 2 * cc + 1, 512 * g:512 * g + 512], in_=psF[64:128, bk, :], func=AF.Identity,
                        bias=par[64:128, col:col + 1]), reads=[b_psF[bk], b_par], writes=[bQ[g]])

                def k_job():
                    bk = bankA()
                    for c in range(8):
                        mm(psF[0:64, bk, :], wkv[:, c, 0:64], hT[:, c, 512 * g:512 * g + 512], c == 0, c == 7,
                           [bw, b_hT[g]], b_psF[bk])
                    S.op("scalar", lambda e: e.activation(
                        out=KT[0:64, 512 * g:512 * g + 512], in_=psF[0:64, bk, :], func=AF.Identity,
                        bias=par[0:64, colk:colk + 1]), reads=[b_psF[bk], b_par], writes=[bK[g]])

                def v_job():
                    if g == 0:
                        S.dma("sync", lambda e: e.dma_start(
                            out=bvrep[:], in_=bqkv_d[a][1280 + 64 * jj:1280 + 64 * jj + 64].partition_broadcast(128)),
                            "bv", writes=[b_bv])
                    bk = bankA()
                    for ti in range(4):
                        t = 4 * g + ti
                        for c in range(8):
                            mm(psF[:, bk, ti * 64:ti * 64 + 64], hT[:, c, 128 * t:128 * t + 128], wkv[:, c, 64:128],
                               c == 0, c == 7, [bw, b_hT[g]], b_psF[bk])
                    S.op("vector", lambda e: e.tensor_tensor(
                        out=VP[:, 4 * g:4 * g + 4, 0:64],
                        in0=psF[:, bk, 0:256].rearrange("p (t n) -> p t n", t=4),
                        in1=bvrep[:].unsqueeze(1).broadcast_to([128, 4, 64]), op=ALU.add),
                        reads=[b_psF[bk], b_bv], writes=[bV[g]])

                return [lambda: q_job(0), lambda: q_job(1), k_job, v_job]

            for j0 in jobs(0, 0):
                j0()
            def batch(jj):
                W = Ws[jj]
                wo, bw = W["wo"], W["b"]
                if jj < 3:
                    Ws[jj + 1] = swa_loads(a, jj + 1)
                else:
                    prefetch_next()
                S.dma("sync", lambda e, jj=jj: e.dma_start(out=BIA[bi][:], in_=bias_d[:, jj].rearrange("v k f -> k v f")),
                      "bia%d" % bi, writes=[b_BIA[bi]])
                for v_ in range(2):
                    S.op("scalar", lambda e, v_=v_: e.activation(out=BIA[bi][:, v_, :], in_=BIA[bi][:, v_, :], func=AF.Exp),
                         reads=[b_BIA[bi]], writes=[b_BIA[bi]])
                blocks = []
                idx0 = {}
                for n in range(16):
                    kbs = [n - 1, n] if n > 0 else [n]
                    if n % 4 == 0:
                        idx0[n // 4] = len(blocks)
                    for idx, kb in enumerate(kbs):
                        blocks.append((n, idx, kb, len(kbs)))
                idx0[4] = len(blocks)
                jsched = {}
                for g in range(4):
                    if g < 3:
                        js = jobs(jj, g + 1)
                    elif jj < 3:
                        js = jobs(jj + 1, 0)
                    else:
                        js = []
                    for k, jb in enumerate(js):
                        st_ = min(idx0[g] + 1 + 2 * k, idx0[g + 1] - 1)
                        jsched.setdefault(st_, []).append(jb)
                state = {}

                def stage1(bi_):
                    n, idx, kb, nk = blocks[bi_]
                    v = 0 if kb == n else 1
                    sb = bankA()
                    mm(psF[:, sb, :], KT[0:64, 128 * kb:128 * kb + 128], QT[0:64, :, 128 * n:128 * n + 128],
                       True, True, [bQ[n // 4], bK[kb // 4]] + coarse, b_psF[sb])
                    ti = nxt("tmp", 3)
                    pi = nxt("pt", 4)
                    S.op("scalar", lambda e: e.activation(out=TMP[ti][:], in_=psF[:, sb, :], func=AF.Exp, scale=0.125),
                         reads=[b_psF[sb]], writes=[b_TMP[ti]])
                    S.op("gpsimd", lambda e: e.tensor_tensor(out=PT[pi][:], in0=TMP[ti][:], in1=BIA[bi][:, v, :],
                                                             op=ALU.mult),
                         reads=[b_TMP[ti], b_BIA[bi]], writes=[b_PT[pi]])
                    state[bi_] = pi

                def stage2(bi_):
                    n, idx, kb, nk = blocks[bi_]
                    pi = state.pop(bi_)
                    if idx == 0:
                        state["acc"] = bankB()
                    acc = state["acc"]
                    mm(psF[:, acc, :], VP[:, kb, :], PT[pi][:], idx == 0, idx == nk - 1,
                       [bV[kb // 4], b_R3v, b_PT[pi]], b_psF[acc])
                    if idx == nk - 1:
                        state[("n", n)] = acc

                def stage3(n):
                    acc = state.pop(("n", n))
                    S.op("vector", lambda e: e.tensor_tensor(
                        out=DEN[0:64, :].rearrange("p (h q) -> p h q", h=4),
                        in0=psF[64:128, acc, :].rearrange("p (h q) -> p h q", h=4),
                        in1=esink[64:128, 4 * jj:4 * jj + 4].unsqueeze(2).broadcast_to([64, 4, 128]), op=ALU.add),
                        reads=[b_psF[acc], b_esink], writes=[b_DEN])
                    S.op("scalar", lambda e: e.activation(out=DEN[0:64, :], in_=DEN[0:64, :], func=AF.Ln),
                         reads=[b_DEN], writes=[b_DEN])
                    S.op("scalar", lambda e: e.activation(out=DEN[0:64, :], in_=DEN[0:64, :], func=AF.Exp, scale=-1.0),
                         reads=[b_DEN], writes=[b_DEN])
                    state[("m", n)] = acc

                def stage4(n):
                    acc = state.pop(("m", n))
                    for two in range(2):
                        S.op("vector", lambda e, two=two: e.tensor_tensor(
                            out=OT[64 * two:64 * two + 64, :, 128 * n:128 * n + 128],
                            in0=psF[0:64, acc, :].rearrange("p (c two q) -> p c two q", c=2, two=2)[:, :, two, :],
                            in1=DEN[0:64, :].rearrange("p (c two q) -> p c two q", c=2, two=2)[:, :, two, :],
                            op=ALU.mult), reads=[b_psF[acc], b_DEN], writes=[b_R4, b_R4b])

                LOOK = 3
                pend3, pend4 = [], []
                for step in range(len(blocks) + LOOK + 2):
                    if pend4:
                        stage4(pend4.pop(0))
                    if pend3:
                        n_ = pend3.pop(0)
                        stage3(n_)
                        pend4.append(n_)
                    if step < len(blocks):
                        stage1(step)
                    b2 = step - LOOK
                    if 0 <= b2 < len(blocks):
                        stage2(b2)
                        n, idx, kb, nk = blocks[b2]
                        if idx == nk - 1:
                            pend3.append(n)
                    for jb in jsched.pop(step, []):
                        jb()
                assert not pend3 and not pend4 and not jsched
                for t in range(NT):
                    for half in range(2):
                        bk = bankA()
                        for c in range(2):
                            mm(psF[:, bk, :], OT[:, c, 128 * t:128 * t + 128], wo[:, c, 512 * half:512 * half + 512],
                               c == 0, c == 1, [b_R4, b_R4b, bw], b_psF[bk])
                        add_to_X_split(t, half, bk, 2 * t + half)
                arena_done.add(bw.name)

            for jj_ in range(4):
                batch(jj_)

        def swa_layer_pre(a):
            S.dma("sync", lambda e: e.dma_start(out=sinkraw[:], in_=sinks_d[a].partition_broadcast(128)),
                  "sink", writes=[b_sinkraw])
            S.op("scalar", lambda e: e.activation(out=esink[:], in_=sinkraw[:], func=AF.Exp),
                 reads=[b_sinkraw], writes=[b_esink])
            S.dma("sync", lambda e: e.dma_start(out=borep[:], in_=boa_d[a].partition_broadcast(128)),
                  "borep", writes=[b_borep])
            for t in range(NT):
                S.op("gpsimd", lambda e, t=t: e.tensor_tensor(out=X[:, t, :], in0=X[:, t, :], in1=borep[:], op=ALU.add),
                     reads=[bX[t], b_borep], writes=[bX[t]])

        def fox_pre_loads(a):
            off, b, over = arena_alloc(128, "wf%d" % a)
            wf = WA[:, off:off + 128].rearrange("p (c n) -> p c n", c=8)
            wload(wf, wqkvf_d[a].rearrange("(c p) n -> p c n", p=128)[:, :, 3072:3088], b, over, True)
            return dict(wf=wf, b=b)

        def fox_pre(a, W):
            wf, bw = W["wf"], W["b"]
            VP = R3[:].rearrange("p (t s n) -> p t s n", t=16, s=2)
            S.op("gpsimd", lambda e: e.memset(ones16, 1.0), writes=[b_borep, b_DEN] + b_PTx)
            S.op("gpsimd", lambda e: e.memset(VP[:, :, 0, 64:128], 1.0), writes=[b_R3v, b_R3])
            S.op("gpsimd", lambda e: e.memset(VP[:, :, 1, 0:64], 1.0), reads=[b_R3v], writes=[b_R3v])
            for g in range(4):
                bk = bankA()
                for c in range(8):
                    mm(psF[0:16, bk, :], wf[:, c, :], hT[:, c, 512 * g:512 * g + 512], c == 0, c == 7,
                       [bw, b_hT[g]], b_psF[bk])
                t1 = nxt("tmp", 3)
                S.op("scalar", lambda e, bk=bk, t1=t1: e.activation(out=TMP[t1][0:16, :], in_=psF[0:16, bk, :],
                                                                    func=AF.Exp, scale=-1.0, bias=nbf[:, a:a + 1]),
                     reads=[b_psF[bk], b_nbf], writes=[b_TMP[t1]])
                S.op("scalar", lambda e, t1=t1: e.activation(out=TMP[t1][0:16, :], in_=TMP[t1][0:16, :], func=AF.Ln,
                                                             bias=1.0), reads=[b_TMP[t1]], writes=[b_TMP[t1]])
                t2 = nxt("tmp", 3)
                if g == 0:
                    S.op("vector", lambda e, t1=t1, t2=t2: e.tensor_tensor_scan(
                        out=TMP[t2][0:16, :], data0=ones16, data1=TMP[t1][0:16, :], initial=0.0,
                        op0=ALU.mult, op1=ALU.subtract), reads=[b_borep, b_TMP[t1]], writes=[b_TMP[t2]])
                else:
                    S.op("vector", lambda e, t1=t1, t2=t2, g=g: e.tensor_tensor_scan(
                        out=TMP[t2][0:16, :], data0=ones16, data1=TMP[t1][0:16, :], initial=carry[:, g - 1:g],
                        op0=ALU.mult, op1=ALU.subtract), reads=[b_borep, b_TMP[t1], b_carry], writes=[b_TMP[t2]])
                S.op("vector", lambda e, t2=t2, g=g: e.tensor_copy(out=carry[:, g:g + 1], in_=TMP[t2][0:16, 511:512]),
                     reads=[b_TMP[t2]], writes=[b_carry])
                cs = slice(512 * g, 512 * g + 512)
                S.op("vector", lambda e, t2=t2, cs=cs: e.tensor_scalar(out=AUG[0:16, cs], in0=TMP[t2][0:16, :], scalar1=8.0,
                                                                       scalar2=None, op0=ALU.mult),
                     reads=[b_TMP[t2]], writes=[b_AUG])
                S.op("vector", lambda e, t1=t1, t2=t2, cs=cs: e.scalar_tensor_tensor(
                    out=TMP[t1][0:16, :], in0=TMP[t2][0:16, :], scalar=8.0, in1=AUG[0:16, cs], op0=ALU.mult,
                    op1=ALU.subtract), reads=[b_TMP[t2], b_AUG], writes=[b_TMP[t1]])
                S.op("vector", lambda e, t1=t1: e.tensor_copy(out=midS[:], in_=TMP[t1][0:16, :]),
                     reads=[b_TMP[t1]], writes=[b_midS])
                S.op("vector", lambda e, cs=cs: e.tensor_copy(out=AUG[32:48, cs], in_=midS[:]),
                     reads=[b_midS], writes=[b_AUG])
                S.op("vector", lambda e, t1=t1, t2=t2: e.tensor_tensor(out=TMP[t2][0:16, :], in0=TMP[t1][0:16, :],
                                                                       in1=midS[:], op=ALU.subtract),
                     reads=[b_TMP[t1], b_midS], writes=[b_TMP[t2]])
                S.op("vector", lambda e, t2=t2, cs=cs: e.tensor_copy(out=AUG[64:80, cs], in_=TMP[t2][0:16, :]),
                     reads=[b_TMP[t2]], writes=[b_AUG])

        def fox_loads(a, b_):
            off, b, over = arena_alloc(4096, "fox%d_%d" % (a, b_))
            wq = WA[:, off:off + 1024].rearrange("p (c n) -> p c n", c=8)
            wk = WA[:, off + 1024:off + 2048].rearrange("p (c n) -> p c n", c=8)
            wv = WA[:, off + 2048:off + 3072].rearrange("p (c n) -> p c n", c=8)
            wo = WA[:, off + 3072:off + 4096]
            src = wqkvf_d[a].rearrange("(c p) n -> p c n", p=128)
            wload(wq, src[:, :, 128 * b_:128 * b_ + 128], b, over, True)
            wload(wk, src[:, :, 1024 + 128 * b_:1024 + 128 * b_ + 128], b, over, False)
            wload(wv, src[:, :, 2048 + 128 * b_:2048 + 128 * b_ + 128], b, over, False)
            wload(wo, wob_d[a][128 * b_:128 * b_ + 128, :], b, over, False)
            return dict(wq=wq, wk=wk, wv=wv, wo=wo, b=b)

        def fox_batch(a, b_, W):
            wq, wk, wv, wo, bw = W["wq"], W["wk"], W["wv"], W["wo"], W["b"]
            QA = R12[:, 0:4096].rearrange("p (s t) -> p s t", s=2)
            KA = R12[:, 4096:8192].rearrange("p (s t) -> p s t", s=2)
            VP = R3[:].rearrange("p (t s n) -> p t s n", t=16, s=2)
            OT = R4[:, 2048 * (b_ % 2):2048 * (b_ % 2) + 2048]
            bOT = b_R4 if b_ % 2 == 0 else b_R4b
            extra = [b_R1, b_R2] if b_ == 0 else []
            for s in range(2):
                S.op("gpsimd", lambda e, s=s: e.memset(QA[64:70, s, :], -1.0), writes=[b_QAaug[s]] + extra)
                S.op("gpsimd", lambda e, s=s: e.memset(KA[64:70, s, :], 1.0), writes=[b_KAaug[s]] + extra)
            for s in range(2):
                h = 2 * b_ + s
                src = bass.AP(AUG, h * S_LEN, [[32 * S_LEN, 3], [1, S_LEN]])
                S.dma("sync", lambda e, s=s, src=src: e.dma_start(out=QA[64:67, s, :], in_=src), "augq%d" % s,
                      reads=[b_AUG], writes=[b_QAaug[s]])
                S.dma("sync", lambda e, s=s, src=src: e.dma_start(out=KA[67:70, s, :], in_=src), "augk%d" % s,
                      reads=[b_AUG], writes=[b_KAaug[s]])
            for (w_, dst, bR) in ((wq, QA, b_R1), (wk, KA, b_R2)):
                for g in range(4):
                    bk = bankA()
                    for c in range(8):
                        mm(psF[:, bk, :], w_[:, c, :], hT[:, c, 512 * g:512 * g + 512], c == 0, c == 7,
                           [bw, b_hT[g]], b_psF[bk])
                    S.op("scalar", lambda e, bk=bk, g=g, dst=dst: e.activation(
                        out=dst[0:64, 0, 512 * g:512 * g + 512], in_=psF[0:64, bk, :], func=AF.Copy),
                        reads=[b_psF[bk]], writes=[bR])
                    S.op("vector", lambda e, bk=bk, g=g, dst=dst: e.tensor_copy(
                        out=dst[0:64, 1, 512 * g:512 * g + 512], in_=psF[64:128, bk, :]),
                        reads=[b_psF[bk]], writes=[bR])
            for g in range(4):
                bk = bankA()
                for c in range(8):
                    mm(psF[:, bk, :], wv[:, c, :], hT[:, c, 512 * g:512 * g + 512], c == 0, c == 7,
                       [bw, b_hT[g]], b_psF[bk])
                pi = nxt("pt", 4)
                S.op("scalar", lambda e, bk=bk, pi=pi: e.activation(out=PT[pi][:], in_=psF[:, bk, :], func=AF.Copy),
                     reads=[b_psF[bk]], writes=[b_PT[pi]])
                ti = nxt("T", 2)
                for i4 in range(4):
                    S.op("tensor", lambda e, pi=pi, ti=ti, i4=i4: e.transpose(
                        out=psTv[ti][:, 128 * i4:128 * i4 + 128], in_=PT[pi][:, 128 * i4:128 * i4 + 128],
                        identity=ident[:]), reads=[b_PT[pi], b_ident], writes=[b_psT[ti]])
                S.op("vector", lambda e, ti=ti, g=g: e.tensor_copy(
                    out=VP[:, 4 * g:4 * g + 4, 0, 0:64],
                    in_=psTv[ti][:, 0:512].rearrange("p (t n) -> p t n", t=4)[:, :, 0:64]),
                    reads=[b_psT[ti]], writes=[b_R3v])
                S.op("vector", lambda e, ti=ti, g=g: e.tensor_copy(
                    out=VP[:, 4 * g:4 * g + 4, 1, 64:128],
                    in_=psTv[ti][:, 0:512].rearrange("p (t n) -> p t n", t=4)[:, :, 64:128]),
                    reads=[b_psT[ti]], writes=[b_R3v])
            blocks = []
            for s_ in range(2):
                for G in range(4):
                    nkb = 4 * G + 4
                    for j in range(nkb):
                        blocks.append((s_, G, j, nkb))
            state = {}

            def stage1(bi_):
                s_, G, j, nkb = blocks[bi_]
                i = j - 4 * G
                qoff = 128 * i if i > 0 else 0
                ncols = 512 - qoff
                q0 = 512 * G + qoff
                sb = bankA()
                mm(psF[:, sb, 0:ncols], KA[0:70, s_, 128 * j:128 * j + 128], QA[0:70, s_, q0:q0 + ncols],
                   True, True, [b_R1, b_R2, b_QAaug[s_], b_KAaug[s_]], b_psF[sb])
                pi = nxt("ptf", 6)
                S.op("scalar", lambda e, sb=sb, pi=pi, ncols=ncols: e.activation(
                    out=PTF[pi][:, 0:ncols], in_=psF[:, sb, 0:ncols], func=AF.Exp, scale=0.125),
                    reads=[b_psF[sb]], writes=[b_PTF[pi]])
                if i >= 0:
                    S.op("gpsimd", lambda e, pi=pi: e.tensor_tensor(out=PTF[pi][:, 0:128], in0=PTF[pi][:, 0:128],
                                                                    in1=tri[:], op=ALU.mult),
                         reads=[b_PTF[pi], b_tri], writes=[b_PTF[pi]])
                state[bi_] = (pi, qoff, ncols)

            def stage2(bi_):
                s_, G, j, nkb = blocks[bi_]
                pi, qoff, ncols = state.pop(bi_)
                if j == 0:
                    state["acc"] = bankB()
                acc = state["acc"]
                mm(psF[:, acc, qoff:512], VP[:, j, s_, :], PTF[pi][:, 0:ncols], j == 0, j == nkb - 1,
                   [b_R3v, b_PTF[pi]], b_psF[acc])
                if j != nkb - 1:
                    return
                ti = nxt("tmp", 3)
                lo, hi = (0, 64) if s_ == 0 else (64, 128)
                slo, shi = (64, 128) if s_ == 0 else (0, 64)
                S.op("vector", lambda e, acc=acc, ti=ti, lo=lo, hi=hi, slo=slo, shi=shi: e.reciprocal(
                    out=TMP[ti][lo:hi, :], in_=psF[slo:shi, acc, :]), reads=[b_psF[acc]], writes=[b_TMP[ti]])
                S.op("vector", lambda e, acc=acc, ti=ti, lo=lo, hi=hi, G=G: e.tensor_tensor(
                    out=OT[lo:hi, 512 * G:512 * G + 512], in0=psF[lo:hi, acc, :], in1=TMP[ti][lo:hi, :],
                    op=ALU.mult), reads=[b_psF[acc], b_TMP[ti]], writes=[bOT])

            LOOK = 5
            for step in range(len(blocks) + LOOK):
                if step < len(blocks):
                    stage1(step)
                if step - LOOK >= 0:
                    stage2(step - LOOK)
            fox_state["wo%d" % (b_ % 2)] = (wo, bw)
            if b_ % 2 == 1:
                OT2 = R4[:].rearrange("p (c t) -> p c t", c=2)
                for t in range(NT):
                    for half in range(2):
                        bk = bankA()
                        for c in range(2):
                            wo_c, bw_c = fox_state["wo%d" % c]
                            mm(psF[:, bk, :], OT2[:, c, 128 * t:128 * t + 128], wo_c[:, 512 * half:512 * half + 512],
                               c == 0, c == 1, [b_R4, b_R4b, bw_c], b_psF[bk])
                        add_to_X_split(t, half, bk, 2 * t + half)

        def mlp_load_up(l, q4):
            off, b, over = arena_alloc(8192, "mup%d_%d" % (l, q4), align=8192)
            wu = WA[:, off:off + 8192].rearrange("p (c n) -> p c n", c=8)
            src = wup_d[l].rearrange("(c p) n -> p c n", p=128)
            wload(wu[:, :, 0:512], src[:, :, 1024 * q4:1024 * q4 + 512], b, over, True)
            wload(wu[:, :, 512:1024], src[:, :, 1024 * q4 + 512:1024 * q4 + 1024], b, over, False)
            return dict(wu=wu, b=b)

        def mlp_load_dn(l, q4):
            off, b, over = arena_alloc(8192, "mdn%d_%d" % (l, q4), align=8192)
            wd = WA[:, off:off + 8192].rearrange("p (f n) -> p f n", f=8)
            src = wdn_d[l][1024 * q4:1024 * q4 + 1024, :].rearrange("(f p) n -> p f n", p=128)
            wload(wd[:, 0:4, :], src[:, 0:4, :], b, over, True)
            wload(wd[:, 4:8, :], src[:, 4:8, :], b, over, False)
            return dict(wd=wd, b=b)

        def mlp_pass(l, q4, W0, W1, hook, next_gcol=None):
            pending_tiles = []
            wu, bwu = W0["wu"], W0["b"]
            wd, bwd = W1["wd"], W1["b"]
            for g in range(4):
                ai = g % 2
                aT = (R12[:, 0:4096] if ai == 0 else R12[:, 4096:8192]).rearrange("p (f t) -> p f t", f=8)
                bA = b_R1 if ai == 0 else b_R2
                for f in range(8):
                    bk = bankA()
                    for c in range(8):
                        mm(psF[:, bk, :], wu[:, c, 128 * f:128 * f + 128], hT[:, c, 512 * g:512 * g + 512],
                           c == 0, c == 7, [bwu, b_hT[g]], b_psF[bk])
                    ti = nxt("tmp", 3)
                    S.op("scalar", lambda e, bk=bk, ti=ti: e.activation(out=TMP[ti][:], in_=psF[:, bk, :], func=AF.Relu),
                         reads=[b_psF[bk]], writes=[b_TMP[ti]])
                    S.op("gpsimd", lambda e, ti=ti, f=f, aT=aT: e.tensor_tensor(out=aT[:, f, :], in0=TMP[ti][:],
                                                                                in1=TMP[ti][:], op=ALU.mult),
                         reads=[b_TMP[ti]], writes=[bA])
                if g == 3:
                    arena_done.add(bwu.name)
                    hook()
                if next_gcol is not None and pending_tiles:
                    norm_group_tiles(pending_tiles.pop(0), next_gcol)
                for tt in range(4):
                    t = 4 * g + tt
                    for half in range(2):
                        bk = bankB()
                        for f in range(8):
                            mm(psF[:, bk, :], aT[:, f, 128 * tt:128 * tt + 128], wd[:, f, 512 * half:512 * half + 512],
                               f == 0, f == 7, [bA, bwd], b_psF[bk])
                        add_to_X(t, half, bk)
                if next_gcol is not None:
                    norm_group_stats(g)
                    pending_tiles.append(g)
            if next_gcol is not None:
                while pending_tiles:
                    norm_group_tiles(pending_tiles.pop(0), next_gcol)
                norm_done.add(next_gcol)

        units = []

        def U(l0, l1, comp, self_pf=False):
            units.append(dict(l0=l0, l1=l1, comp=comp, self_pf=self_pf))

        for l in layers:
            a = l // 2
            U(None, None, lambda W0, W1, hook, l=l: (None if (PC_GMIX + l * 8) in norm_done else norm_phase(PC_GMIX + l * 8)))
            if l % 2 == 0:
                U(None, None, lambda W0, W1, hook, a=a: swa_layer_pre(a))
                U(lambda a=a: swa_loads(a, 0), None, lambda W0, W1, hook, pf, a=a: swa_layer(a, W0, pf), self_pf=True)
            else:
                U(lambda a=a: fox_pre_loads(a), None, lambda W0, W1, hook, a=a: fox_pre(a, W0))
                for b_ in range(8):
                    U(lambda a=a, b_=b_: fox_loads(a, b_), None, lambda W0, W1, hook, a=a, b_=b_: fox_batch(a, b_, W0))
            U(None, None, lambda W0, W1, hook, l=l: norm_phase(PC_GMLP + l * 8))
            for q4 in range(4):
                U(lambda l=l, q4=q4: mlp_load_up(l, q4), lambda l=l, q4=q4: mlp_load_dn(l, q4),
                  lambda W0, W1, hook, l=l, q4=q4: mlp_pass(
                      l, q4, W0, W1, hook,
                      next_gcol=(PC_GMIX + (l + 1) * 8) if (q4 == 3 and (l + 1) in layers) else None))

        L0, L1 = {}, {}

        def load0(i):
            if i < len(units) and units[i]["l0"] is not None and i not in L0:
                L0[i] = units[i]["l0"]()

        def load1(i):
            if i < len(units) and units[i]["l1"] is not None and i not in L1:
                L1[i] = units[i]["l1"]()

        def next_loader(i):
            j = i + 1
            while j < len(units) and units[j]["l0"] is None:
                j += 1
            return j

        first = 0 if units[0]["l0"] is not None else next_loader(0)
        load0(first)
        load1(first)
        for i, u in enumerate(units):
            if u["l0"] is None:
                u["comp"](None, None, None)
                continue
            load0(i)
            load1(i)
            nj = next_loader(i)
            if not u["self_pf"]:
                load0(nj)
            called = [False]

            def hook(nj=nj, called=called):
                if not called[0]:
                    called[0] = True
                    load1(nj)

            if u["self_pf"]:
                u["comp"](L0[i], L1.get(i), hook, lambda nj=nj: load0(nj))
                load0(nj)
            else:
                u["comp"](L0[i], L1.get(i), hook)
            arena_done.add(L0[i]["b"].name)
            if i in L1:
                arena_done.add(L1[i]["b"].name)
            hook()

        if final:
            S.dma("sync", lambda e: e.dma_start(out=borep[:], in_=gfin_d.partition_broadcast(128)),
                  "borep", writes=[b_borep])
            rms_stats()
            for t in range(NT):
                S.op("vector", lambda e, t=t: e.scalar_tensor_tensor(
                    out=X[:, t, :], in0=X[:, t, :], scalar=rstd[:, t:t + 1], in1=borep[:], op0=ALU.mult, op1=ALU.mult),
                    reads=[bX[t], b_rstd, b_borep], writes=[bX[t]])
        ov = out_d.rearrange("(t p) d -> p t d", p=128)
        outs = []
        for t4 in range(4):
            outs.append(S.dma("sync", lambda e, t4=t4: e.dma_start(out=ov[:, 4 * t4:4 * t4 + 4, :],
                                                                    in_=X[:, 4 * t4:4 * t4 + 4, :]),
                              "out%d" % t4, reads=bX[4 * t4:4 * t4 + 4]))
        with nc.Block() as block:
            S.emit(block, final_wait_ops=outs)
    return nc


_PROGRAMS = {}


def _get_program(key):
    if key not in _PROGRAMS:
        _PROGRAMS[key] = build(*key)
    return _PROGRAMS[key]


LAUNCH_PLAN = [((0, 1, 2, 3), True)]


def kernel(x, rel_bias, norm_mix, norm_mlp, w_qkv_a, b_qkv_a, sinks_a, w_o_a, b_o_a,
           w_qkvf_b, b_f_b, w_o_b, w_up, w_down, norm_final):
    f = lambda a: np.ascontiguousarray(np.asarray(a, dtype=np.float32))
    x = f(x)
    params = _layout_params(f(norm_mix), f(norm_mlp), f(b_qkv_a), f(b_f_b))
    biasT = _layout_bias(f(rel_bias)).reshape(2, 4, 128, 512)
    shared = {"params": params, "biasT": biasT, "w_qkv_a": f(w_qkv_a), "b_qkv_a": f(b_qkv_a),
              "sinks_a": f(sinks_a), "w_o_a": f(w_o_a), "b_o_a": f(b_o_a), "w_qkvf_b": f(w_qkvf_b),
              "w_o_b": f(w_o_b), "w_up": f(w_up), "w_down": f(w_down), "norm_final": f(norm_final)}
    cur = [x[b] for b in range(8)]
    for key in LAUNCH_PLAN:
        nc = _get_program(key)
        in_maps = [dict(shared, x=cur[b]) for b in range(8)]
        res = run_bass_kernel_spmd(nc, in_maps, core_ids=list(range(8)))
        cur = [np.asarray(res.results[b]["out"], dtype=np.float32) for b in range(8)]
    return np.stack(cur, axis=0).astype(np.float32)
```

```python
from contextlib import ExitStack
import math
import numpy as np
import concourse.bass as bass
import concourse.mybir as mybir
from concourse.bass_utils import run_bass_kernel_spmd

F32 = mybir.dt.float32
BF16 = mybir.dt.bfloat16
AF = mybir.ActivationFunctionType
ALU = mybir.AluOpType

S_LEN = 2048
D = 1024
DFF = 4096
NT = 16
DEPTH = 4
NEG = -1e30
EPS = 1e-6

ENGS = ("tensor", "scalar", "vector", "gpsimd", "sync")
STRICT = True


class Buf:
    __slots__ = ("name", "writers", "readers")

    def __init__(self, name):
        self.name = name
        self.writers = []
        self.readers = []


class Op:
    __slots__ = ("idx", "eng", "fn", "deps", "wdeps", "is_dma", "sem", "semval", "milestone", "needed", "eff")

    def __init__(self, idx, eng, fn, is_dma=False):
        self.idx = idx
        self.eng = eng
        self.fn = fn
        self.deps = set()
        self.wdeps = set()
        self.eff = ()
        self.is_dma = is_dma
        self.sem = None
        self.semval = 0
        self.milestone = 0
        self.needed = False


class Sched:
    def __init__(self, nc, stack):
        self.nc = nc
        self.stack = stack
        self.ops = []
        self.dma_sems = {}
        self.eng_sem = {}
        for e in ENGS:
            self.eng_sem[e] = stack.enter_context(nc.semaphore("es_" + e))

    def _dma_sem(self, key):
        if key not in self.dma_sems:
            h = self.stack.enter_context(self.nc.semaphore("ds_%s" % key))
            self.dma_sems[key] = [h, 0]
        return self.dma_sems[key]

    def _add(self, op, reads, writes):
        for b in reads:
            for w in b.writers:
                op.deps.add(w)
        for b in writes:
            for w in b.writers:
                op.wdeps.add(w)
            for r in b.readers:
                op.wdeps.add(r)
        op.deps.discard(op.idx)
        op.wdeps.discard(op.idx)
        for b in reads:
            b.readers.append(op.idx)
        for b in writes:
            b.writers = [op.idx]
            b.readers = []
        self.ops.append(op)
        return op

    def op(self, eng, fn, reads=(), writes=()):
        return self._add(Op(len(self.ops), eng, fn), reads, writes)

    def dma(self, eng, fn, semkey, reads=(), writes=(), join=False):
        o = Op(len(self.ops), eng, fn, is_dma=True)
        s = self._dma_sem(semkey)
        s[1] += 16
        o.sem = s[0]
        o.semval = s[1]
        if join:
            for b in reads:
                for w in b.writers:
                    o.deps.add(w)
                b.readers.append(o.idx)
            for b in writes:
                for w in b.writers:
                    o.deps |= self.ops[w].deps
                    o.wdeps |= self.ops[w].wdeps
                b.writers = b.writers + [o.idx]
            self.ops.append(o)
            return o
        return self._add(o, reads, writes)

    def emit(self, block, final_wait_ops=()):
        ops = self.ops
        for o in ops:
            eff = []
            for d in o.deps:
                p = ops[d]
                if (not p.is_dma) and p.eng == o.eng and p.eng == "tensor":
                    continue
                eff.append(d)
            for d in o.wdeps:
                if d in o.deps:
                    continue
                p = ops[d]
                if (not p.is_dma) and p.eng == o.eng and not o.is_dma and (p.eng == "tensor" or not STRICT):
                    continue
                eff.append(d)
            o.eff = eff
            for d in eff:
                if not ops[d].is_dma:
                    ops[d].needed = True
        cnt = {e: 0 for e in ENGS}
        for o in ops:
            if not o.is_dma and o.needed:
                cnt[o.eng] += 1
                o.milestone = cnt[o.eng]
        per_eng = {e: [] for e in ENGS}
        for o in ops:
            per_eng[o.eng].append(o)
        sched = self

        def run(engname, eng):
            waited = {}
            for o in per_eng[engname]:
                need = {}
                for d in o.eff:
                    p = ops[d]
                    if p.is_dma:
                        key = ("d", id(p.sem))
                        v = p.semval
                        h = p.sem
                    else:
                        key = ("e", p.eng)
                        v = p.milestone
                        h = sched.eng_sem[p.eng]
                    if need.get(key, (None, 0))[1] < v:
                        need[key] = (h, v)
                for key, (h, v) in need.items():
                    if waited.get(key, 0) >= v:
                        continue
                    eng.wait_ge(h, v)
                    waited[key] = v
                ins = o.fn(eng)
                if o.is_dma:
                    ins.then_inc(o.sem, 16)
                elif o.needed:
                    ins.then_inc(sched.eng_sem[engname], 1)
            if engname == "sync":
                for o in final_wait_ops:
                    eng.wait_ge(o.sem, o.semval)

        @block.tensor
        def _(e):
            run("tensor", e)

        @block.scalar
        def _(e):
            run("scalar", e)

        @block.vector
        def _(e):
            run("vector", e)

        @block.gpsimd
        def _(e):
            run("gpsimd", e)

        @block.sync
        def _(e):
            run("sync", e)


NPAR = 96
PC_GMIX, PC_GMLP, PC_BQ, PC_BK, PC_BF = 0, 32, 64, 80, 88


def _t5_bucket_np(dist):
    n_buckets, max_distance = 32, 128
    max_exact = n_buckets // 2
    d = np.maximum(dist, 0)
    dl = np.maximum(d, 1).astype(np.float32)
    large = max_exact + (np.log(dl / np.float32(max_exact)) / np.float32(math.log(max_distance / max_exact))
                         * np.float32(n_buckets - max_exact)).astype(np.int32)
    large = np.minimum(large, n_buckets - 1)
    return np.where(d < max_exact, d, large)


def _bucket_table():
    return _t5_bucket_np(np.arange(256, dtype=np.int32))


def _layout_params(norm_mix, norm_mlp, b_qkv_a, b_f_b):
    p = np.zeros((128, NPAR), np.float32)
    for l in range(DEPTH):
        p[:, PC_GMIX + l * 8:PC_GMIX + l * 8 + 8] = norm_mix[l].reshape(8, 128).T
        p[:, PC_GMLP + l * 8:PC_GMLP + l * 8 + 8] = norm_mlp[l].reshape(8, 128).T
    for a in range(2):
        p[:, PC_BQ + a * 8:PC_BQ + a * 8 + 8] = b_qkv_a[a][0:1024].reshape(8, 128).T
        p[0:64, PC_BK + a * 4:PC_BK + a * 4 + 4] = b_qkv_a[a][1024:1280].reshape(4, 64).T
        p[0:16, PC_BF + a] = b_f_b[a]
    return p


def _layout_bias(rel_bias):
    bt = _bucket_table()
    k = np.arange(128)[:, None]
    q = np.arange(128)[None, :]
    out = np.empty((2, 4, 128, 4, 128), np.float32)
    for v in range(2):
        dist = q - k + (128 if v == 1 else 0)
        valid = (dist >= 0) & (dist < 128)
        idx = bt[np.clip(dist, 0, 255)]
        for h in range(16):
            g = rel_bias[idx, h]
            out[v, h // 4, :, h % 4, :] = np.where(valid, g, np.float32(NEG))
    return out


def build(layers=(0, 1, 2, 3), final=True):
    nc = bass.Bass("TRN2", target_bir_lowering=False)
    x_d = nc.dram_tensor("x", [S_LEN, D], F32, kind="ExternalInput").ap()
    par_d = nc.dram_tensor("params", [128, NPAR], F32, kind="ExternalInput").ap()
    bias_d = nc.dram_tensor("biasT", [2, 4, 128, 512], F32, kind="ExternalInput").ap()
    wqkv_d = nc.dram_tensor("w_qkv_a", [2, D, 1536], F32, kind="ExternalInput").ap()
    bqkv_d = nc.dram_tensor("b_qkv_a", [2, 1536], F32, kind="ExternalInput").ap()
    sinks_d = nc.dram_tensor("sinks_a", [2, 16], F32, kind="ExternalInput").ap()
    woa_d = nc.dram_tensor("w_o_a", [2, D, D], F32, kind="ExternalInput").ap()
    boa_d = nc.dram_tensor("b_o_a", [2, D], F32, kind="ExternalInput").ap()
    wqkvf_d = nc.dram_tensor("w_qkvf_b", [2, D, 3088], F32, kind="ExternalInput").ap()
    wob_d = nc.dram_tensor("w_o_b", [2, D, D], F32, kind="ExternalInput").ap()
    wup_d = nc.dram_tensor("w_up", [DEPTH, D, DFF], F32, kind="ExternalInput").ap()
    wdn_d = nc.dram_tensor("w_down", [DEPTH, DFF, D], F32, kind="ExternalInput").ap()
    gfin_d = nc.dram_tensor("norm_final", [D], F32, kind="ExternalInput").ap()
    out_d = nc.dram_tensor("out", [S_LEN, D], F32, kind="ExternalOutput").ap()

    with ExitStack() as st:
        S = Sched(nc, st)

        def T(name, shape, dt):
            return st.enter_context(nc.sbuf_tensor(name, shape, dt))

        X = T("X", [128, NT, D], F32)
        hT = T("hT", [128, 8, S_LEN], BF16)
        ARENA = 24576
        WA = T("WA", [128, ARENA], BF16)
        R12 = T("R12", [128, 8192], BF16)
        R3 = T("R3", [128, 4096], BF16)
        R4 = T("R4", [128, 4096], BF16)
        PT = [T("PT%d" % i, [128, 512], BF16) for i in range(4)]
        TMP = [T("TMP%d" % i, [128, 512], F32) for i in range(3)]
        BIA = [T("BIA0", [128, 2, 512], F32)]
        ident = T("ident", [128, 128], BF16)
        identf = T("identf", [128, 128], F32)
        tri = T("tri", [128, 128], BF16)
        par = T("par", [128, NPAR], F32)
        ss = T("ss", [128, 16], F32)
        ms = T("ms", [128, 16], F32)
        rstd = T("rstd", [128, 16], F32)
        XN = [T("XN%d" % i, [128, D], BF16) for i in range(2)]
        junk = XN[1]
        bvrep = T("bvrep", [128, 64], F32)
        DEN = T("DEN", [128, 512], F32)
        DENb = DEN[:].bitcast(BF16)
        PTX = [DENb[:, 0:512], DENb[:, 512:1024]]
        sinkraw = T("sinkraw", [128, 16], F32)
        esink = T("esink", [128, 16], F32)
        borep = T("borep", [128, D], F32)
        AUG = T("AUG", [128, S_LEN], BF16)
        ones16 = borep[0:16, 0:512]
        midS = T("midS", [16, 512], BF16)
        carry = T("carry", [16, 4], F32)
        nbf = T("nbf", [16, 2], F32)
        psF = st.enter_context(nc.psum_tensor("psF", [128, 8, 512], F32))
        psTv = [psF[:, 6, :].bitcast(BF16), psF[:, 7, :].bitcast(BF16)]

        bX = [Buf("X%d" % t) for t in range(NT)]
        b_hT = [Buf("hT%d" % g) for g in range(4)]
        b_R1, b_R2, b_R3, b_R4 = Buf("R1"), Buf("R2"), Buf("R3"), Buf("R4")
        b_R3v = Buf("R3v")
        b_R4b = Buf("R4b")
        b_QAaug = [Buf("QAaug0"), Buf("QAaug1")]
        b_KAaug = [Buf("KAaug0"), Buf("KAaug1")]
        fox_state = {}
        b_PT = [Buf("PT%d" % i) for i in range(4)]
        b_TMP = [Buf("TMP%d" % i) for i in range(3)]
        b_BIA = [Buf("BIA0")]
        b_ident, b_identf, b_tri, b_par = Buf("ident"), Buf("identf"), Buf("tri"), Buf("par")
        b_ss, b_ms, b_rstd = Buf("ss"), Buf("ms"), Buf("rstd")
        b_ssg = [Buf("ssg%d" % g) for g in range(4)]
        b_msg = [Buf("msg%d" % g) for g in range(4)]
        b_rstdg = [Buf("rstdg%d" % g) for g in range(4)]
        norm_done = set()
        b_XN = [Buf("XN0"), Buf("XN1")]
        b_junk = b_XN[1]
        b_DEN = Buf("DEN")
        b_PTx = [Buf("PTx0"), Buf("PTx1")]
        b_bv, b_sinkraw, b_esink, b_borep = Buf("bv"), Buf("sinkraw"), Buf("esink"), Buf("borep")
        b_AUG, b_midS, b_carry, b_nbf, b_ones16 = Buf("AUG"), Buf("midS"), Buf("carry"), Buf("nbf"), Buf("ones16")
        b_psF = [Buf("psF%d" % i) for i in range(8)]
        b_psT = [b_psF[6], b_psF[7]]

        rr = {"A": 0, "B": 0, "T": 0, "pt": 0, "tmp": 0, "xn": 0, "ptf": 0}
        PTF = PT + PTX
        b_PTF = b_PT + b_PTx

        def bankA():
            i = rr["A"] % 4
            rr["A"] += 1
            return i

        def bankB():
            i = 4 + rr["B"] % 4
            rr["B"] += 1
            return i

        def nxt(key, n):
            i = rr[key] % n
            rr[key] += 1
            return i

        arena = {"off": 0, "live": []}
        arena_done = set()

        def arena_alloc(n, name, align=1):
            off = ((arena["off"] + align - 1) // align) * align
            if off + n > ARENA:
                off = 0
            end = off + n
            over = [a for a in arena["live"] if not (a[1] <= off or a[0] >= end)]
            for a_ in over:
                assert a_[2].name in arena_done, "arena overlap with pending allocation %s" % a_[2].name
            arena["live"] = [a for a in arena["live"] if (a[1] <= off or a[0] >= end)]
            b = Buf(name)
            arena["live"].append((off, end, b))
            arena["off"] = end
            return off, b, [a[2] for a in over]

        wcount = [0]

        def wload(dst_ap, src_ap, b, over, first):
            wcount[0] += 1
            key = "w_" + b.name
            if first:
                S.dma("gpsimd", lambda e: e.dma_start(out=dst_ap, in_=src_ap), key, writes=[b] + over)
            else:
                S.dma("gpsimd", lambda e: e.dma_start(out=dst_ap, in_=src_ap), key, writes=[b], join=True)

        S.dma("sync", lambda e: e.dma_start(out=par[:], in_=par_d), "par", writes=[b_par])
        xv = x_d.rearrange("(t p) d -> p t d", p=128)
        for t4 in range(4):
            S.dma("sync", lambda e, t4=t4: e.dma_start(out=X[:, 4 * t4:4 * t4 + 4, :], in_=xv[:, 4 * t4:4 * t4 + 4, :]),
                  "x%d" % t4, writes=bX[4 * t4:4 * t4 + 4])
        S.op("gpsimd", lambda e: e.memset(identf[:], 1.0), writes=[b_identf])
        S.op("gpsimd", lambda e: e.affine_select(out=identf[:], in_=identf[:], pattern=[[1, 128]],
                                                 compare_op=ALU.is_equal, fill=0.0, base=0, channel_multiplier=-1),
             reads=[b_identf], writes=[b_identf])
        S.op("vector", lambda e: e.tensor_copy(out=ident[:], in_=identf[:]), reads=[b_identf], writes=[b_ident])
        S.op("gpsimd", lambda e: e.memset(identf[:], 1.0), reads=[b_identf], writes=[b_identf])
        S.op("gpsimd", lambda e: e.affine_select(out=identf[:], in_=identf[:], pattern=[[1, 128]],
                                                 compare_op=ALU.is_ge, fill=0.0, base=0, channel_multiplier=-1),
             reads=[b_identf], writes=[b_identf])
        S.op("vector", lambda e: e.tensor_copy(out=tri[:], in_=identf[:]), reads=[b_identf], writes=[b_tri])
        S.op("vector", lambda e: e.tensor_scalar(out=nbf[:], in0=par[0:16, PC_BF:PC_BF + 2], scalar1=-1.0, scalar2=None,
                                                 op0=ALU.mult), reads=[b_par], writes=[b_nbf])

        def rms_stats():
            for t in range(NT):
                S.op("scalar", lambda e, t=t: e.activation(out=junk[:], in_=X[:, t, :], func=AF.Square,
                                                           accum_out=ss[:, t:t + 1]),
                     reads=[bX[t]], writes=[b_junk, b_ss] + b_ssg)
            S.op("vector", lambda e: e.tensor_scalar(out=ms[:], in0=ss[:], scalar1=1.0 / D, scalar2=EPS,
                                                     op0=ALU.mult, op1=ALU.add), reads=[b_ss], writes=[b_ms] + b_msg)
            S.op("scalar", lambda e: e.activation(out=ms[:], in_=ms[:], func=AF.Sqrt), reads=[b_ms], writes=[b_ms])
            S.op("vector", lambda e: e.reciprocal(out=rstd[:], in_=ms[:]), reads=[b_ms], writes=[b_rstd] + b_rstdg)

        def norm_phase(gcol):
            rms_stats()
            for t in range(NT):
                xi = nxt("xn", 2)
                ti = nxt("T", 2)
                S.op("scalar", lambda e, t=t, xi=xi: e.activation(out=XN[xi][:], in_=X[:, t, :], func=AF.Copy,
                                                                  scale=rstd[:, t:t + 1]),
                     reads=[bX[t], b_rstd], writes=[b_XN[xi]])
                for c in range(8):
                    S.op("tensor", lambda e, c=c, xi=xi, ti=ti: e.transpose(
                        out=psTv[ti][:, c * 128:(c + 1) * 128], in_=XN[xi][:, c * 128:(c + 1) * 128], identity=ident[:]),
                        reads=[b_XN[xi], b_ident], writes=[b_psT[ti]])
                S.op("vector", lambda e, t=t, ti=ti: e.tensor_tensor(
                    out=hT[:, :, t * 128:(t + 1) * 128],
                    in0=psTv[ti].rearrange("p (c t) -> p c t", c=8),
                    in1=par[:, gcol:gcol + 8].unsqueeze(2).broadcast_to([128, 8, 128]), op=ALU.mult),
                    reads=[b_psT[ti], b_par], writes=[b_hT[t // 4]])

        def norm_group_stats(g):
            for t in range(4 * g, 4 * g + 4):
                S.op("scalar", lambda e, t=t: e.activation(out=junk[:], in_=X[:, t, :], func=AF.Square,
                                                           accum_out=ss[:, t:t + 1]),
                     reads=[bX[t]], writes=[b_junk, b_ssg[g]])
            S.op("vector", lambda e: e.tensor_scalar(out=ms[:, 4 * g:4 * g + 4], in0=ss[:, 4 * g:4 * g + 4],
                                                     scalar1=1.0 / D, scalar2=EPS, op0=ALU.mult, op1=ALU.add),
                 reads=[b_ssg[g]], writes=[b_msg[g]])
            S.op("scalar", lambda e: e.activation(out=ms[:, 4 * g:4 * g + 4], in_=ms[:, 4 * g:4 * g + 4], func=AF.Sqrt),
                 reads=[b_msg[g]], writes=[b_msg[g]])
            S.op("vector", lambda e: e.reciprocal(out=rstd[:, 4 * g:4 * g + 4], in_=ms[:, 4 * g:4 * g + 4]),
                 reads=[b_msg[g]], writes=[b_rstdg[g]])

        def norm_group_tiles(g, gcol):
            for t in range(4 * g, 4 * g + 4):
                xi = nxt("xn", 2)
                ti = nxt("T", 2)
                S.op("scalar", lambda e, t=t, xi=xi: e.activation(out=XN[xi][:], in_=X[:, t, :], func=AF.Copy,
                                                                  scale=rstd[:, t:t + 1]),
                     reads=[bX[t], b_rstdg[g]], writes=[b_XN[xi]])
                for c in range(8):
                    S.op("tensor", lambda e, c=c, xi=xi, ti=ti: e.transpose(
                        out=psTv[ti][:, c * 128:(c + 1) * 128], in_=XN[xi][:, c * 128:(c + 1) * 128], identity=ident[:]),
                        reads=[b_XN[xi], b_ident], writes=[b_psT[ti]])
                S.op("vector", lambda e, t=t, ti=ti: e.tensor_tensor(
                    out=hT[:, :, t * 128:(t + 1) * 128],
                    in0=psTv[ti].rearrange("p (c t) -> p c t", c=8),
                    in1=par[:, gcol:gcol + 8].unsqueeze(2).broadcast_to([128, 8, 128]), op=ALU.mult),
                    reads=[b_psT[ti], b_par], writes=[b_hT[t // 4]])

        def mm(out_ap, lhsT, rhs, start, stop, reads, wbuf):
            S.op("tensor", lambda e: e.matmul(out_ap, lhsT=lhsT, rhs=rhs, start=start, stop=stop),
                 reads=reads, writes=[wbuf])

        def add_to_X_split(t, half, bk, k):
            if k % 2 == 0:
                add_to_X(t, half, bk)
                return
            ti = nxt("tmp", 3)
            S.op("scalar", lambda e: e.activation(out=TMP[ti][:], in_=psF[:, bk, :], func=AF.Copy),
                 reads=[b_psF[bk]], writes=[b_TMP[ti]])
            S.op("gpsimd", lambda e: e.tensor_tensor(out=X[:, t, 512 * half:512 * half + 512], in0=TMP[ti][:],
                                                     in1=X[:, t, 512 * half:512 * half + 512], op=ALU.add),
                 reads=[b_TMP[ti], bX[t]], writes=[bX[t]])

        def add_to_X(t, half, bk):
            S.op("vector", lambda e: e.tensor_tensor(out=X[:, t, 512 * half:512 * half + 512], in0=psF[:, bk, :],
                                                     in1=X[:, t, 512 * half:512 * half + 512], op=ALU.add),
                 reads=[b_psF[bk], bX[t]], writes=[bX[t]])

        def swa_loads(a, jj):
            off, b, over = arena_alloc(5120, "swa%d_%d" % (a, jj))
            wq = WA[:, off:off + 2048].rearrange("p (c n) -> p c n", c=8)
            wkv = WA[:, off + 2048:off + 3072].rearrange("p (c n) -> p c n", c=8)
            wo = WA[:, off + 3072:off + 5120].rearrange("p (c n) -> p c n", c=2)
            src = wqkv_d[a].rearrange("(c p) n -> p c n", p=128)
            wload(wq, src[:, :, 256 * jj:256 * jj + 256], b, over, True)
            wload(wkv[:, :, 0:64], src[:, :, 1024 + 64 * jj:1024 + 64 * jj + 64], b, over, False)
            wload(wkv[:, :, 64:128], src[:, :, 1280 + 64 * jj:1280 + 64 * jj + 64], b, over, False)
            wload(wo, woa_d[a][256 * jj:256 * jj + 256, :].rearrange("(c p) n -> p c n", p=128), b, over, False)
            return dict(wq=wq, wkv=wkv, wo=wo, b=b)

        def swa_batch(a, jj, W):
            wq, wkv, wo, bw = W["wq"], W["wkv"], W["wo"], W["b"]
            QT = R12[:].rearrange("p (h t) -> p h t", h=4)
            KT = R3[:, 0:2048]
            VP = R3[:, 2048:4096].rearrange("p (t n) -> p t n", t=16)
            OT = R4[:].rearrange("p (c t) -> p c t", c=2)
            bi = 0
            S.dma("sync", lambda e: e.dma_start(out=BIA[bi][:], in_=bias_d[:, jj].rearrange("v k f -> k v f")),
                  "bia%d" % bi, writes=[b_BIA[bi]])
            S.dma("sync", lambda e: e.dma_start(
                out=bvrep[:], in_=bqkv_d[a][1280 + 64 * jj:1280 + 64 * jj + 64].partition_broadcast(128)),
                "bv", writes=[b_bv])
            S.op("gpsimd", lambda e: e.memset(VP[:, :, 64:128], 1.0), writes=[b_R3v, b_R3])
            for cc in range(2):
                col = PC_BQ + a * 8 + jj * 2 + cc
                for g in range(4):
                    bk = bankA()
                    for c in range(8):
                        mm(psF[:, bk, :], wq[:, c, 128 * cc:128 * cc + 128], hT[:, c, 512 * g:512 * g + 512],
                           c == 0, c == 7, [bw, b_hT[g]], b_psF[bk])
                    bR = b_R1 if cc == 0 else b_R2
                    S.op("scalar", lambda e, bk=bk, cc=cc, g=g, col=col: e.activation(
                        out=QT[0:64, 2 * cc, 512 * g:512 * g + 512], in_=psF[0:64, bk, :], func=AF.Identity,
                        bias=par[0:64, col:col + 1]), reads=[b_psF[bk], b_par], writes=[bR])
                    S.op("scalar", lambda e, bk=bk, cc=cc, g=g, col=col: e.activation(
                        out=QT[0:64, 2 * cc + 1, 512 * g:512 * g + 512], in_=psF[64:128, bk, :], func=AF.Identity,
                        bias=par[64:128, col:col + 1]), reads=[b_psF[bk], b_par], writes=[bR])
            colk = PC_BK + a * 4 + jj
            for g in range(4):
                bk = bankA()
                for c in range(8):
                    mm(psF[0:64, bk, :], wkv[:, c, 0:64], hT[:, c, 512 * g:512 * g + 512], c == 0, c == 7,
                       [bw, b_hT[g]], b_psF[bk])
                S.op("scalar", lambda e, bk=bk, g=g: e.activation(
                    out=KT[0:64, 512 * g:512 * g + 512], in_=psF[0:64, bk, :], func=AF.Identity,
                    bias=par[0:64, colk:colk + 1]), reads=[b_psF[bk], b_par], writes=[b_R3])
            for t8 in range(2):
                bk = bankA()
                for ti in range(8):
                    t = t8 * 8 + ti
                    for c in range(8):
                        mm(psF[:, bk, ti * 64:ti * 64 + 64], hT[:, c, 128 * t:128 * t + 128], wkv[:, c, 64:128],
                           c == 0, c == 7, [bw, b_hT[t // 4]], b_psF[bk])
                S.op("vector", lambda e, bk=bk, t8=t8: e.tensor_tensor(
                    out=VP[:, 8 * t8:8 * t8 + 8, 0:64],
                    in0=psF[:, bk, :].rearrange("p (t n) -> p t n", t=8),
                    in1=bvrep[:].unsqueeze(1).broadcast_to([128, 8, 64]), op=ALU.add),
                    reads=[b_psF[bk], b_bv], writes=[b_R3v])
            blocks = []
            for n in range(16):
                kbs = [n - 1, n] if n > 0 else [n]
                for idx, kb in enumerate(kbs):
                    blocks.append((n, idx, kb, len(kbs)))
            state = {}

            for v_ in range(2):
                S.op("scalar", lambda e, v_=v_: e.activation(out=BIA[bi][:, v_, :], in_=BIA[bi][:, v_, :], func=AF.Exp),
                     reads=[b_BIA[bi]], writes=[b_BIA[bi]])

            def stage1(bi_):
                n, idx, kb, nk = blocks[bi_]
                v = 0 if kb == n else 1
                sb = bankA()
                mm(psF[:, sb, :], KT[0:64, 128 * kb:128 * kb + 128], QT[0:64, :, 128 * n:128 * n + 128],
                   True, True, [b_R1, b_R2, b_R3], b_psF[sb])
                ti = nxt("tmp", 3)
                pi = nxt("pt", 4)
                S.op("scalar", lambda e, sb=sb, ti=ti: e.activation(out=TMP[ti][:], in_=psF[:, sb, :], func=AF.Exp,
                                                                    scale=0.125),
                     reads=[b_psF[sb]], writes=[b_TMP[ti]])
                S.op("gpsimd",
                     lambda e, ti=ti, pi=pi, v=v: e.tensor_tensor(out=PT[pi][:], in0=TMP[ti][:],
                                                                  in1=BIA[bi][:, v, :], op=ALU.mult),
                     reads=[b_TMP[ti], b_BIA[bi]], writes=[b_PT[pi]])
                state[bi_] = pi

            def stage2(bi_):
                n, idx, kb, nk = blocks[bi_]
                pi = state.pop(bi_)
                if idx == 0:
                    state["acc"] = bankB()
                acc = state["acc"]
                mm(psF[:, acc, :], VP[:, kb, :], PT[pi][:], idx == 0, idx == nk - 1,
                   [b_R3v, b_PT[pi]], b_psF[acc])
                if idx != nk - 1:
                    return
                state[("n", n)] = acc

            def stage3(n):
                acc = state.pop(("n", n))
                S.op("vector", lambda e: e.tensor_tensor(
                    out=DEN[0:64, :].rearrange("p (h q) -> p h q", h=4),
                    in0=psF[64:128, acc, :].rearrange("p (h q) -> p h q", h=4),
                    in1=esink[64:128, 4 * jj:4 * jj + 4].unsqueeze(2).broadcast_to([64, 4, 128]), op=ALU.add),
                    reads=[b_psF[acc], b_esink], writes=[b_DEN])
                S.op("scalar", lambda e: e.activation(out=DEN[0:64, :], in_=DEN[0:64, :], func=AF.Ln),
                     reads=[b_DEN], writes=[b_DEN])
                S.op("scalar", lambda e: e.activation(out=DEN[0:64, :], in_=DEN[0:64, :], func=AF.Exp, scale=-1.0),
                     reads=[b_DEN], writes=[b_DEN])
                state[("m", n)] = acc

            def stage4(n):
                acc = state.pop(("m", n))
                for two in range(2):
                    S.op("vector", lambda e, two=two: e.tensor_tensor(
                        out=OT[64 * two:64 * two + 64, :, 128 * n:128 * n + 128],
                        in0=psF[0:64, acc, :].rearrange("p (c two q) -> p c two q", c=2, two=2)[:, :, two, :],
                        in1=DEN[0:64, :].rearrange("p (c two q) -> p c two q", c=2, two=2)[:, :, two, :],
                        op=ALU.mult), reads=[b_psF[acc], b_DEN], writes=[b_R4, b_R4b])

            LOOK = 2
            pend3, pend4 = [], []
            for step in range(len(blocks) + LOOK + 2):
                if pend4:
                    stage4(pend4.pop(0))
                if pend3:
                    n_ = pend3.pop(0)
                    stage3(n_)
                    pend4.append(n_)
                if step < len(blocks):
                    stage1(step)
                b2 = step - LOOK
                if 0 <= b2 < len(blocks):
                    stage2(b2)
                    n, idx, kb, nk = blocks[b2]
                    if idx == nk - 1:
                        pend3.append(n)
            assert not pend3 and not pend4

            for t in range(NT):
                for half in range(2):
                    bk = bankA()
                    for c in range(2):
                        mm(psF[:, bk, :], OT[:, c, 128 * t:128 * t + 128], wo[:, c, 512 * half:512 * half + 512],
                           c == 0, c == 1, [b_R4, b_R4b, bw], b_psF[bk])
                    add_to_X_split(t, half, bk, 2 * t + half)

        def swa_layer(a, W_first, prefetch_next):
            QT = R12[:].rearrange("p (h t) -> p h t", h=4)
            KT = R3[:, 0:2048]
            VP = R3[:, 2048:4096].rearrange("p (t n) -> p t n", t=16)
            OT = R4[:].rearrange("p (c t) -> p c t", c=2)
            bQ = [Buf("swaQ%d" % g) for g in range(4)]
            bK = [Buf("swaK%d" % g) for g in range(4)]
            bV = [Buf("swaV%d" % g) for g in range(4)]
            coarse = [b_R1, b_R2, b_R3, b_R3v]
            S.op("gpsimd", lambda e: e.memset(VP[:, :, 64:128], 1.0), writes=coarse + bQ + bK + bV + [b_DEN] + b_PTx)
            Ws = {0: W_first}
            bi = 0

            def jobs(jj, g):
                W = Ws[jj]
                wq, wkv, bw = W["wq"], W["wkv"], W["b"]
                colk = PC_BK + a * 4 + jj
                out = []

                def q_job(cc):
                    col = PC_BQ + a * 8 + jj * 2 + cc
                    bk = bankA()
                    for c in range(8):
                        mm(psF[:, bk, :], wq[:, c, 128 * cc:128 * cc + 128], hT[:, c, 512 * g:512 * g + 512],
                           c == 0, c == 7, [bw, b_hT[g]], b_psF[bk])
                    S.op("scalar", lambda e: e.activation(
                        out=QT[0:64, 2 * cc, 512 * g:512 * g + 512], in_=psF[0:64, bk, :], func=AF.Identity,
                        bias=par[0:64, col:col + 1]), reads=[b_psF[bk], b_par], writes=[bQ[g]])
                    S.op("scalar", lambda e: e.activation(
                        out=QT[0:64, 2 * cc + 1, 512 * g:512 * g + 512], in_=psF[64:128, bk, :], func=AF.Identity,
                        bias=par[64:128, col:col + 1]), reads=[b_psF[bk], b_par], writes=[bQ[g]])

                def k_job():
                    bk = bankA()
                    for c in range(8):
                        mm(psF[0:64, bk, :], wkv[:, c, 0:64], hT[:, c, 512 * g:512 * g + 512], c == 0, c == 7,
                           [bw, b_hT[g]], b_psF[bk])
                    S.op("scalar", lambda e: e.activation(
                        out=KT[0:64, 512 * g:512 * g + 512], in_=psF[0:64, bk, :], func=AF.Identity,
                        bias=par[0:64, colk:colk + 1]), reads=[b_psF[bk], b_par], writes=[bK[g]])

                def v_job():
                    if g == 0:
                        S.dma("sync", lambda e: e.dma_start(
                            out=bvrep[:], in_=bqkv_d[a][1280 + 64 * jj:1280 + 64 * jj + 64].partition_broadcast(128)),
                            "bv", writes=[b_bv])
                    bk = bankA()
                    for ti in range(4):
                        t = 4 * g + ti
                        for c in range(8):
                            mm(psF[:, bk, ti * 64:ti * 64 + 64], hT[:, c, 128 * t:128 * t + 128], wkv[:, c, 64:128],
                               c == 0, c == 7, [bw, b_hT[g]], b_psF[bk])
                    S.op("vector", lambda e: e.tensor_tensor(
                        out=VP[:, 4 * g:4 * g + 4, 0:64],
                        in0=psF[:, bk, 0:256].rearrange("p (t n) -> p t n", t=4),
                        in1=bvrep[:].unsqueeze(1).broadcast_to([128, 4, 64]), op=ALU.add),
                        reads=[b_psF[bk], b_bv], writes=[bV[g]])

                return [lambda: q_job(0), lambda: q_job(1), k_job, v_job]

            for j0 in jobs(0, 0):
                j0()
            def batch(jj):
                W = Ws[jj]
                wo, bw = W["wo"], W["b"]
                if jj < 3:
                    Ws[jj + 1] = swa_loads(a, jj + 1)
                else:
                    prefetch_next()
                S.dma("sync", lambda e, jj=jj: e.dma_start(out=BIA[bi][:], in_=bias_d[:, jj].rearrange("v k f -> k v f")),
                      "bia%d" % bi, writes=[b_BIA[bi]])
                for v_ in range(2):
                    S.op("scalar", lambda e, v_=v_: e.activation(out=BIA[bi][:, v_, :], in_=BIA[bi][:, v_, :], func=AF.Exp),
                         reads=[b_BIA[bi]], writes=[b_BIA[bi]])
                blocks = []
                idx0 = {}
                for n in range(16):
                    kbs = [n - 1, n] if n > 0 else [n]
                    if n % 4 == 0:
                        idx0[n // 4] = len(blocks)
                    for idx, kb in enumerate(kbs):
                        blocks.append((n, idx, kb, len(kbs)))
                idx0[4] = len(blocks)
                jsched = {}
                for g in range(4):
                    if g < 3:
                        js = jobs(jj, g + 1)
                    elif jj < 3:
                        js = jobs(jj + 1, 0)
                    else:
                        js = []
                    for k, jb in enumerate(js):
                        st_ = min(idx0[g] + 1 + 2 * k, idx0[g + 1] - 1)
                        jsched.setdefault(st_, []).append(jb)
                state = {}

                def stage1(bi_):
                    n, idx, kb, nk = blocks[bi_]
                    v = 0 if kb == n else 1
                    sb = bankA()
                    mm(psF[:, sb, :], KT[0:64, 128 * kb:128 * kb + 128], QT[0:64, :, 128 * n:128 * n + 128],
                       True, True, [bQ[n // 4], bK[kb // 4]] + coarse, b_psF[sb])
                    ti = nxt("tmp", 3)
                    pi = nxt("pt", 4)
                    S.op("scalar", lambda e: e.activation(out=TMP[ti][:], in_=psF[:, sb, :], func=AF.Exp, scale=0.125),
                         reads=[b_psF[sb]], writes=[b_TMP[ti]])
                    S.op("gpsimd", lambda e: e.tensor_tensor(out=PT[pi][:], in0=TMP[ti][:], in1=BIA[bi][:, v, :],
                                                             op=ALU.mult),
                         reads=[b_TMP[ti], b_BIA[bi]], writes=[b_PT[pi]])
                    state[bi_] = pi

                def stage2(bi_):
                    n, idx, kb, nk = blocks[bi_]
                    pi = state.pop(bi_)
                    if idx == 0:
                        state["acc"] = bankB()
                    acc = state["acc"]
                    mm(psF[:, acc, :], VP[:, kb, :], PT[pi][:], idx == 0, idx == nk - 1,
                       [bV[kb // 4], b_R3v, b_PT[pi]], b_psF[acc])
                    if idx == nk - 1:
                        state[("n", n)] = acc

                def stage3(n):
                    acc = state.pop(("n", n))
                    S.op("vector", lambda e: e.tensor_tensor(
                        out=DEN[0:64, :].rearrange("p (h q) -> p h q", h=4),
                        in0=psF[64:128, acc, :].rearrange("p (h q) -> p h q", h=4),
                        in1=esink[64:128, 4 * jj:4 * jj + 4].unsqueeze(2).broadcast_to([64, 4, 128]), op=ALU.add),
                        reads=[b_psF[acc], b_esink], writes=[b_DEN])
                    S.op("scalar", lambda e: e.activation(out=DEN[0:64, :], in_=DEN[0:64, :], func=AF.Ln),
                         reads=[b_DEN], writes=[b_DEN])
                    S.op("scalar", lambda e: e.activation(out=DEN[0:64, :], in_=DEN[0:64, :], func=AF.Exp, scale=-1.0),
                         reads=[b_DEN], writes=[b_DEN])
                    state[("m", n)] = acc

                def stage4(n):
                    acc = state.pop(("m", n))
                    for two in range(2):
                        S.op("vector", lambda e, two=two: e.tensor_tensor(
                            out=OT[64 * two:64 * two + 64, :, 128 * n:128 * n + 128],
                            in0=psF[0:64, acc, :].rearrange("p (c two q) -> p c two q", c=2, two=2)[:, :, two, :],
                            in1=DEN[0:64, :].rearrange("p (c two q) -> p c two q", c=2, two=2)[:, :, two, :],
                            op=ALU.mult), reads=[b_psF[acc], b_DEN], writes=[b_R4, b_R4b])

                LOOK = 3
                pend3, pend4 = [], []
                for step in range(len(blocks) + LOOK + 2):
                    if pend4:
                        stage4(pend4.pop(0))
                    if pend3:
                        n_ = pend3.pop(0)
                        stage3(n_)
                        pend4.append(n_)
                    if step < len(blocks):
                        stage1(step)
                    b2 = step - LOOK
                    if 0 <= b2 < len(blocks):
                        stage2(b2)
                        n, idx, kb, nk = blocks[b2]
                        if idx == nk - 1:
                            pend3.append(n)
                    for jb in jsched.pop(step, []):
                        jb()
                assert not pend3 and not pend4 and not jsched
                for t in range(NT):
                    for half in range(2):
                        bk = bankA()
                        for c in range(2):
                            mm(psF[:, bk, :], OT[:, c, 128 * t:128 * t + 128], wo[:, c, 512 * half:512 * half + 512],
                               c == 0, c == 1, [b_R4, b_R4b, bw], b_psF[bk])
                        add_to_X_split(t, half, bk, 2 * t + half)
                arena_done.add(bw.name)

            for jj_ in range(4):
                batch(jj_)

        def swa_layer_pre(a):
            S.dma("sync", lambda e: e.dma_start(out=sinkraw[:], in_=sinks_d[a].partition_broadcast(128)),
                  "sink", writes=[b_sinkraw])
            S.op("scalar", lambda e: e.activation(out=esink[:], in_=sinkraw[:], func=AF.Exp),
                 reads=[b_sinkraw], writes=[b_esink])
            S.dma("sync", lambda e: e.dma_start(out=borep[:], in_=boa_d[a].partition_broadcast(128)),
                  "borep", writes=[b_borep])
            for t in range(NT):
                S.op("gpsimd", lambda e, t=t: e.tensor_tensor(out=X[:, t, :], in0=X[:, t, :], in1=borep[:], op=ALU.add),
                     reads=[bX[t], b_borep], writes=[bX[t]])

        def fox_pre_loads(a):
            off, b, over = arena_alloc(128, "wf%d" % a)
            wf = WA[:, off:off + 128].rearrange("p (c n) -> p c n", c=8)
            wload(wf, wqkvf_d[a].rearrange("(c p) n -> p c n", p=128)[:, :, 3072:3088], b, over, True)
            return dict(wf=wf, b=b)

        def fox_pre(a, W):
            wf, bw = W["wf"], W["b"]
            VP = R3[:].rearrange("p (t s n) -> p t s n", t=16, s=2)
            S.op("gpsimd", lambda e: e.memset(ones16, 1.0), writes=[b_borep, b_DEN] + b_PTx)
            S.op("gpsimd", lambda e: e.memset(VP[:, :, 0, 64:128], 1.0), writes=[b_R3v, b_R3])
            S.op("gpsimd", lambda e: e.memset(VP[:, :, 1, 0:64], 1.0), reads=[b_R3v], writes=[b_R3v])
            for g in range(4):
                bk = bankA()
                for c in range(8):
                    mm(psF[0:16, bk, :], wf[:, c, :], hT[:, c, 512 * g:512 * g + 512], c == 0, c == 7,
                       [bw, b_hT[g]], b_psF[bk])
                t1 = nxt("tmp", 3)
                S.op("scalar", lambda e, bk=bk, t1=t1: e.activation(out=TMP[t1][0:16, :], in_=psF[0:16, bk, :],
                                                                    func=AF.Exp, scale=-1.0, bias=nbf[:, a:a + 1]),
                     reads=[b_psF[bk], b_nbf], writes=[b_TMP[t1]])
                S.op("scalar", lambda e, t1=t1: e.activation(out=TMP[t1][0:16, :], in_=TMP[t1][0:16, :], func=AF.Ln,
                                                             bias=1.0), reads=[b_TMP[t1]], writes=[b_TMP[t1]])
                t2 = nxt("tmp", 3)
                if g == 0:
                    S.op("vector", lambda e, t1=t1, t2=t2: e.tensor_tensor_scan(
                        out=TMP[t2][0:16, :], data0=ones16, data1=TMP[t1][0:16, :], initial=0.0,
                        op0=ALU.mult, op1=ALU.subtract), reads=[b_borep, b_TMP[t1]], writes=[b_TMP[t2]])
                else:
                    S.op("vector", lambda e, t1=t1, t2=t2, g=g: e.tensor_tensor_scan(
                        out=TMP[t2][0:16, :], data0=ones16, data1=TMP[t1][0:16, :], initial=carry[:, g - 1:g],
                        op0=ALU.mult, op1=ALU.subtract), reads=[b_borep, b_TMP[t1], b_carry], writes=[b_TMP[t2]])
                S.op("vector", lambda e, t2=t2, g=g: e.tensor_copy(out=carry[:, g:g + 1], in_=TMP[t2][0:16, 511:512]),
                     reads=[b_TMP[t2]], writes=[b_carry])
                cs = slice(512 * g, 512 * g + 512)
                S.op("vector", lambda e, t2=t2, cs=cs: e.tensor_scalar(out=AUG[0:16, cs], in0=TMP[t2][0:16, :], scalar1=8.0,
                                                                       scalar2=None, op0=ALU.mult),
                     reads=[b_TMP[t2]], writes=[b_AUG])
                S.op("vector", lambda e, t1=t1, t2=t2, cs=cs: e.scalar_tensor_tensor(
                    out=TMP[t1][0:16, :], in0=TMP[t2][0:16, :], scalar=8.0, in1=AUG[0:16, cs], op0=ALU.mult,
                    op1=ALU.subtract), reads=[b_TMP[t2], b_AUG], writes=[b_TMP[t1]])
                S.op("vector", lambda e, t1=t1: e.tensor_copy(out=midS[:], in_=TMP[t1][0:16, :]),
                     reads=[b_TMP[t1]], writes=[b_midS])
                S.op("vector", lambda e, cs=cs: e.tensor_copy(out=AUG[32:48, cs], in_=midS[:]),
                     reads=[b_midS], writes=[b_AUG])
                S.op("vector", lambda e, t1=t1, t2=t2: e.tensor_tensor(out=TMP[t2][0:16, :], in0=TMP[t1][0:16, :],
                                                                       in1=midS[:], op=ALU.subtract),
                     reads=[b_TMP[t1], b_midS], writes=[b_TMP[t2]])
                S.op("vector", lambda e, t2=t2, cs=cs: e.tensor_copy(out=AUG[64:80, cs], in_=TMP[t2][0:16, :]),
                     reads=[b_TMP[t2]], writes=[b_AUG])

        def fox_loads(a, b_):
            off, b, over = arena_alloc(4096, "fox%d_%d" % (a, b_))
            wq = WA[:, off:off + 1024].rearrange("p (c n) -> p c n", c=8)
            wk = WA[:, off + 1024:off + 2048].rearrange("p (c n) -> p c n", c=8)
            wv = WA[:, off + 2048:off + 3072].rearrange("p (c n) -> p c n", c=8)
            wo = WA[:, off + 3072:off + 4096]
            src = wqkvf_d[a].rearrange("(c p) n -> p c n", p=128)
            wload(wq, src[:, :, 128 * b_:128 * b_ + 128], b, over, True)
            wload(wk, src[:, :, 1024 + 128 * b_:1024 + 128 * b_ + 128], b, over, False)
            wload(wv, src[:, :, 2048 + 128 * b_:2048 + 128 * b_ + 128], b, over, False)
            wload(wo, wob_d[a][128 * b_:128 * b_ + 128, :], b, over, False)
            return dict(wq=wq, wk=wk, wv=wv, wo=wo, b=b)

        def fox_batch(a, b_, W):
            wq, wk, wv, wo, bw = W["wq"], W["wk"], W["wv"], W["wo"], W["b"]
            QA = R12[:, 0:4096].rearrange("p (s t) -> p s t", s=2)
            KA = R12[:, 4096:8192].rearrange("p (s t) -> p s t", s=2)
            VP = R3[:].rearrange("p (t s n) -> p t s n", t=16, s=2)
            OT = R4[:, 2048 * (b_ % 2):2048 * (b_ % 2) + 2048]
            bOT = b_R4 if b_ % 2 == 0 else b_R4b
            extra = [b_R1, b_R2] if b_ == 0 else []
            for s in range(2):
                S.op("gpsimd", lambda e, s=s: e.memset(QA[64:70, s, :], -1.0), writes=[b_QAaug[s]] + extra)
                S.op("gpsimd", lambda e, s=s: e.memset(KA[64:70, s, :], 1.0), writes=[b_KAaug[s]] + extra)
            for s in range(2):
                h = 2 * b_ + s
                src = bass.AP(AUG, h * S_LEN, [[32 * S_LEN, 3], [1, S_LEN]])
                S.dma("sync", lambda e, s=s, src=src: e.dma_start(out=QA[64:67, s, :], in_=src), "augq%d" % s,
                      reads=[b_AUG], writes=[b_QAaug[s]])
                S.dma("sync", lambda e, s=s, src=src: e.dma_start(out=KA[67:70, s, :], in_=src), "augk%d" % s,
                      reads=[b_AUG], writes=[b_KAaug[s]])
            for (w_, dst, bR) in ((wq, QA, b_R1), (wk, KA, b_R2)):
                for g in range(4):
                    bk = bankA()
                    for c in range(8):
                        mm(psF[:, bk, :], w_[:, c, :], hT[:, c, 512 * g:512 * g + 512], c == 0, c == 7,
                           [bw, b_hT[g]], b_psF[bk])
                    S.op("scalar", lambda e, bk=bk, g=g, dst=dst: e.activation(
                        out=dst[0:64, 0, 512 * g:512 * g + 512], in_=psF[0:64, bk, :], func=AF.Copy),
                        reads=[b_psF[bk]], writes=[bR])
                    S.op("vector", lambda e, bk=bk, g=g, dst=dst: e.tensor_copy(
                        out=dst[0:64, 1, 512 * g:512 * g + 512], in_=psF[64:128, bk, :]),
                        reads=[b_psF[bk]], writes=[bR])
            for g in range(4):
                bk = bankA()
                for c in range(8):
                    mm(psF[:, bk, :], wv[:, c, :], hT[:, c, 512 * g:512 * g + 512], c == 0, c == 7,
                       [bw, b_hT[g]], b_psF[bk])
                pi = nxt("pt", 4)
                S.op("scalar", lambda e, bk=bk, pi=pi: e.activation(out=PT[pi][:], in_=psF[:, bk, :], func=AF.Copy),
                     reads=[b_psF[bk]], writes=[b_PT[pi]])
                ti = nxt("T", 2)
                for i4 in range(4):
                    S.op("tensor", lambda e, pi=pi, ti=ti, i4=i4: e.transpose(
                        out=psTv[ti][:, 128 * i4:128 * i4 + 128], in_=PT[pi][:, 128 * i4:128 * i4 + 128],
                        identity=ident[:]), reads=[b_PT[pi], b_ident], writes=[b_psT[ti]])
                S.op("vector", lambda e, ti=ti, g=g: e.tensor_copy(
                    out=VP[:, 4 * g:4 * g + 4, 0, 0:64],
                    in_=psTv[ti][:, 0:512].rearrange("p (t n) -> p t n", t=4)[:, :, 0:64]),
                    reads=[b_psT[ti]], writes=[b_R3v])
                S.op("vector", lambda e, ti=ti, g=g: e.tensor_copy(
                    out=VP[:, 4 * g:4 * g + 4, 1, 64:128],
                    in_=psTv[ti][:, 0:512].rearrange("p (t n) -> p t n", t=4)[:, :, 64:128]),
                    reads=[b_psT[ti]], writes=[b_R3v])
            blocks = []
            for s_ in range(2):
                for G in range(4):
                    nkb = 4 * G + 4
                    for j in range(nkb):
                        blocks.append((s_, G, j, nkb))
            state = {}

            def stage1(bi_):
                s_, G, j, nkb = blocks[bi_]
                i = j - 4 * G
                qoff = 128 * i if i > 0 else 0
                ncols = 512 - qoff
                q0 = 512 * G + qoff
                sb = bankA()
                mm(psF[:, sb, 0:ncols], KA[0:70, s_, 128 * j:128 * j + 128], QA[0:70, s_, q0:q0 + ncols],
                   True, True, [b_R1, b_R2, b_QAaug[s_], b_KAaug[s_]], b_psF[sb])
                pi = nxt("ptf", 6)
                S.op("scalar", lambda e, sb=sb, pi=pi, ncols=ncols: e.activation(
                    out=PTF[pi][:, 0:ncols], in_=psF[:, sb, 0:ncols], func=AF.Exp, scale=0.125),
                    reads=[b_psF[sb]], writes=[b_PTF[pi]])
                if i >= 0:
                    S.op("gpsimd", lambda e, pi=pi: e.tensor_tensor(out=PTF[pi][:, 0:128], in0=PTF[pi][:, 0:128],
                                                                    in1=tri[:], op=ALU.mult),
                         reads=[b_PTF[pi], b_tri], writes=[b_PTF[pi]])
                state[bi_] = (pi, qoff, ncols)

            def stage2(bi_):
                s_, G, j, nkb = blocks[bi_]
                pi, qoff, ncols = state.pop(bi_)
                if j == 0:
                    state["acc"] = bankB()
                acc = state["acc"]
                mm(psF[:, acc, qoff:512], VP[:, j, s_, :], PTF[pi][:, 0:ncols], j == 0, j == nkb - 1,
                   [b_R3v, b_PTF[pi]], b_psF[acc])
                if j != nkb - 1:
                    return
                ti = nxt("tmp", 3)
                lo, hi = (0, 64) if s_ == 0 else (64, 128)
                slo, shi = (64, 128) if s_ == 0 else (0, 64)
                S.op("vector", lambda e, acc=acc, ti=ti, lo=lo, hi=hi, slo=slo, shi=shi: e.reciprocal(
                    out=TMP[ti][lo:hi, :], in_=psF[slo:shi, acc, :]), reads=[b_psF[acc]], writes=[b_TMP[ti]])
                S.op("vector", lambda e, acc=acc, ti=ti, lo=lo, hi=hi, G=G: e.tensor_tensor(
                    out=OT[lo:hi, 512 * G:512 * G + 512], in0=psF[lo:hi, acc, :], in1=TMP[ti][lo:hi, :],
                    op=ALU.mult), reads=[b_psF[acc], b_TMP[ti]], writes=[bOT])

            LOOK = 5
            for step in range(len(blocks) + LOOK):
                if step < len(blocks):
                    stage1(step)
                if step - LOOK >= 0:
                    stage2(step - LOOK)
            fox_state["wo%d" % (b_ % 2)] = (wo, bw)
            if b_ % 2 == 1:
                OT2 = R4[:].rearrange("p (c t) -> p c t", c=2)
                for t in range(NT):
                    for half in range(2):
                        bk = bankA()
                        for c in range(2):
                            wo_c, bw_c = fox_state["wo%d" % c]
                            mm(psF[:, bk, :], OT2[:, c, 128 * t:128 * t + 128], wo_c[:, 512 * half:512 * half + 512],
                               c == 0, c == 1, [b_R4, b_R4b, bw_c], b_psF[bk])
                        add_to_X_split(t, half, bk, 2 * t + half)

        def mlp_load_up(l, q4):
            off, b, over = arena_alloc(8192, "mup%d_%d" % (l, q4), align=8192)
            wu = WA[:, off:off + 8192].rearrange("p (c n) -> p c n", c=8)
            src = wup_d[l].rearrange("(c p) n -> p c n", p=128)
            wload(wu[:, :, 0:512], src[:, :, 1024 * q4:1024 * q4 + 512], b, over, True)
            wload(wu[:, :, 512:1024], src[:, :, 1024 * q4 + 512:1024 * q4 + 1024], b, over, False)
            return dict(wu=wu, b=b)

        def mlp_load_dn(l, q4):
            off, b, over = arena_alloc(8192, "mdn%d_%d" % (l, q4), align=8192)
            wd = WA[:, off:off + 8192].rearrange("p (f n) -> p f n", f=8)
            src = wdn_d[l][1024 * q4:1024 * q4 + 1024, :].rearrange("(f p) n -> p f n", p=128)
            wload(wd[:, 0:4, :], src[:, 0:4, :], b, over, True)
            wload(wd[:, 4:8, :], src[:, 4:8, :], b, over, False)
            return dict(wd=wd, b=b)

        def mlp_pass(l, q4, W0, W1, hook, next_gcol=None):
            pending_tiles = []
            wu, bwu = W0["wu"], W0["b"]
            wd, bwd = W1["wd"], W1["b"]

            def bufs(g):
                ai = g % 2
                aT = (R12[:, 0:4096] if ai == 0 else R12[:, 4096:8192]).rearrange("p (f t) -> p f t", f=8)
                return aT, (b_R1 if ai == 0 else b_R2)

            def up(g):
                aT, bA = bufs(g)
                for f in range(8):
                    bk = bankA()
                    for c in range(8):
                        mm(psF[:, bk, :], wu[:, c, 128 * f:128 * f + 128], hT[:, c, 512 * g:512 * g + 512],
                           c == 0, c == 7, [bwu, b_hT[g]], b_psF[bk])
                    ti = nxt("tmp", 3)
                    S.op("scalar", lambda e, bk=bk, ti=ti: e.activation(out=TMP[ti][:], in_=psF[:, bk, :], func=AF.Relu),
                         reads=[b_psF[bk]], writes=[b_TMP[ti]])
                    S.op("gpsimd", lambda e, ti=ti, f=f, aT=aT: e.tensor_tensor(out=aT[:, f, :], in0=TMP[ti][:],
                                                                                in1=TMP[ti][:], op=ALU.mult),
                         reads=[b_TMP[ti]], writes=[bA])
                if g == 3:
                    arena_done.add(bwu.name)
                    hook()
                if next_gcol is not None and pending_tiles:
                    norm_group_tiles(pending_tiles.pop(0), next_gcol)

            def down(g):
                aT, bA = bufs(g)
                for tt in range(4):
                    t = 4 * g + tt
                    for half in range(2):
                        bk = bankB()
                        for f in range(8):
                            mm(psF[:, bk, :], aT[:, f, 128 * tt:128 * tt + 128], wd[:, f, 512 * half:512 * half + 512],
                               f == 0, f == 7, [bA, bwd], b_psF[bk])
                        add_to_X(t, half, bk)
                if next_gcol is not None:
                    norm_group_stats(g)
                    pending_tiles.append(g)

            up(0)
            up(1)
            down(0)
            up(2)
            down(1)
            up(3)
            down(2)
            down(3)
            if next_gcol is not None:
                while pending_tiles:
                    norm_group_tiles(pending_tiles.pop(0), next_gcol)
                norm_done.add(next_gcol)

        units = []

        def U(l0, l1, comp, self_pf=False):
            units.append(dict(l0=l0, l1=l1, comp=comp, self_pf=self_pf))

        for l in layers:
            a = l // 2
            U(None, None, lambda W0, W1, hook, l=l: (None if (PC_GMIX + l * 8) in norm_done else norm_phase(PC_GMIX + l * 8)))
            if l % 2 == 0:
                U(None, None, lambda W0, W1, hook, a=a: swa_layer_pre(a))
                U(lambda a=a: swa_loads(a, 0), None, lambda W0, W1, hook, pf, a=a: swa_layer(a, W0, pf), self_pf=True)
            else:
                U(lambda a=a: fox_pre_loads(a), None, lambda W0, W1, hook, a=a: fox_pre(a, W0))
                for b_ in range(8):
                    U(lambda a=a, b_=b_: fox_loads(a, b_), None, lambda W0, W1, hook, a=a, b_=b_: fox_batch(a, b_, W0))
            U(None, None, lambda W0, W1, hook, l=l: norm_phase(PC_GMLP + l * 8))
            for q4 in range(4):
                U(lambda l=l, q4=q4: mlp_load_up(l, q4), lambda l=l, q4=q4: mlp_load_dn(l, q4),
                  lambda W0, W1, hook, l=l, q4=q4: mlp_pass(
                      l, q4, W0, W1, hook,
                      next_gcol=(PC_GMIX + (l + 1) * 8) if (q4 == 3 and (l + 1) in layers) else None))

        L0, L1 = {}, {}

        def load0(i):
            if i < len(units) and units[i]["l0"] is not None and i not in L0:
                L0[i] = units[i]["l0"]()

        def load1(i):
            if i < len(units) and units[i]["l1"] is not None and i not in L1:
                L1[i] = units[i]["l1"]()

        def next_loader(i):
            j = i + 1
            while j < len(units) and units[j]["l0"] is None:
                j += 1
            return j

        first = 0 if units[0]["l0"] is not None else next_loader(0)
        load0(first)
        load1(first)
        for i, u in enumerate(units):
            if u["l0"] is None:
                u["comp"](None, None, None)
                continue
            load0(i)
            load1(i)
            nj = next_loader(i)
            if not u["self_pf"]:
                load0(nj)
            called = [False]

            def hook(nj=nj, called=called):
                if not called[0]:
                    called[0] = True
                    load1(nj)

            if u["self_pf"]:
                u["comp"](L0[i], L1.get(i), hook, lambda nj=nj: load0(nj))
                load0(nj)
            else:
                u["comp"](L0[i], L1.get(i), hook)
            arena_done.add(L0[i]["b"].name)
            if i in L1:
                arena_done.add(L1[i]["b"].name)
            hook()

        if final:
            S.dma("sync", lambda e: e.dma_start(out=borep[:], in_=gfin_d.partition_broadcast(128)),
                  "borep", writes=[b_borep])
            rms_stats()
            for t in range(NT):
                S.op("vector", lambda e, t=t: e.scalar_tensor_tensor(
                    out=X[:, t, :], in0=X[:, t, :], scalar=rstd[:, t:t + 1], in1=borep[:], op0=ALU.mult, op1=ALU.mult),
                    reads=[bX[t], b_rstd, b_borep], writes=[bX[t]])
        ov = out_d.rearrange("(t p) d -> p t d", p=128)
        outs = []
        for t4 in range(4):
            outs.append(S.dma("sync", lambda e, t4=t4: e.dma_start(out=ov[:, 4 * t4:4 * t4 + 4, :],
                                                                    in_=X[:, 4 * t4:4 * t4 + 4, :]),
                              "out%d" % t4, reads=bX[4 * t4:4 * t4 + 4]))
        with nc.Block() as block:
            S.emit(block, final_wait_ops=outs)
    return nc


_PROGRAMS = {}


def _get_program(key):
    if key not in _PROGRAMS:
        _PROGRAMS[key] = build(*key)
    return _PROGRAMS[key]


LAUNCH_PLAN = [((0, 1, 2, 3), True)]


def kernel(x, rel_bias, norm_mix, norm_mlp, w_qkv_a, b_qkv_a, sinks_a, w_o_a, b_o_a,
           w_qkvf_b, b_f_b, w_o_b, w_up, w_down, norm_final):
    f = lambda a: np.ascontiguousarray(np.asarray(a, dtype=np.float32))
    x = f(x)
    params = _layout_params(f(norm_mix), f(norm_mlp), f(b_qkv_a), f(b_f_b))
    biasT = _layout_bias(f(rel_bias)).reshape(2, 4, 128, 512)
    shared = {"params": params, "biasT": biasT, "w_qkv_a": f(w_qkv_a), "b_qkv_a": f(b_qkv_a),
              "sinks_a": f(sinks_a), "w_o_a": f(w_o_a), "b_o_a": f(b_o_a), "w_qkvf_b": f(w_qkvf_b),
              "w_o_b": f(w_o_b), "w_up": f(w_up), "w_down": f(w_down), "norm_final": f(norm_final)}
    cur = [x[b] for b in range(8)]
    for key in LAUNCH_PLAN:
        nc = _get_program(key)
        in_maps = [dict(shared, x=cur[b]) for b in range(8)]
        res = run_bass_kernel_spmd(nc, in_maps, core_ids=list(range(8)))
        cur = [np.asarray(res.results[b]["out"], dtype=np.float32) for b in range(8)]
    return np.stack(cur, axis=0).astype(np.float32)
```

```python
from contextlib import ExitStack
import math
import numpy as np
import concourse.bass as bass
import concourse.mybir as mybir
from concourse.bass_utils import run_bass_kernel_spmd

F32 = mybir.dt.float32
BF16 = mybir.dt.bfloat16
AF = mybir.ActivationFunctionType
ALU = mybir.AluOpType

S_LEN = 2048
D = 1024
DFF = 4096
NT = 16
DEPTH = 4
NEG = -1e30
EPS = 1e-6

ENGS = ("tensor", "scalar", "vector", "gpsimd", "sync")
STRICT = True


class Buf:
    __slots__ = ("name", "writers", "readers")

    def __init__(self, name):
        self.name = name
        self.writers = []
        self.readers = []


class Op:
    __slots__ = ("idx", "eng", "fn", "deps", "wdeps", "is_dma", "sem", "semval", "milestone", "needed", "eff")

    def __init__(self, idx, eng, fn, is_dma=False):
        self.idx = idx
        self.eng = eng
        self.fn = fn
        self.deps = set()
        self.wdeps = set()
        self.eff = ()
        self.is_dma = is_dma
        self.sem = None
        self.semval = 0
        self.milestone = 0
        self.needed = False


class Sched:
    def __init__(self, nc, stack):
        self.nc = nc
        self.stack = stack
        self.ops = []
        self.dma_sems = {}
        self.eng_sem = {}
        for e in ENGS:
            self.eng_sem[e] = stack.enter_context(nc.semaphore("es_" + e))

    def _dma_sem(self, key):
        if key not in self.dma_sems:
            h = self.stack.enter_context(self.nc.semaphore("ds_%s" % key))
            self.dma_sems[key] = [h, 0]
        return self.dma_sems[key]

    def _add(self, op, reads, writes):
        for b in reads:
            for w in b.writers:
                op.deps.add(w)
        for b in writes:
            for w in b.writers:
                op.wdeps.add(w)
            for r in b.readers:
                op.wdeps.add(r)
        op.deps.discard(op.idx)
        op.wdeps.discard(op.idx)
        for b in reads:
            b.readers.append(op.idx)
        for b in writes:
            b.writers = [op.idx]
            b.readers = []
        self.ops.append(op)
        return op

    def op(self, eng, fn, reads=(), writes=()):
        return self._add(Op(len(self.ops), eng, fn), reads, writes)

    def dma(self, eng, fn, semkey, reads=(), writes=(), join=False):
        o = Op(len(self.ops), eng, fn, is_dma=True)
        s = self._dma_sem(semkey)
        s[1] += 16
        o.sem = s[0]
        o.semval = s[1]
        if join:
            for b in reads:
                for w in b.writers:
                    o.deps.add(w)
                b.readers.append(o.idx)
            for b in writes:
                for w in b.writers:
                    o.deps |= self.ops[w].deps
                    o.wdeps |= self.ops[w].wdeps
                b.writers = b.writers + [o.idx]
            self.ops.append(o)
            return o
        return self._add(o, reads, writes)

    def emit(self, block, final_wait_ops=()):
        ops = self.ops
        for o in ops:
            eff = []
            for d in o.deps:
                p = ops[d]
                if (not p.is_dma) and p.eng == o.eng and p.eng == "tensor":
                    continue
                eff.append(d)
            for d in o.wdeps:
                if d in o.deps:
                    continue
                p = ops[d]
                if (not p.is_dma) and p.eng == o.eng and not o.is_dma and (p.eng == "tensor" or not STRICT):
                    continue
                eff.append(d)
            o.eff = eff
            for d in eff:
                if not ops[d].is_dma:
                    ops[d].needed = True
        cnt = {e: 0 for e in ENGS}
        for o in ops:
            if not o.is_dma and o.needed:
                cnt[o.eng] += 1
                o.milestone = cnt[o.eng]
        per_eng = {e: [] for e in ENGS}
        for o in ops:
            per_eng[o.eng].append(o)
        sched = self

        def run(engname, eng):
            waited = {}
            for o in per_eng[engname]:
                need = {}
                for d in o.eff:
                    p = ops[d]
                    if p.is_dma:
                        key = ("d", id(p.sem))
                        v = p.semval
                        h = p.sem
                    else:
                        key = ("e", p.eng)
                        v = p.milestone
                        h = sched.eng_sem[p.eng]
                    if need.get(key, (None, 0))[1] < v:
                        need[key] = (h, v)
                for key, (h, v) in need.items():
                    if waited.get(key, 0) >= v:
                        continue
                    eng.wait_ge(h, v)
                    waited[key] = v
                ins = o.fn(eng)
                if o.is_dma:
                    ins.then_inc(o.sem, 16)
                elif o.needed:
                    ins.then_inc(sched.eng_sem[engname], 1)
            if engname == "sync":
                for o in final_wait_ops:
                    eng.wait_ge(o.sem, o.semval)

        @block.tensor
        def _(e):
            run("tensor", e)

        @block.scalar
        def _(e):
            run("scalar", e)

        @block.vector
        def _(e):
            run("vector", e)

        @block.gpsimd
        def _(e):
            run("gpsimd", e)

        @block.sync
        def _(e):
            run("sync", e)


NPAR = 96
PC_GMIX, PC_GMLP, PC_BQ, PC_BK, PC_BF = 0, 32, 64, 80, 88


def _t5_bucket_np(dist):
    n_buckets, max_distance = 32, 128
    max_exact = n_buckets // 2
    d = np.maximum(dist, 0)
    dl = np.maximum(d, 1).astype(np.float32)
    large = max_exact + (np.log(dl / np.float32(max_exact)) / np.float32(math.log(max_distance / max_exact))
                         * np.float32(n_buckets - max_exact)).astype(np.int32)
    large = np.minimum(large, n_buckets - 1)
    return np.where(d < max_exact, d, large)


def _bucket_table():
    return _t5_bucket_np(np.arange(256, dtype=np.int32))


def _layout_params(norm_mix, norm_mlp, b_qkv_a, b_f_b):
    p = np.zeros((128, NPAR), np.float32)
    for l in range(DEPTH):
        p[:, PC_GMIX + l * 8:PC_GMIX + l * 8 + 8] = norm_mix[l].reshape(8, 128).T
        p[:, PC_GMLP + l * 8:PC_GMLP + l * 8 + 8] = norm_mlp[l].reshape(8, 128).T
    for a in range(2):
        p[:, PC_BQ + a * 8:PC_BQ + a * 8 + 8] = b_qkv_a[a][0:1024].reshape(8, 128).T
        p[0:64, PC_BK + a * 4:PC_BK + a * 4 + 4] = b_qkv_a[a][1024:1280].reshape(4, 64).T
        p[0:16, PC_BF + a] = b_f_b[a]
    return p


def _layout_bias(rel_bias):
    bt = _bucket_table()
    k = np.arange(128)[:, None]
    q = np.arange(128)[None, :]
    out = np.empty((2, 4, 128, 4, 128), np.float32)
    for v in range(2):
        dist = q - k + (128 if v == 1 else 0)
        valid = (dist >= 0) & (dist < 128)
        idx = bt[np.clip(dist, 0, 255)]
        for h in range(16):
            g = rel_bias[idx, h]
            out[v, h // 4, :, h % 4, :] = np.where(valid, g, np.float32(NEG))
    return out


def build(layers=(0, 1, 2, 3), final=True):
    nc = bass.Bass("TRN2", target_bir_lowering=False)
    x_d = nc.dram_tensor("x", [S_LEN, D], F32, kind="ExternalInput").ap()
    par_d = nc.dram_tensor("params", [128, NPAR], F32, kind="ExternalInput").ap()
    bias_d = nc.dram_tensor("biasT", [2, 4, 128, 512], F32, kind="ExternalInput").ap()
    wqkv_d = nc.dram_tensor("w_qkv_a", [2, D, 1536], F32, kind="ExternalInput").ap()
    bqkv_d = nc.dram_tensor("b_qkv_a", [2, 1536], F32, kind="ExternalInput").ap()
    sinks_d = nc.dram_tensor("sinks_a", [2, 16], F32, kind="ExternalInput").ap()
    woa_d = nc.dram_tensor("w_o_a", [2, D, D], F32, kind="ExternalInput").ap()
    boa_d = nc.dram_tensor("b_o_a", [2, D], F32, kind="ExternalInput").ap()
    wqkvf_d = nc.dram_tensor("w_qkvf_b", [2, D, 3088], F32, kind="ExternalInput").ap()
    wob_d = nc.dram_tensor("w_o_b", [2, D, D], F32, kind="ExternalInput").ap()
    wup_d = nc.dram_tensor("w_up", [DEPTH, D, DFF], F32, kind="ExternalInput").ap()
    wdn_d = nc.dram_tensor("w_down", [DEPTH, DFF, D], F32, kind="ExternalInput").ap()
    gfin_d = nc.dram_tensor("norm_final", [D], F32, kind="ExternalInput").ap()
    out_d = nc.dram_tensor("out", [S_LEN, D], F32, kind="ExternalOutput").ap()

    with ExitStack() as st:
        S = Sched(nc, st)

        def T(name, shape, dt):
            return st.enter_context(nc.sbuf_tensor(name, shape, dt))

        X = T("X", [128, NT, D], F32)
        hT = T("hT", [128, 8, S_LEN], BF16)
        ARENA = 24576
        WA = T("WA", [128, ARENA], BF16)
        R12 = T("R12", [128, 8192], BF16)
        R3 = T("R3", [128, 4096], BF16)
        R4 = T("R4", [128, 4096], BF16)
        PT = [T("PT%d" % i, [128, 512], BF16) for i in range(4)]
        TMP = [T("TMP%d" % i, [128, 512], F32) for i in range(3)]
        BIA = [T("BIA0", [128, 2, 512], F32)]
        ident = T("ident", [128, 128], BF16)
        identf = T("identf", [128, 128], F32)
        tri = T("tri", [128, 128], BF16)
        par = T("par", [128, NPAR], F32)
        ss = T("ss", [128, 16], F32)
        ms = T("ms", [128, 16], F32)
        rstd = T("rstd", [128, 16], F32)
        XN = [T("XN%d" % i, [128, D], BF16) for i in range(2)]
        junk = XN[1]
        bvrep = T("bvrep", [128, 64], F32)
        DEN = T("DEN", [128, 512], F32)
        DENb = DEN[:].bitcast(BF16)
        PTX = [DENb[:, 0:512], DENb[:, 512:1024]]
        sinkraw = T("sinkraw", [128, 16], F32)
        esink = T("esink", [128, 16], F32)
        borep = T("borep", [128, D], F32)
        AUG = T("AUG", [128, S_LEN], BF16)
        ones16 = borep[0:16, 0:512]
        midS = T("midS", [16, 512], BF16)
        carry = T("carry", [16, 4], F32)
        nbf = T("nbf", [16, 2], F32)
        psF = st.enter_context(nc.psum_tensor("psF", [128, 8, 512], F32))
        psTv = [psF[:, 6, :].bitcast(BF16), psF[:, 7, :].bitcast(BF16)]

        bX = [Buf("X%d" % t) for t in range(NT)]
        b_hT = [Buf("hT%d" % g) for g in range(4)]
        b_R1, b_R2, b_R3, b_R4 = Buf("R1"), Buf("R2"), Buf("R3"), Buf("R4")
        b_R3v = Buf("R3v")
        b_R4b = Buf("R4b")
        b_QAaug = [Buf("QAaug0"), Buf("QAaug1")]
        b_KAaug = [Buf("KAaug0"), Buf("KAaug1")]
        fox_state = {}
        b_PT = [Buf("PT%d" % i) for i in range(4)]
        b_TMP = [Buf("TMP%d" % i) for i in range(3)]
        b_BIA = [Buf("BIA0")]
        b_ident, b_identf, b_tri, b_par = Buf("ident"), Buf("identf"), Buf("tri"), Buf("par")
        b_ss, b_ms, b_rstd = Buf("ss"), Buf("ms"), Buf("rstd")
        b_ssg = [Buf("ssg%d" % g) for g in range(4)]
        b_msg = [Buf("msg%d" % g) for g in range(4)]
        b_rstdg = [Buf("rstdg%d" % g) for g in range(4)]
        norm_done = set()
        b_XN = [Buf("XN0"), Buf("XN1")]
        b_junk = b_XN[1]
        b_DEN = Buf("DEN")
        b_PTx = [Buf("PTx0"), Buf("PTx1")]
        b_bv, b_sinkraw, b_esink, b_borep = Buf("bv"), Buf("sinkraw"), Buf("esink"), Buf("borep")
        b_AUG, b_midS, b_carry, b_nbf, b_ones16 = Buf("AUG"), Buf("midS"), Buf("carry"), Buf("nbf"), Buf("ones16")
        b_psF = [Buf("psF%d" % i) for i in range(8)]
        b_psT = [b_psF[6], b_psF[7]]

        rr = {"A": 0, "B": 0, "T": 0, "pt": 0, "tmp": 0, "xn": 0, "ptf": 0}
        PTF = PT + PTX
        b_PTF = b_PT + b_PTx

        def bankA():
            i = rr["A"] % 4
            rr["A"] += 1
            return i

        def bankB():
            i = 4 + rr["B"] % 4
            rr["B"] += 1
            return i

        def nxt(key, n):
            i = rr[key] % n
            rr[key] += 1
            return i

        arena = {"off": 0, "live": []}
        arena_done = set()

        def arena_alloc(n, name, align=1):
            off = ((arena["off"] + align - 1) // align) * align
            if off + n > ARENA:
                off = 0
            end = off + n
            over = [a for a in arena["live"] if not (a[1] <= off or a[0] >= end)]
            for a_ in over:
                assert a_[2].name in arena_done, "arena overlap with pending allocation %s" % a_[2].name
            arena["live"] = [a for a in arena["live"] if (a[1] <= off or a[0] >= end)]
            b = Buf(name)
            arena["live"].append((off, end, b))
            arena["off"] = end
            return off, b, [a[2] for a in over]

        wcount = [0]

        def wload(dst_ap, src_ap, b, over, first):
            wcount[0] += 1
            key = "w_" + b.name
            if first:
                S.dma("gpsimd", lambda e: e.dma_start(out=dst_ap, in_=src_ap), key, writes=[b] + over)
            else:
                S.dma("gpsimd", lambda e: e.dma_start(out=dst_ap, in_=src_ap), key, writes=[b], join=True)

        S.dma("sync", lambda e: e.dma_start(out=par[:], in_=par_d), "par", writes=[b_par])
        xv = x_d.rearrange("(t p) d -> p t d", p=128)
        for t4 in range(4):
            S.dma("sync", lambda e, t4=t4: e.dma_start(out=X[:, 4 * t4:4 * t4 + 4, :], in_=xv[:, 4 * t4:4 * t4 + 4, :]),
                  "x%d" % t4, writes=bX[4 * t4:4 * t4 + 4])
        S.op("gpsimd", lambda e: e.memset(identf[:], 1.0), writes=[b_identf])
        S.op("gpsimd", lambda e: e.affine_select(out=identf[:], in_=identf[:], pattern=[[1, 128]],
                                                 compare_op=ALU.is_equal, fill=0.0, base=0, channel_multiplier=-1),
             reads=[b_identf], writes=[b_identf])
        S.op("vector", lambda e: e.tensor_copy(out=ident[:], in_=identf[:]), reads=[b_identf], writes=[b_ident])
        S.op("gpsimd", lambda e: e.memset(identf[:], 1.0), reads=[b_identf], writes=[b_identf])
        S.op("gpsimd", lambda e: e.affine_select(out=identf[:], in_=identf[:], pattern=[[1, 128]],
                                                 compare_op=ALU.is_ge, fill=0.0, base=0, channel_multiplier=-1),
             reads=[b_identf], writes=[b_identf])
        S.op("vector", lambda e: e.tensor_copy(out=tri[:], in_=identf[:]), reads=[b_identf], writes=[b_tri])
        S.op("vector", lambda e: e.tensor_scalar(out=nbf[:], in0=par[0:16, PC_BF:PC_BF + 2], scalar1=-1.0, scalar2=None,
                                                 op0=ALU.mult), reads=[b_par], writes=[b_nbf])

        def rms_stats():
            for t in range(NT):
                S.op("scalar", lambda e, t=t: e.activation(out=junk[:], in_=X[:, t, :], func=AF.Square,
                                                           accum_out=ss[:, t:t + 1]),
                     reads=[bX[t]], writes=[b_junk, b_ss] + b_ssg)
            S.op("vector", lambda e: e.tensor_scalar(out=ms[:], in0=ss[:], scalar1=1.0 / D, scalar2=EPS,
                                                     op0=ALU.mult, op1=ALU.add), reads=[b_ss], writes=[b_ms] + b_msg)
            S.op("scalar", lambda e: e.activation(out=ms[:], in_=ms[:], func=AF.Sqrt), reads=[b_ms], writes=[b_ms])
            S.op("vector", lambda e: e.reciprocal(out=rstd[:], in_=ms[:]), reads=[b_ms], writes=[b_rstd] + b_rstdg)

        def norm_phase(gcol):
            rms_stats()
            for t in range(NT):
                xi = nxt("xn", 2)
                ti = nxt("T", 2)
                S.op("scalar", lambda e, t=t, xi=xi: e.activation(out=XN[xi][:], in_=X[:, t, :], func=AF.Copy,
                                                                  scale=rstd[:, t:t + 1]),
                     reads=[bX[t], b_rstd], writes=[b_XN[xi]])
                for c in range(8):
                    S.op("tensor", lambda e, c=c, xi=xi, ti=ti: e.transpose(
                        out=psTv[ti][:, c * 128:(c + 1) * 128], in_=XN[xi][:, c * 128:(c + 1) * 128], identity=ident[:]),
                        reads=[b_XN[xi], b_ident], writes=[b_psT[ti]])
                S.op("vector", lambda e, t=t, ti=ti: e.tensor_tensor(
                    out=hT[:, :, t * 128:(t + 1) * 128],
                    in0=psTv[ti].rearrange("p (c t) -> p c t", c=8),
                    in1=par[:, gcol:gcol + 8].unsqueeze(2).broadcast_to([128, 8, 128]), op=ALU.mult),
                    reads=[b_psT[ti], b_par], writes=[b_hT[t // 4]])

        def norm_group_stats(g):
            for t in range(4 * g, 4 * g + 4):
                S.op("scalar", lambda e, t=t: e.activation(out=junk[:], in_=X[:, t, :], func=AF.Square,
                                                           accum_out=ss[:, t:t + 1]),
                     reads=[bX[t]], writes=[b_junk, b_ssg[g]])
            S.op("vector", lambda e: e.tensor_scalar(out=ms[:, 4 * g:4 * g + 4], in0=ss[:, 4 * g:4 * g + 4],
                                                     scalar1=1.0 / D, scalar2=EPS, op0=ALU.mult, op1=ALU.add),
                 reads=[b_ssg[g]], writes=[b_msg[g]])
            S.op("scalar", lambda e: e.activation(out=ms[:, 4 * g:4 * g + 4], in_=ms[:, 4 * g:4 * g + 4], func=AF.Sqrt),
                 reads=[b_msg[g]], writes=[b_msg[g]])
            S.op("vector", lambda e: e.reciprocal(out=rstd[:, 4 * g:4 * g + 4], in_=ms[:, 4 * g:4 * g + 4]),
                 reads=[b_msg[g]], writes=[b_rstdg[g]])

        def norm_group_tiles(g, gcol):
            for t in range(4 * g, 4 * g + 4):
                xi = nxt("xn", 2)
                ti = nxt("T", 2)
                S.op("scalar", lambda e, t=t, xi=xi: e.activation(out=XN[xi][:], in_=X[:, t, :], func=AF.Copy,
                                                                  scale=rstd[:, t:t + 1]),
                     reads=[bX[t], b_rstdg[g]], writes=[b_XN[xi]])
                for c in range(8):
                    S.op("tensor", lambda e, c=c, xi=xi, ti=ti: e.transpose(
                        out=psTv[ti][:, c * 128:(c + 1) * 128], in_=XN[xi][:, c * 128:(c + 1) * 128], identity=ident[:]),
                        reads=[b_XN[xi], b_ident], writes=[b_psT[ti]])
                S.op("vector", lambda e, t=t, ti=ti: e.tensor_tensor(
                    out=hT[:, :, t * 128:(t + 1) * 128],
                    in0=psTv[ti].rearrange("p (c t) -> p c t", c=8),
                    in1=par[:, gcol:gcol + 8].unsqueeze(2).broadcast_to([128, 8, 128]), op=ALU.mult),
                    reads=[b_psT[ti], b_par], writes=[b_hT[t // 4]])

        def mm(out_ap, lhsT, rhs, start, stop, reads, wbuf):
            S.op("tensor", lambda e: e.matmul(out_ap, lhsT=lhsT, rhs=rhs, start=start, stop=stop),
                 reads=reads, writes=[wbuf])

        def add_to_X_split(t, half, bk, k):
            if k % 2 == 0:
                add_to_X(t, half, bk)
                return
            ti = nxt("tmp", 3)
            S.op("scalar", lambda e: e.activation(out=TMP[ti][:], in_=psF[:, bk, :], func=AF.Copy),
                 reads=[b_psF[bk]], writes=[b_TMP[ti]])
            S.op("gpsimd", lambda e: e.tensor_tensor(out=X[:, t, 512 * half:512 * half + 512], in0=TMP[ti][:],
                                                     in1=X[:, t, 512 * half:512 * half + 512], op=ALU.add),
                 reads=[b_TMP[ti], bX[t]], writes=[bX[t]])

        def add_to_X(t, half, bk):
            S.op("vector", lambda e: e.tensor_tensor(out=X[:, t, 512 * half:512 * half + 512], in0=psF[:, bk, :],
                                                     in1=X[:, t, 512 * half:512 * half + 512], op=ALU.add),
                 reads=[b_psF[bk], bX[t]], writes=[bX[t]])

        def swa_loads(a, jj):
            off, b, over = arena_alloc(5120, "swa%d_%d" % (a, jj))
            wq = WA[:, off:off + 2048].rearrange("p (c n) -> p c n", c=8)
            wkv = WA[:, off + 2048:off + 3072].rearrange("p (c n) -> p c n", c=8)
            wo = WA[:, off + 3072:off + 5120].rearrange("p (c n) -> p c n", c=2)
            src = wqkv_d[a].rearrange("(c p) n -> p c n", p=128)
            wload(wq, src[:, :, 256 * jj:256 * jj + 256], b, over, True)
            wload(wkv[:, :, 0:64], src[:, :, 1024 + 64 * jj:1024 + 64 * jj + 64], b, over, False)
            wload(wkv[:, :, 64:128], src[:, :, 1280 + 64 * jj:1280 + 64 * jj + 64], b, over, False)
            wload(wo, woa_d[a][256 * jj:256 * jj + 256, :].rearrange("(c p) n -> p c n", p=128), b, over, False)
            return dict(wq=wq, wkv=wkv, wo=wo, b=b)

        def swa_batch(a, jj, W):
            wq, wkv, wo, bw = W["wq"], W["wkv"], W["wo"], W["b"]
            QT = R12[:].rearrange("p (h t) -> p h t", h=4)
            KT = R3[:, 0:2048]
            VP = R3[:, 2048:4096].rearrange("p (t n) -> p t n", t=16)
            OT = R4[:].rearrange("p (c t) -> p c t", c=2)
            bi = 0
            S.dma("sync", lambda e: e.dma_start(out=BIA[bi][:], in_=bias_d[:, jj].rearrange("v k f -> k v f")),
                  "bia%d" % bi, writes=[b_BIA[bi]])
            S.dma("sync", lambda e: e.dma_start(
                out=bvrep[:], in_=bqkv_d[a][1280 + 64 * jj:1280 + 64 * jj + 64].partition_broadcast(128)),
                "bv", writes=[b_bv])
            S.op("gpsimd", lambda e: e.memset(VP[:, :, 64:128], 1.0), writes=[b_R3v, b_R3])
            for cc in range(2):
                col = PC_BQ + a * 8 + jj * 2 + cc
                for g in range(4):
                    bk = bankA()
                    for c in range(8):
                        mm(psF[:, bk, :], wq[:, c, 128 * cc:128 * cc + 128], hT[:, c, 512 * g:512 * g + 512],
                           c == 0, c == 7, [bw, b_hT[g]], b_psF[bk])
                    bR = b_R1 if cc == 0 else b_R2
                    S.op("scalar", lambda e, bk=bk, cc=cc, g=g, col=col: e.activation(
                        out=QT[0:64, 2 * cc, 512 * g:512 * g + 512], in_=psF[0:64, bk, :], func=AF.Identity,
                        bias=par[0:64, col:col + 1]), reads=[b_psF[bk], b_par], writes=[bR])
                    S.op("scalar", lambda e, bk=bk, cc=cc, g=g, col=col: e.activation(
                        out=QT[0:64, 2 * cc + 1, 512 * g:512 * g + 512], in_=psF[64:128, bk, :], func=AF.Identity,
                        bias=par[64:128, col:col + 1]), reads=[b_psF[bk], b_par], writes=[bR])
            colk = PC_BK + a * 4 + jj
            for g in range(4):
                bk = bankA()
                for c in range(8):
                    mm(psF[0:64, bk, :], wkv[:, c, 0:64], hT[:, c, 512 * g:512 * g + 512], c == 0, c == 7,
                       [bw, b_hT[g]], b_psF[bk])
                S.op("scalar", lambda e, bk=bk, g=g: e.activation(
                    out=KT[0:64, 512 * g:512 * g + 512], in_=psF[0:64, bk, :], func=AF.Identity,
                    bias=par[0:64, colk:colk + 1]), reads=[b_psF[bk], b_par], writes=[b_R3])
            for t8 in range(2):
                bk = bankA()
                for ti in range(8):
                    t = t8 * 8 + ti
                    for c in range(8):
                        mm(psF[:, bk, ti * 64:ti * 64 + 64], hT[:, c, 128 * t:128 * t + 128], wkv[:, c, 64:128],
                           c == 0, c == 7, [bw, b_hT[t // 4]], b_psF[bk])
                S.op("vector", lambda e, bk=bk, t8=t8: e.tensor_tensor(
                    out=VP[:, 8 * t8:8 * t8 + 8, 0:64],
                    in0=psF[:, bk, :].rearrange("p (t n) -> p t n", t=8),
                    in1=bvrep[:].unsqueeze(1).broadcast_to([128, 8, 64]), op=ALU.add),
                    reads=[b_psF[bk], b_bv], writes=[b_R3v])
            blocks = []
            for n in range(16):
                kbs = [n - 1, n] if n > 0 else [n]
                for idx, kb in enumerate(kbs):
                    blocks.append((n, idx, kb, len(kbs)))
            state = {}

            for v_ in range(2):
                S.op("scalar", lambda e, v_=v_: e.activation(out=BIA[bi][:, v_, :], in_=BIA[bi][:, v_, :], func=AF.Exp),
                     reads=[b_BIA[bi]], writes=[b_BIA[bi]])

            def stage1(bi_):
                n, idx, kb, nk = blocks[bi_]
                v = 0 if kb == n else 1
                sb = bankA()
                mm(psF[:, sb, :], KT[0:64, 128 * kb:128 * kb + 128], QT[0:64, :, 128 * n:128 * n + 128],
                   True, True, [b_R1, b_R2, b_R3], b_psF[sb])
                ti = nxt("tmp", 3)
                pi = nxt("pt", 4)
                S.op("scalar", lambda e, sb=sb, ti=ti: e.activation(out=TMP[ti][:], in_=psF[:, sb, :], func=AF.Exp,
                                                                    scale=0.125),
                     reads=[b_psF[sb]], writes=[b_TMP[ti]])
                S.op("gpsimd",
                     lambda e, ti=ti, pi=pi, v=v: e.tensor_tensor(out=PT[pi][:], in0=TMP[ti][:],
                                                                  in1=BIA[bi][:, v, :], op=ALU.mult),
                     reads=[b_TMP[ti], b_BIA[bi]], writes=[b_PT[pi]])
                state[bi_] = pi

            def stage2(bi_):
                n, idx, kb, nk = blocks[bi_]
                pi = state.pop(bi_)
                if idx == 0:
                    state["acc"] = bankB()
                acc = state["acc"]
                mm(psF[:, acc, :], VP[:, kb, :], PT[pi][:], idx == 0, idx == nk - 1,
                   [b_R3v, b_PT[pi]], b_psF[acc])
                if idx != nk - 1:
                    return
                state[("n", n)] = acc

            def stage3(n):
                acc = state.pop(("n", n))
                S.op("vector", lambda e: e.tensor_tensor(
                    out=DEN[0:64, :].rearrange("p (h q) -> p h q", h=4),
                    in0=psF[64:128, acc, :].rearrange("p (h q) -> p h q", h=4),
                    in1=esink[64:128, 4 * jj:4 * jj + 4].unsqueeze(2).broadcast_to([64, 4, 128]), op=ALU.add),
                    reads=[b_psF[acc], b_esink], writes=[b_DEN])
                S.op("scalar", lambda e: e.activation(out=DEN[0:64, :], in_=DEN[0:64, :], func=AF.Ln),
                     reads=[b_DEN], writes=[b_DEN])
                S.op("scalar", lambda e: e.activation(out=DEN[0:64, :], in_=DEN[0:64, :], func=AF.Exp, scale=-1.0),
                     reads=[b_DEN], writes=[b_DEN])
                state[("m", n)] = acc

            def stage4(n):
                acc = state.pop(("m", n))
                for two in range(2):
                    S.op("vector", lambda e, two=two: e.tensor_tensor(
                        out=OT[64 * two:64 * two + 64, :, 128 * n:128 * n + 128],
                        in0=psF[0:64, acc, :].rearrange("p (c two q) -> p c two q", c=2, two=2)[:, :, two, :],
                        in1=DEN[0:64, :].rearrange("p (c two q) -> p c two q", c=2, two=2)[:, :, two, :],
                        op=ALU.mult), reads=[b_psF[acc], b_DEN], writes=[b_R4, b_R4b])

            LOOK = 2
            pend3, pend4 = [], []
            for step in range(len(blocks) + LOOK + 2):
                if pend4:
                    stage4(pend4.pop(0))
                if pend3:
                    n_ = pend3.pop(0)
                    stage3(n_)
                    pend4.append(n_)
                if step < len(blocks):
                    stage1(step)
                b2 = step - LOOK
                if 0 <= b2 < len(blocks):
                    stage2(b2)
                    n, idx, kb, nk = blocks[b2]
                    if idx == nk - 1:
                        pend3.append(n)
            assert not pend3 and not pend4

            for t in range(NT):
                for half in range(2):
                    bk = bankA()
                    for c in range(2):
                        mm(psF[:, bk, :], OT[:, c, 128 * t:128 * t + 128], wo[:, c, 512 * half:512 * half + 512],
                           c == 0, c == 1, [b_R4, b_R4b, bw], b_psF[bk])
                    add_to_X_split(t, half, bk, 2 * t + half)

        def swa_layer(a, W_first, prefetch_next):
            QT = R12[:].rearrange("p (h t) -> p h t", h=4)
            KT = R3[:, 0:2048]
            VP = R3[:, 2048:4096].rearrange("p (t n) -> p t n", t=16)
            OT = R4[:].rearrange("p (c t) -> p c t", c=2)
            bQ = [Buf("swaQ%d" % g) for g in range(4)]
            bK = [Buf("swaK%d" % g) for g in range(4)]
            bV = [Buf("swaV%d" % g) for g in range(4)]
            coarse = [b_R1, b_R2, b_R3, b_R3v]
            S.op("gpsimd", lambda e: e.memset(VP[:, :, 64:128], 1.0), writes=coarse + bQ + bK + bV + [b_DEN] + b_PTx)
            Ws = {0: W_first}
            bi = 0

            def jobs(jj, g):
                W = Ws[jj]
                wq, wkv, bw = W["wq"], W["wkv"], W["b"]
                colk = PC_BK + a * 4 + jj
                out = []

                def q_job(cc):
                    col = PC_BQ + a * 8 + jj * 2 + cc
                    bk = bankA()
                    for c in range(8):
                        mm(psF[:, bk, :], wq[:, c, 128 * cc:128 * cc + 128], hT[:, c, 512 * g:512 * g + 512],
                           c == 0, c == 7, [bw, b_hT[g]], b_psF[bk])
                    S.op("scalar", lambda e: e.activation(
                        out=QT[0:64, 2 * cc, 512 * g:512 * g + 512], in_=psF[0:64, bk, :], func=AF.Identity,
                        bias=par[0:64, col:col + 1]), reads=[b_psF[bk], b_par], writes=[bQ[g]])
                    S.op("scalar", lambda e: e.activation(
                        out=QT[0:64, 2 * cc + 1, 512 * g:512 * g + 512], in_=psF[64:128, bk, :], func=AF.Identity,
                        bias=par[64:128, col:col + 1]), reads=[b_psF[bk], b_par], writes=[bQ[g]])

                def k_job():
                    bk = bankA()
                    for c in range(8):
                        mm(psF[0:64, bk, :], wkv[:, c, 0:64], hT[:, c, 512 * g:512 * g + 512], c == 0, c == 7,
                           [bw, b_hT[g]], b_psF[bk])
                    S.op("scalar", lambda e: e.activation(
                        out=KT[0:64, 512 * g:512 * g + 512], in_=psF[0:64, bk, :], func=AF.Identity,
                        bias=par[0:64, colk:colk + 1]), reads=[b_psF[bk], b_par], writes=[bK[g]])

                def v_job():
                    if g == 0:
                        S.dma("sync", lambda e: e.dma_start(
                            out=bvrep[:], in_=bqkv_d[a][1280 + 64 * jj:1280 + 64 * jj + 64].partition_broadcast(128)),
                            "bv", writes=[b_bv])
                    bk = bankA()
                    for ti in range(4):
                        t = 4 * g + ti
                        for c in range(8):
                            mm(psF[:, bk, ti * 64:ti * 64 + 64], hT[:, c, 128 * t:128 * t + 128], wkv[:, c, 64:128],
                               c == 0, c == 7, [bw, b_hT[g]], b_psF[bk])
                    S.op("vector", lambda e: e.tensor_tensor(
                        out=VP[:, 4 * g:4 * g + 4, 0:64],
                        in0=psF[:, bk, 0:256].rearrange("p (t n) -> p t n", t=4),
                        in1=bvrep[:].unsqueeze(1).broadcast_to([128, 4, 64]), op=ALU.add),
                        reads=[b_psF[bk], b_bv], writes=[bV[g]])

                return [lambda: q_job(0), lambda: q_job(1), k_job, v_job]

            for j0 in jobs(0, 0):
                j0()
            def batch(jj):
                W = Ws[jj]
                wo, bw = W["wo"], W["b"]
                if jj < 3:
                    Ws[jj + 1] = swa_loads(a, jj + 1)
                else:
                    prefetch_next()
                S.dma("sync", lambda e, jj=jj: e.dma_start(out=BIA[bi][:], in_=bias_d[:, jj].rearrange("v k f -> k v f")),
                      "bia%d" % bi, writes=[b_BIA[bi]])
                for v_ in range(2):
                    S.op("scalar", lambda e, v_=v_: e.activation(out=BIA[bi][:, v_, :], in_=BIA[bi][:, v_, :], func=AF.Exp),
                         reads=[b_BIA[bi]], writes=[b_BIA[bi]])
                blocks = []
                idx0 = {}
                for n in range(16):
                    kbs = [n - 1, n] if n > 0 else [n]
                    if n % 4 == 0:
                        idx0[n // 4] = len(blocks)
                    for idx, kb in enumerate(kbs):
                        blocks.append((n, idx, kb, len(kbs)))
                idx0[4] = len(blocks)
                jsched = {}
                for g in range(4):
                    if g < 3:
                        js = jobs(jj, g + 1)
                    elif jj < 3:
                        js = jobs(jj + 1, 0)
                    else:
                        js = []
                    for k, jb in enumerate(js):
                        st_ = min(idx0[g] + 1 + 2 * k, idx0[g + 1] - 1)
                        jsched.setdefault(st_, []).append(jb)
                state = {}

                def stage1(bi_):
                    n, idx, kb, nk = blocks[bi_]
                    v = 0 if kb == n else 1
                    sb = bankA()
                    mm(psF[:, sb, :], KT[0:64, 128 * kb:128 * kb + 128], QT[0:64, :, 128 * n:128 * n + 128],
                       True, True, [bQ[n // 4], bK[kb // 4]] + coarse, b_psF[sb])
                    ti = nxt("tmp", 3)
                    pi = nxt("pt", 4)
                    S.op("scalar", lambda e: e.activation(out=TMP[ti][:], in_=psF[:, sb, :], func=AF.Exp, scale=0.125),
                         reads=[b_psF[sb]], writes=[b_TMP[ti]])
                    S.op("gpsimd", lambda e: e.tensor_tensor(out=PT[pi][:], in0=TMP[ti][:], in1=BIA[bi][:, v, :],
                                                             op=ALU.mult),
                         reads=[b_TMP[ti], b_BIA[bi]], writes=[b_PT[pi]])
                    state[bi_] = pi

                def stage2(bi_):
                    n, idx, kb, nk = blocks[bi_]
                    pi = state.pop(bi_)
                    if idx == 0:
                        state["acc"] = bankB()
                    acc = state["acc"]
                    mm(psF[:, acc, :], VP[:, kb, :], PT[pi][:], idx == 0, idx == nk - 1,
                       [bV[kb // 4], b_R3v, b_PT[pi]], b_psF[acc])
                    if idx == nk - 1:
                        state[("n", n)] = acc

                def stage3(n):
                    acc = state.pop(("n", n))
                    S.op("vector", lambda e: e.tensor_tensor(
                        out=DEN[0:64, :].rearrange("p (h q) -> p h q", h=4),
                        in0=psF[64:128, acc, :].rearrange("p (h q) -> p h q", h=4),
                        in1=esink[64:128, 4 * jj:4 * jj + 4].unsqueeze(2).broadcast_to([64, 4, 128]), op=ALU.add),
                        reads=[b_psF[acc], b_esink], writes=[b_DEN])
                    S.op("scalar", lambda e: e.activation(out=DEN[0:64, :], in_=DEN[0:64, :], func=AF.Ln),
                         reads=[b_DEN], writes=[b_DEN])
                    S.op("scalar", lambda e: e.activation(out=DEN[0:64, :], in_=DEN[0:64, :], func=AF.Exp, scale=-1.0),
                         reads=[b_DEN], writes=[b_DEN])
                    state[("m", n)] = acc

                def stage4(n):
                    acc = state.pop(("m", n))
                    for two in range(2):
                        S.op("vector", lambda e, two=two: e.tensor_tensor(
                            out=OT[64 * two:64 * two + 64, :, 128 * n:128 * n + 128],
                            in0=psF[0:64, acc, :].rearrange("p (c two q) -> p c two q", c=2, two=2)[:, :, two, :],
                            in1=DEN[0:64, :].rearrange("p (c two q) -> p c two q", c=2, two=2)[:, :, two, :],
                            op=ALU.mult), reads=[b_psF[acc], b_DEN], writes=[b_R4, b_R4b])

                LOOK = 3
                pend3, pend4 = [], []
                for step in range(len(blocks) + LOOK + 2):
                    if pend4:
                        stage4(pend4.pop(0))
                    if pend3:
                        n_ = pend3.pop(0)
                        stage3(n_)
                        pend4.append(n_)
                    if step < len(blocks):
                        stage1(step)
                    b2 = step - LOOK
                    if 0 <= b2 < len(blocks):
                        stage2(b2)
                        n, idx, kb, nk = blocks[b2]
                        if idx == nk - 1:
                            pend3.append(n)
                    for jb in jsched.pop(step, []):
                        jb()
                assert not pend3 and not pend4 and not jsched
                for t in range(NT):
                    for half in range(2):
                        bk = bankA()
                        for c in range(2):
                            mm(psF[:, bk, :], OT[:, c, 128 * t:128 * t + 128], wo[:, c, 512 * half:512 * half + 512],
                               c == 0, c == 1, [b_R4, b_R4b, bw], b_psF[bk])
                        add_to_X_split(t, half, bk, 2 * t + half)
                arena_done.add(bw.name)

            for jj_ in range(4):
                batch(jj_)

        def swa_layer_pre(a):
            S.dma("sync", lambda e: e.dma_start(out=sinkraw[:], in_=sinks_d[a].partition_broadcast(128)),
                  "sink", writes=[b_sinkraw])
            S.op("scalar", lambda e: e.activation(out=esink[:], in_=sinkraw[:], func=AF.Exp),
                 reads=[b_sinkraw], writes=[b_esink])
            S.dma("sync", lambda e: e.dma_start(out=borep[:], in_=boa_d[a].partition_broadcast(128)),
                  "borep", writes=[b_borep])
            for t in range(NT):
                S.op("gpsimd", lambda e, t=t: e.tensor_tensor(out=X[:, t, :], in0=X[:, t, :], in1=borep[:], op=ALU.add),
                     reads=[bX[t], b_borep], writes=[bX[t]])

        def fox_pre_loads(a):
            off, b, over = arena_alloc(128, "wf%d" % a)
            wf = WA[:, off:off + 128].rearrange("p (c n) -> p c n", c=8)
            wload(wf, wqkvf_d[a].rearrange("(c p) n -> p c n", p=128)[:, :, 3072:3088], b, over, True)
            return dict(wf=wf, b=b)

        def fox_pre(a, W):
            wf, bw = W["wf"], W["b"]
            VP = R3[:].rearrange("p (t s n) -> p t s n", t=16, s=2)
            S.op("gpsimd", lambda e: e.memset(ones16, 1.0), writes=[b_borep, b_DEN] + b_PTx)
            S.op("gpsimd", lambda e: e.memset(VP[:, :, 0, 64:128], 1.0), writes=[b_R3v, b_R3])
            S.op("gpsimd", lambda e: e.memset(VP[:, :, 1, 0:64], 1.0), reads=[b_R3v], writes=[b_R3v])
            for g in range(4):
                bk = bankA()
                for c in range(8):
                    mm(psF[0:16, bk, :], wf[:, c, :], hT[:, c, 512 * g:512 * g + 512], c == 0, c == 7,
                       [bw, b_hT[g]], b_psF[bk])
                t1 = nxt("tmp", 3)
                S.op("scalar", lambda e, bk=bk, t1=t1: e.activation(out=TMP[t1][0:16, :], in_=psF[0:16, bk, :],
                                                                    func=AF.Exp, scale=-1.0, bias=nbf[:, a:a + 1]),
                     reads=[b_psF[bk], b_nbf], writes=[b_TMP[t1]])
                S.op("scalar", lambda e, t1=t1: e.activation(out=TMP[t1][0:16, :], in_=TMP[t1][0:16, :], func=AF.Ln,
                                                             bias=1.0), reads=[b_TMP[t1]], writes=[b_TMP[t1]])
                t2 = nxt("tmp", 3)
                if g == 0:
                    S.op("vector", lambda e, t1=t1, t2=t2: e.tensor_tensor_scan(
                        out=TMP[t2][0:16, :], data0=ones16, data1=TMP[t1][0:16, :], initial=0.0,
                        op0=ALU.mult, op1=ALU.subtract), reads=[b_borep, b_TMP[t1]], writes=[b_TMP[t2]])
                else:
                    S.op("vector", lambda e, t1=t1, t2=t2, g=g: e.tensor_tensor_scan(
                        out=TMP[t2][0:16, :], data0=ones16, data1=TMP[t1][0:16, :], initial=carry[:, g - 1:g],
                        op0=ALU.mult, op1=ALU.subtract), reads=[b_borep, b_TMP[t1], b_carry], writes=[b_TMP[t2]])
                S.op("vector", lambda e, t2=t2, g=g: e.tensor_copy(out=carry[:, g:g + 1], in_=TMP[t2][0:16, 511:512]),
                     reads=[b_TMP[t2]], writes=[b_carry])
                cs = slice(512 * g, 512 * g + 512)
                S.op("vector", lambda e, t2=t2, cs=cs: e.tensor_scalar(out=AUG[0:16, cs], in0=TMP[t2][0:16, :], scalar1=8.0,
                                                                       scalar2=None, op0=ALU.mult),
                     reads=[b_TMP[t2]], writes=[b_AUG])
                S.op("vector", lambda e, t1=t1, t2=t2, cs=cs: e.scalar_tensor_tensor(
                    out=TMP[t1][0:16, :], in0=TMP[t2][0:16, :], scalar=8.0, in1=AUG[0:16, cs], op0=ALU.mult,
                    op1=ALU.subtract), reads=[b_TMP[t2], b_AUG], writes=[b_TMP[t1]])
                S.op("vector", lambda e, t1=t1: e.tensor_copy(out=midS[:], in_=TMP[t1][0:16, :]),
                     reads=[b_TMP[t1]], writes=[b_midS])
                S.op("vector", lambda e, cs=cs: e.tensor_copy(out=AUG[32:48, cs], in_=midS[:]),
                     reads=[b_midS], writes=[b_AUG])
                S.op("vector", lambda e, t1=t1, t2=t2: e.tensor_tensor(out=TMP[t2][0:16, :], in0=TMP[t1][0:16, :],
                                                                       in1=midS[:], op=ALU.subtract),
                     reads=[b_TMP[t1], b_midS], writes=[b_TMP[t2]])
                S.op("vector", lambda e, t2=t2, cs=cs: e.tensor_copy(out=AUG[64:80, cs], in_=TMP[t2][0:16, :]),
                     reads=[b_TMP[t2]], writes=[b_AUG])

        def fox_loads(a, b_):
            off, b, over = arena_alloc(4096, "fox%d_%d" % (a, b_))
            wq = WA[:, off:off + 1024].rearrange("p (c n) -> p c n", c=8)
            wk = WA[:, off + 1024:off + 2048].rearrange("p (c n) -> p c n", c=8)
            wv = WA[:, off + 2048:off + 3072].rearrange("p (c n) -> p c n", c=8)
            wo = WA[:, off + 3072:off + 4096]
            src = wqkvf_d[a].rearrange("(c p) n -> p c n", p=128)
            wload(wq, src[:, :, 128 * b_:128 * b_ + 128], b, over, True)
            wload(wk, src[:, :, 1024 + 128 * b_:1024 + 128 * b_ + 128], b, over, False)
            wload(wv, src[:, :, 2048 + 128 * b_:2048 + 128 * b_ + 128], b, over, False)
            wload(wo, wob_d[a][128 * b_:128 * b_ + 128, :], b, over, False)
            return dict(wq=wq, wk=wk, wv=wv, wo=wo, b=b)

        def fox_batch(a, b_, W):
            wq, wk, wv, wo, bw = W["wq"], W["wk"], W["wv"], W["wo"], W["b"]
            QA = R12[:, 0:4096].rearrange("p (s t) -> p s t", s=2)
            KA = R12[:, 4096:8192].rearrange("p (s t) -> p s t", s=2)
            VP = R3[:].rearrange("p (t s n) -> p t s n", t=16, s=2)
            OT = R4[:, 2048 * (b_ % 2):2048 * (b_ % 2) + 2048]
            bOT = b_R4 if b_ % 2 == 0 else b_R4b
            extra = [b_R1, b_R2] if b_ == 0 else []
            for s in range(2):
                S.op("gpsimd", lambda e, s=s: e.memset(QA[64:70, s, :], -1.0), writes=[b_QAaug[s]] + extra)
                S.op("gpsimd", lambda e, s=s: e.memset(KA[64:70, s, :], 1.0), writes=[b_KAaug[s]] + extra)
            for s in range(2):
                h = 2 * b_ + s
                src = bass.AP(AUG, h * S_LEN, [[32 * S_LEN, 3], [1, S_LEN]])
                S.dma("sync", lambda e, s=s, src=src: e.dma_start(out=QA[64:67, s, :], in_=src), "augq%d" % s,
                      reads=[b_AUG], writes=[b_QAaug[s]])
                S.dma("sync", lambda e, s=s, src=src: e.dma_start(out=KA[67:70, s, :], in_=src), "augk%d" % s,
                      reads=[b_AUG], writes=[b_KAaug[s]])
            for (w_, dst, bR) in ((wq, QA, b_R1), (wk, KA, b_R2)):
                for g in range(4):
                    bk = bankA()
                    for c in range(8):
                        mm(psF[:, bk, :], w_[:, c, :], hT[:, c, 512 * g:512 * g + 512], c == 0, c == 7,
                           [bw, b_hT[g]], b_psF[bk])
                    S.op("scalar", lambda e, bk=bk, g=g, dst=dst: e.activation(
                        out=dst[0:64, 0, 512 * g:512 * g + 512], in_=psF[0:64, bk, :], func=AF.Copy),
                        reads=[b_psF[bk]], writes=[bR])
                    S.op("vector", lambda e, bk=bk, g=g, dst=dst: e.tensor_copy(
                        out=dst[0:64, 1, 512 * g:512 * g + 512], in_=psF[64:128, bk, :]),
                        reads=[b_psF[bk]], writes=[bR])
            for g in range(4):
                bk = bankA()
                for c in range(8):
                    mm(psF[:, bk, :], wv[:, c, :], hT[:, c, 512 * g:512 * g + 512], c == 0, c == 7,
                       [bw, b_hT[g]], b_psF[bk])
                pi = nxt("pt", 4)
                S.op("scalar", lambda e, bk=bk, pi=pi: e.activation(out=PT[pi][:], in_=psF[:, bk, :], func=AF.Copy),
                     reads=[b_psF[bk]], writes=[b_PT[pi]])
                ti = nxt("T", 2)
                for i4 in range(4):
                    S.op("tensor", lambda e, pi=pi, ti=ti, i4=i4: e.transpose(
                        out=psTv[ti][:, 128 * i4:128 * i4 + 128], in_=PT[pi][:, 128 * i4:128 * i4 + 128],
                        identity=ident[:]), reads=[b_PT[pi], b_ident], writes=[b_psT[ti]])
                S.op("vector", lambda e, ti=ti, g=g: e.tensor_copy(
                    out=VP[:, 4 * g:4 * g + 4, 0, 0:64],
                    in_=psTv[ti][:, 0:512].rearrange("p (t n) -> p t n", t=4)[:, :, 0:64]),
                    reads=[b_psT[ti]], writes=[b_R3v])
                S.op("vector", lambda e, ti=ti, g=g: e.tensor_copy(
                    out=VP[:, 4 * g:4 * g + 4, 1, 64:128],
                    in_=psTv[ti][:, 0:512].rearrange("p (t n) -> p t n", t=4)[:, :, 64:128]),
                    reads=[b_psT[ti]], writes=[b_R3v])
            blocks = []
            for s_ in range(2):
                for G in range(4):
                    nkb = 4 * G + 4
                    for j in range(nkb):
                        blocks.append((s_, G, j, nkb))
            state = {}

            def stage1(bi_):
                s_, G, j, nkb = blocks[bi_]
                i = j - 4 * G
                qoff = 128 * i if i > 0 else 0
                ncols = 512 - qoff
                q0 = 512 * G + qoff
                sb = bankA()
                mm(psF[:, sb, 0:ncols], KA[0:70, s_, 128 * j:128 * j + 128], QA[0:70, s_, q0:q0 + ncols],
                   True, True, [b_R1, b_R2, b_QAaug[s_], b_KAaug[s_]], b_psF[sb])
                pi = nxt("ptf", 6)
                S.op("scalar", lambda e, sb=sb, pi=pi, ncols=ncols: e.activation(
                    out=PTF[pi][:, 0:ncols], in_=psF[:, sb, 0:ncols], func=AF.Exp, scale=0.125),
                    reads=[b_psF[sb]], writes=[b_PTF[pi]])
                if i >= 0:
                    S.op("gpsimd", lambda e, pi=pi: e.tensor_tensor(out=PTF[pi][:, 0:128], in0=PTF[pi][:, 0:128],
                                                                    in1=tri[:], op=ALU.mult),
                         reads=[b_PTF[pi], b_tri], writes=[b_PTF[pi]])
                state[bi_] = (pi, qoff, ncols)

            def stage2(bi_):
                s_, G, j, nkb = blocks[bi_]
                pi, qoff, ncols = state.pop(bi_)
                if j == 0:
                    state["acc"] = bankB()
                acc = state["acc"]
                mm(psF[:, acc, qoff:512], VP[:, j, s_, :], PTF[pi][:, 0:ncols], j == 0, j == nkb - 1,
                   [b_R3v, b_PTF[pi]], b_psF[acc])
                if j != nkb - 1:
                    return
                ti = nxt("tmp", 3)
                lo, hi = (0, 64) if s_ == 0 else (64, 128)
                slo, shi = (64, 128) if s_ == 0 else (0, 64)
                S.op("vector", lambda e, acc=acc, ti=ti, lo=lo, hi=hi, slo=slo, shi=shi: e.reciprocal(
                    out=TMP[ti][lo:hi, :], in_=psF[slo:shi, acc, :]), reads=[b_psF[acc]], writes=[b_TMP[ti]])
                S.op("vector", lambda e, acc=acc, ti=ti, lo=lo, hi=hi, G=G: e.tensor_tensor(
                    out=OT[lo:hi, 512 * G:512 * G + 512], in0=psF[lo:hi, acc, :], in1=TMP[ti][lo:hi, :],
                    op=ALU.mult), reads=[b_psF[acc], b_TMP[ti]], writes=[bOT])

            LOOK = 5
            for step in range(len(blocks) + LOOK):
                if step < len(blocks):
                    stage1(step)
                if step - LOOK >= 0:
                    stage2(step - LOOK)
            fox_state["wo%d" % (b_ % 2)] = (wo, bw)
            if b_ % 2 == 1:
                OT2 = R4[:].rearrange("p (c t) -> p c t", c=2)
                for t in range(NT):
                    for half in range(2):
                        bk = bankA()
                        for c in range(2):
                            wo_c, bw_c = fox_state["wo%d" % c]
                            mm(psF[:, bk, :], OT2[:, c, 128 * t:128 * t + 128], wo_c[:, 512 * half:512 * half + 512],
                               c == 0, c == 1, [b_R4, b_R4b, bw_c], b_psF[bk])
                        add_to_X_split(t, half, bk, 2 * t + half)

        def mlp_load_up(l, q4):
            off, b, over = arena_alloc(8192, "mup%d_%d" % (l, q4), align=8192)
            wu = WA[:, off:off + 8192].rearrange("p (c n) -> p c n", c=8)
            src = wup_d[l].rearrange("(c p) n -> p c n", p=128)
            wload(wu[:, :, 0:512], src[:, :, 1024 * q4:1024 * q4 + 512], b, over, True)
            wload(wu[:, :, 512:1024], src[:, :, 1024 * q4 + 512:1024 * q4 + 1024], b, over, False)
            return dict(wu=wu, b=b)

        def mlp_load_dn(l, q4):
            off, b, over = arena_alloc(8192, "mdn%d_%d" % (l, q4), align=8192)
            wd = WA[:, off:off + 8192].rearrange("p (f n) -> p f n", f=8)
            src = wdn_d[l][1024 * q4:1024 * q4 + 1024, :].rearrange("(f p) n -> p f n", p=128)
            wload(wd[:, 0:4, :], src[:, 0:4, :], b, over, True)
            wload(wd[:, 4:8, :], src[:, 4:8, :], b, over, False)
            return dict(wd=wd, b=b)

        def mlp_pass(l, q4, W0, W1, hook, next_gcol=None):
            pending_tiles = []
            wu, bwu = W0["wu"], W0["b"]
            wd, bwd = W1["wd"], W1["b"]

            def bufs(g):
                ai = g % 2
                aT = (R12[:, 0:4096] if ai == 0 else R12[:, 4096:8192]).rearrange("p (f t) -> p f t", f=8)
                return aT, (b_R1 if ai == 0 else b_R2)

            def up(g):
                aT, bA = bufs(g)
                for f in range(8):
                    bk = bankA()
                    for c in range(8):
                        mm(psF[:, bk, :], wu[:, c, 128 * f:128 * f + 128], hT[:, c, 512 * g:512 * g + 512],
                           c == 0, c == 7, [bwu, b_hT[g]], b_psF[bk])
                    ti = nxt("tmp", 3)
                    S.op("scalar", lambda e, bk=bk, ti=ti: e.activation(out=TMP[ti][:], in_=psF[:, bk, :], func=AF.Relu),
                         reads=[b_psF[bk]], writes=[b_TMP[ti]])
                    S.op("gpsimd", lambda e, ti=ti, f=f, aT=aT: e.tensor_tensor(out=aT[:, f, :], in0=TMP[ti][:],
                                                                                in1=TMP[ti][:], op=ALU.mult),
                         reads=[b_TMP[ti]], writes=[bA])
                if g == 3:
                    arena_done.add(bwu.name)
                    hook()
                if next_gcol is not None and pending_tiles:
                    norm_group_tiles(pending_tiles.pop(0), next_gcol)

            def down(g):
                aT, bA = bufs(g)
                for tt in range(4):
                    t = 4 * g + tt
                    for half in range(2):
                        bk = bankB()
                        for f in range(8):
                            mm(psF[:, bk, :], aT[:, f, 128 * tt:128 * tt + 128], wd[:, f, 512 * half:512 * half + 512],
                               f == 0, f == 7, [bA, bwd], b_psF[bk])
                        add_to_X(t, half, bk)
                if next_gcol is not None:
                    norm_group_stats(g)
                    pending_tiles.append(g)

            up(0)
            up(1)
            down(0)
            up(2)
            down(1)
            up(3)
            down(2)
            down(3)
            if next_gcol is not None:
                while pending_tiles:
                    norm_group_tiles(pending_tiles.pop(0), next_gcol)
                norm_done.add(next_gcol)

        units = []

        def U(l0, l1, comp, self_pf=False):
            units.append(dict(l0=l0, l1=l1, comp=comp, self_pf=self_pf))

        for l in layers:
            a = l // 2
            if l == layers[0]:
                def first_norm(W0, W1, hook, l=l):
                    gc = PC_GMIX + l * 8
                    norm_group_stats(0)
                    norm_group_stats(1)
                    norm_group_tiles(0, gc)
                    norm_group_stats(2)
                    norm_group_tiles(1, gc)
                    norm_group_stats(3)
                    norm_group_tiles(2, gc)
                    norm_group_tiles(3, gc)
                U(None, None, first_norm)
            else:
                U(None, None, lambda W0, W1, hook, l=l: (None if (PC_GMIX + l * 8) in norm_done else norm_phase(PC_GMIX + l * 8)))
            if l % 2 == 0:
                U(None, None, lambda W0, W1, hook, a=a: swa_layer_pre(a))
                U(lambda a=a: swa_loads(a, 0), None, lambda W0, W1, hook, pf, a=a: swa_layer(a, W0, pf), self_pf=True)
            else:
                U(lambda a=a: fox_pre_loads(a), None, lambda W0, W1, hook, a=a: fox_pre(a, W0))
                for b_ in range(8):
                    U(lambda a=a, b_=b_: fox_loads(a, b_), None, lambda W0, W1, hook, a=a, b_=b_: fox_batch(a, b_, W0))
            U(None, None, lambda W0, W1, hook, l=l: norm_phase(PC_GMLP + l * 8))
            for q4 in range(4):
                U(lambda l=l, q4=q4: mlp_load_up(l, q4), lambda l=l, q4=q4: mlp_load_dn(l, q4),
                  lambda W0, W1, hook, l=l, q4=q4: mlp_pass(
                      l, q4, W0, W1, hook,
                      next_gcol=(PC_GMIX + (l + 1) * 8) if (q4 == 3 and (l + 1) in layers) else None))

        L0, L1 = {}, {}

        def load0(i):
            if i < len(units) and units[i]["l0"] is not None and i not in L0:
                L0[i] = units[i]["l0"]()

        def load1(i):
            if i < len(units) and units[i]["l1"] is not None and i not in L1:
                L1[i] = units[i]["l1"]()

        def next_loader(i):
            j = i + 1
            while j < len(units) and units[j]["l0"] is None:
                j += 1
            return j

        first = 0 if units[0]["l0"] is not None else next_loader(0)
        load0(first)
        load1(first)
        for i, u in enumerate(units):
            if u["l0"] is None:
                u["comp"](None, None, None)
                continue
            load0(i)
            load1(i)
            nj = next_loader(i)
            if not u["self_pf"]:
                load0(nj)
            called = [False]

            def hook(nj=nj, called=called):
                if not called[0]:
                    called[0] = True
                    load1(nj)

            if u["self_pf"]:
                u["comp"](L0[i], L1.get(i), hook, lambda nj=nj: load0(nj))
                load0(nj)
            else:
                u["comp"](L0[i], L1.get(i), hook)
            arena_done.add(L0[i]["b"].name)
            if i in L1:
                arena_done.add(L1[i]["b"].name)
            hook()

        if final:
            S.dma("sync", lambda e: e.dma_start(out=borep[:], in_=gfin_d.partition_broadcast(128)),
                  "borep", writes=[b_borep])
            rms_stats()
            for t in range(NT):
                S.op("vector", lambda e, t=t: e.scalar_tensor_tensor(
                    out=X[:, t, :], in0=X[:, t, :], scalar=rstd[:, t:t + 1], in1=borep[:], op0=ALU.mult, op1=ALU.mult),
                    reads=[bX[t], b_rstd, b_borep], writes=[bX[t]])
        ov = out_d.rearrange("(t p) d -> p t d", p=128)
        outs = []
        for t4 in range(4):
            outs.append(S.dma("sync", lambda e, t4=t4: e.dma_start(out=ov[:, 4 * t4:4 * t4 + 4, :],
                                                                    in_=X[:, 4 * t4:4 * t4 + 4, :]),
                              "out%d" % t4, reads=bX[4 * t4:4 * t4 + 4]))
        with nc.Block() as block:
            S.emit(block, final_wait_ops=outs)
    return nc


_PROGRAMS = {}


def _get_program(key):
    if key not in _PROGRAMS:
        _PROGRAMS[key] = build(*key)
    return _PROGRAMS[key]


LAUNCH_PLAN = [((0, 1, 2, 3), True)]


def kernel(x, rel_bias, norm_mix, norm_mlp, w_qkv_a, b_qkv_a, sinks_a, w_o_a, b_o_a,
           w_qkvf_b, b_f_b, w_o_b, w_up, w_down, norm_final):
    f = lambda a: np.ascontiguousarray(np.asarray(a, dtype=np.float32))
    x = f(x)
    params = _layout_params(f(norm_mix), f(norm_mlp), f(b_qkv_a), f(b_f_b))
    biasT = _layout_bias(f(rel_bias)).reshape(2, 4, 128, 512)
    shared = {"params": params, "biasT": biasT, "w_qkv_a": f(w_qkv_a), "b_qkv_a": f(b_qkv_a),
              "sinks_a": f(sinks_a), "w_o_a": f(w_o_a), "b_o_a": f(b_o_a), "w_qkvf_b": f(w_qkvf_b),
              "w_o_b": f(w_o_b), "w_up": f(w_up), "w_down": f(w_down), "norm_final": f(norm_final)}
    cur = [x[b] for b in range(8)]
    for key in LAUNCH_PLAN:
        nc = _get_program(key)
        in_maps = [dict(shared, x=cur[b]) for b in range(8)]
        res = run_bass_kernel_spmd(nc, in_maps, core_ids=list(range(8)))
        cur = [np.asarray(res.results[b]["out"], dtype=np.float32) for b in range(8)]
    return np.stack(cur, axis=0).astype(np.float32)
```
